# Optimizing a Trainium2 kernel written in Bass

```python
import jax, jax.numpy as jnp
from jax import lax
import numpy as np

D_MODEL = 1024
BATCH = 4
SEQ = 4096
DEPTH = 1

D_MIX = D_MODEL
RW_HEAD_DIM = 64
RW_WIDTH = D_MIX // 2
RW_HEADS = RW_WIDTH // RW_HEAD_DIM
HG_DK = 128
HG_WIDTH = D_MIX - RW_WIDTH
HG_HEADS = HG_WIDTH // HG_DK
HG_DV = HG_WIDTH // HG_HEADS
HG_KEY = HG_HEADS * HG_DK
DECAY_LORA = 64
ICL_LORA = 64
N_DIR = 2
HG_CHUNK = 64
NORM_EPS = 1e-6
GN_EPS = 64e-5
RW_SIZES = (RW_WIDTH, RW_WIDTH, RW_WIDTH, RW_WIDTH, DECAY_LORA, DECAY_LORA, ICL_LORA, ICL_LORA)
HG_SIZES = (HG_KEY, HG_KEY, HG_KEY, HG_WIDTH, HG_WIDTH)
RW_COLS = 4 * RW_WIDTH + N_DIR * (DECAY_LORA + ICL_LORA)
HG_COLS = 3 * HG_KEY + 2 * HG_WIDTH
IN_COLS = RW_COLS + HG_COLS

kernel_name = "hybrid_rwkv7_hgrn2_bidir_layer"


def rms_norm(x, g):
    xf = x.astype(jnp.float32)
    y = xf * lax.rsqrt(jnp.mean(xf * xf, axis=-1, keepdims=True) + NORM_EPS)
    return (y * g.astype(jnp.float32)).astype(x.dtype)


def split_cols(z, sizes):
    offs = np.cumsum(np.array(sizes))[:-1].tolist()
    return jnp.split(z, offs, axis=-1)


def centred_token_shift(p, mu_prev, mu_next):
    zeros = jnp.zeros_like(p[:, :1])
    p_prev = jnp.concatenate([zeros, p[:, :-1]], axis=1)
    p_next = jnp.concatenate([p[:, 1:], zeros], axis=1)
    return p + mu_prev * (p_prev - p) + mu_next * (p_next - p)


def orient(t2):
    return jnp.stack([t2[0], jnp.flip(t2[1], axis=1)], axis=0)


def merge_dirs(y2):
    return y2[0] + jnp.flip(y2[1], axis=1)


def rwkv7_scan(r, w, k, v, kk, b):
    def step(S, inp):
        r_t, w_t, k_t, v_t, kk_t, b_t = inp
        S = (S * w_t[..., None, :]
             - jnp.einsum('dbhij,dbhj->dbhi', S, kk_t)[..., None] * b_t[..., None, :]
             + v_t[..., :, None] * k_t[..., None, :])
        return S, jnp.einsum('dbhij,dbhj->dbhi', S, r_t)
    xs = [jnp.moveaxis(a, 2, 0) for a in (r, w, k, v, kk, b)]
    n_dir, bsz, _, heads, n = r.shape
    S0 = jnp.zeros((n_dir, bsz, heads, n, n), jnp.float32)
    _, y = lax.scan(step, S0, xs)
    return jnp.moveaxis(y, 0, 2)


def rwkv7_mixer(z, w0, w2, a0, a2, k_k, k_a, r_k, ln_w, ln_b):
    bsz, seq, _ = z.shape
    zf = z.astype(jnp.float32)
    r, k, v, g, wd_f, wd_b, ad_f, ad_b = split_cols(zf, RW_SIZES)
    heads = lambda t: t.reshape(t.shape[:-1] + (RW_HEADS, RW_HEAD_DIM))
    w_raw = w0[:, None, None, :] + jnp.einsum('dbtr,drc->dbtc', jnp.tanh(jnp.stack([wd_f, wd_b])), w2)
    w_log = -jax.nn.softplus(-w_raw) - 0.5
    decay = jnp.exp(-jnp.exp(w_log))
    a = jax.nn.sigmoid(a0[:, None, None, :] + jnp.einsum('dbtr,drc->dbtc', jnp.stack([ad_f, ad_b]), a2))
    kk = heads(k * k_k)
    kk = kk / jnp.maximum(jnp.linalg.norm(kk, axis=-1, keepdims=True), 1e-12)
    k_dir = heads(k[None] * (1.0 + (a - 1.0) * k_a))
    a_h = heads(a)
    r_h, v_h = heads(r), heads(v)
    y2 = rwkv7_scan(orient(jnp.stack([r_h, r_h])), orient(heads(decay)), orient(k_dir),
                    orient(jnp.stack([v_h, v_h])), orient(jnp.stack([kk, kk])),
                    orient(kk[None] * a_h))
    y = merge_dirs(y2)
    mu = jnp.mean(y, axis=-1, keepdims=True)
    var = jnp.mean(jnp.square(y - mu), axis=-1, keepdims=True)
    y = (y - mu) * lax.rsqrt(var + GN_EPS) * heads(ln_w) + heads(ln_b)
    bonus = jnp.sum(r_h[None] * k_dir * r_k, axis=(0, -1))[..., None] * v_h
    out = (y + bonus).reshape(bsz, seq, RW_WIDTH) * jax.nn.silu(g)
    return out.astype(z.dtype)


def hgrn2_chunked_scan(q, k, v, log_f):
    n_dir, bsz, seq, heads, dk = q.shape
    dv = v.shape[-1]
    n_chunks = seq // HG_CHUNK
    to_chunks = lambda t: t.reshape(n_dir, bsz, n_chunks, HG_CHUNK, heads, t.shape[-1]).transpose(2, 0, 1, 4, 3, 5)
    causal = jnp.tril(jnp.ones((HG_CHUNK, HG_CHUNK), bool))[:, :, None]

    def step(S, inp):
        q_c, k_c, v_c, g_c = inp
        b = jnp.cumsum(g_c, axis=-2)
        o_inter = jnp.einsum('dbhck,dbhkv->dbhcv', q_c * jnp.exp(b), S)
        diff = b[..., :, None, :] - b[..., None, :, :]
        dec = jnp.where(causal, jnp.exp(jnp.where(causal, diff, 0.0)), 0.0)
        scores = jnp.einsum('dbhtk,dbhtsk,dbhsk->dbhts', q_c, dec, k_c)
        o_intra = jnp.einsum('dbhts,dbhsv->dbhtv', scores, v_c)
        b_end = b[..., -1:, :]
        S = (jnp.exp(b_end[..., 0, :])[..., None] * S
             + jnp.einsum('dbhck,dbhcv->dbhkv', k_c * jnp.exp(b_end - b), v_c))
        return S, o_inter + o_intra

    S0 = jnp.zeros((n_dir, bsz, heads, dk, dv), jnp.float32)
    _, o = lax.scan(step, S0, [to_chunks(t) for t in (q, k, v, log_f)])
    return o.transpose(1, 2, 0, 4, 3, 5).reshape(n_dir, bsz, seq, heads, dv)


def hgrn2_mixer(z, lower_bound, norm_g):
    bsz, seq, _ = z.shape
    zf = z.astype(jnp.float32)
    q, f_f, f_b, i, g = split_cols(zf, HG_SIZES)
    kheads = lambda t: t.reshape(t.shape[:-1] + (HG_HEADS, HG_DK))
    vheads = lambda t: t.reshape(t.shape[:-1] + (HG_HEADS, HG_DV))
    f = lower_bound + (1.0 - lower_bound) * jax.nn.sigmoid(jnp.stack([f_f, f_b]))
    q_h, i_h = kheads(q), vheads(i)
    o2 = hgrn2_chunked_scan(orient(jnp.stack([q_h, q_h])), orient(kheads(1.0 - f)),
                            orient(jnp.stack([i_h, i_h])), orient(kheads(jnp.log(f))))
    o = merge_dirs(o2)
    o = o * lax.rsqrt(jnp.mean(o * o, axis=-1, keepdims=True) + NORM_EPS) * norm_g
    out = o.reshape(bsz, seq, HG_WIDTH) * jax.nn.silu(g)
    return out.astype(z.dtype)


def setup_inputs(seed: int = 0) -> dict:
    key = jax.random.key(seed)
    ks = jax.random.split(key, 20)
    nrm = lambda k, shape: jax.random.normal(k, shape, jnp.float32)
    L = DEPTH
    w0_base = jnp.linspace(-6.0, -1.0, RW_WIDTH, dtype=jnp.float32)
    return {
        "x": nrm(ks[0], (BATCH, SEQ, D_MODEL)),
        "pre_norm_g": 1.0 + 0.02 * nrm(ks[1], (L, D_MODEL)),
        "w_in": nrm(ks[2], (L, D_MODEL, IN_COLS)) * D_MODEL ** -0.5,
        "rw_shift_prev": jax.random.uniform(ks[3], (L, RW_COLS), jnp.float32, 0.0, 0.5),
        "rw_shift_next": jax.random.uniform(ks[4], (L, RW_COLS), jnp.float32, 0.0, 0.5),
        "rw_w0": w0_base + 0.1 * nrm(ks[5], (L, N_DIR, RW_WIDTH)),
        "rw_w2": 0.1 * nrm(ks[6], (L, N_DIR, DECAY_LORA, RW_WIDTH)) * DECAY_LORA ** -0.5,
        "rw_a0": 0.1 * nrm(ks[7], (L, N_DIR, RW_WIDTH)),
        "rw_a2": 0.1 * nrm(ks[8], (L, N_DIR, ICL_LORA, RW_WIDTH)) * ICL_LORA ** -0.5,
        "rw_k_k": 0.85 + 0.02 * nrm(ks[9], (L, RW_WIDTH)),
        "rw_k_a": 1.0 + 0.02 * nrm(ks[10], (L, RW_WIDTH)),
        "rw_r_k": -0.04 + 0.02 * nrm(ks[11], (L, RW_HEADS, RW_HEAD_DIM)),
        "rw_ln_w": 1.0 + 0.02 * nrm(ks[12], (L, RW_WIDTH)),
        "rw_ln_b": 0.02 * nrm(ks[13], (L, RW_WIDTH)),
        "hg_lb_logits": 0.5 * nrm(ks[14], (DEPTH + 1, HG_KEY)),
        "hg_norm_g": 1.0 + 0.02 * nrm(ks[15], (L, HG_DV)),
        "w_out": nrm(ks[16], (L, D_MIX, D_MODEL)) * D_MIX ** -0.5,
        "post_norm_g": 1.0 + 0.02 * nrm(ks[17], (L, D_MODEL)),
    }


def reference(x, pre_norm_g, w_in, rw_shift_prev, rw_shift_next, rw_w0, rw_w2, rw_a0, rw_a2,
              rw_k_k, rw_k_a, rw_r_k, rw_ln_w, rw_ln_b, hg_lb_logits, hg_norm_g, w_out, post_norm_g):
    lower_bounds = jnp.cumsum(jax.nn.softmax(hg_lb_logits.astype(jnp.float32), axis=0), axis=0)
    for l in range(DEPTH):
        h = rms_norm(x, pre_norm_g[l])
        p = h @ w_in[l]
        rw_in = centred_token_shift(p[..., :RW_COLS], rw_shift_prev[l], rw_shift_next[l])
        y_rw = rwkv7_mixer(rw_in, rw_w0[l], rw_w2[l], rw_a0[l], rw_a2[l], rw_k_k[l], rw_k_a[l],
                           rw_r_k[l], rw_ln_w[l], rw_ln_b[l])
        y_hg = hgrn2_mixer(p[..., RW_COLS:], lower_bounds[l], hg_norm_g[l])
        y = jnp.concatenate([y_rw, y_hg], axis=-1) @ w_out[l]
        x = x + rms_norm(y, post_norm_g[l])
    return x
```

```python
import numpy as np
import concourse.bass as bass
import concourse.mybir as mybir
from concourse.bass_utils import run_bass_kernel_spmd

F32 = mybir.dt.float32
BF16 = mybir.dt.bfloat16
AF = mybir.ActivationFunctionType
ALU = mybir.AluOpType

T = 4096
D = 1024
NPC = 8
PW = 512
C = 64
NCH = 64
NCB = 38
INC = 4864
CDEC = 0.6065306597126334
NORM_EPS = 1e-6
GN_EPS = 64e-5
HG_LVL = 99
STAGGER = 0
TRANSITIVE = True
NPAR = 96


class Prog:
    ENG = ('pe', 'act', 'dve', 'pool', 'sp')

    def __init__(self, nc, n_dma_sems=10):
        self.nc = nc
        self.ops = {e: [] for e in self.ENG}
        self.sems = {e: nc.alloc_semaphore(name='s_' + e) for e in self.ENG}
        self.dma_sems = [nc.alloc_semaphore(name='d%d' % i) for i in range(n_dma_sems)]
        self.dma_val = [0] * n_dma_sems
        self.dma_rr = 0
        self.count = {e: 0 for e in self.ENG}
        self.seen = {e: {} for e in self.ENG}
        self.pending = {e: [] for e in self.ENG}
        self.lastw = {}
        self.readers = {}
        self.clock = {}

    def _need(self, eng, ev, waits):
        if ev is None:
            return
        sid, val, src = ev
        if src == eng and eng == 'pe':
            return
        if self.seen[eng].get(sid, 0) >= val:
            return
        self.seen[eng][sid] = val
        waits.append((sid, val))
        if TRANSITIVE:
            for k2, v2 in self.clock.get((sid, val), {}).items():
                if self.seen[eng].get(k2, 0) < v2:
                    self.seen[eng][k2] = v2

    def op(self, eng, fn, reads=(), writes=(), dma=False):
        waits = []
        for ev in self.pending[eng]:
            self._need(eng, ev, waits)
        self.pending[eng] = []
        for k in reads:
            self._need(eng, self.lastw.get(k), waits)
        for k in writes:
            self._need(eng, self.lastw.get(k), waits)
            for ev in self.readers.get(k, ()):
                self._need(eng, ev, waits)
        if dma:
            di = self.dma_rr
            self.dma_rr = (self.dma_rr + 1) % len(self.dma_sems)
            if self.dma_val[di] > 0:
                self._need(eng, (('d', di), self.dma_val[di], 'dma'), waits)
            self.dma_val[di] += 16
            ev = (('d', di), self.dma_val[di], 'dma')
            inc = (self.dma_sems[di], 16)
        else:
            self.count[eng] += 1
            ev = (('e', eng), self.count[eng], eng)
            inc = (self.sems[eng], 1)
        if TRANSITIVE:
            self.clock[(ev[0], ev[1])] = dict(self.seen[eng])
        for k in writes:
            self.lastw[k] = ev
            self.readers[k] = []
        for k in reads:
            if k not in writes:
                self.readers.setdefault(k, []).append(ev)
        self.ops[eng].append((fn, waits, inc))
        return ev

    def all_events(self):
        evs = [(('e', e), self.count[e], e) for e in self.ENG if self.count[e] > 0]
        evs += [(('d', i), v, 'dma') for i, v in enumerate(self.dma_val) if v > 0]
        return evs

    def barrier(self):
        evs = self.all_events()
        for e in self.ENG:
            self.pending[e] = list(evs)

    def _semh(self, sid):
        return self.dma_sems[sid[1]] if sid[0] == 'd' else self.sems[sid[1]]

    def emit(self):
        nc = self.nc
        final_events = self.all_events()
        with nc.Block() as block:
            def run(engname, engobj):
                for fn, waits, inc in self.ops[engname]:
                    for sid, val in waits:
                        engobj.wait_ge(self._semh(sid), val)
                    fn(engobj).then_inc(inc[0], inc[1])
                if engname == 'sp':
                    for sid, val, _ in final_events:
                        engobj.wait_ge(self._semh(sid), val)

            @block.tensor
            def _(e): run('pe', e)

            @block.scalar
            def _(e): run('act', e)

            @block.vector
            def _(e): run('dve', e)

            @block.gpsimd
            def _(e): run('pool', e)

            @block.sync
            def _(e): run('sp', e)


class Arena:
    def __init__(self, nc, cap):
        self.nc = nc
        self.cur = 16512
        self.cap = cap
        self.n = 0

    def alloc(self, name, shape, dtype):
        sz = 1
        for s in shape[1:]:
            sz *= s
        sz *= 2 if dtype == BF16 else 4
        sz = (sz + 63) // 64 * 64
        off = self.cur
        self.cur += sz
        assert self.cur <= self.cap, (name, self.cur, self.cap)
        self.n += 1
        return self.nc.alloc_sbuf_tensor_at("%s_%d" % (name, self.n), list(shape), dtype, offset=off)

    def mark(self):
        return self.cur

    def reset(self, m):
        print('arena peak', self.cur, 'of', self.cap)
        self.cur = m


def build(stop_after=None, rw_units=4, hg_units=4):
    nc = bass.Bass("TRN2", target_bir_lowering=False)
    xT_d = nc.dram_tensor("xT", [D, T], F32, kind="ExternalInput").ap()
    x_d = nc.dram_tensor("x", [T, D], F32, kind="ExternalInput").ap()
    win_d = nc.dram_tensor("w_in", [D, INC], F32, kind="ExternalInput").ap()
    wout_d = nc.dram_tensor("w_out", [D, D], F32, kind="ExternalInput").ap()
    par_d = nc.dram_tensor("par", [128, NPAR], F32, kind="ExternalInput").ap()
    w2s_d = nc.dram_tensor("w2s", [128, 512], F32, kind="ExternalInput").ap()
    a2s_d = nc.dram_tensor("a2s", [128, 512], F32, kind="ExternalInput").ap()
    gpost_d = nc.dram_tensor("gpost", [128, D], F32, kind="ExternalInput").ap()
    out_d = nc.dram_tensor("out", [T, D], F32, kind="ExternalOutput").ap()
    dbg = stop_after is not None
    pT_d = nc.dram_tensor("pT", [INC, T], F32, kind="ExternalOutput" if dbg else "Internal").ap()
    mixT_d = nc.dram_tensor("mixT", [D, T], BF16, kind="ExternalOutput" if dbg else "Internal").ap()

    P = Prog(nc)
    A = Arena(nc, 229344)
    psb = [nc.alloc_psum_tensor("psb%d" % i, [128, 512], F32) for i in range(8)]
    ps_rr = [0]

    def PS():
        b = ps_rr[0]
        ps_rr[0] = (b + 1) % 8
        return psb[b], "ps%d" % b

    def tt(eng, out, in0, in1, op, r, w):
        P.op(eng, lambda e: e.tensor_tensor(out=out, in0=in0, in1=in1, op=op), reads=r, writes=w)

    def ts(eng, out, in0, s1, s2, op0, op1, r, w):
        if s2 is None:
            P.op(eng, lambda e: e.tensor_scalar(out=out, in0=in0, scalar1=s1, scalar2=None, op0=op0), reads=r, writes=w)
        else:
            P.op(eng, lambda e: e.tensor_scalar(out=out, in0=in0, scalar1=s1, scalar2=s2, op0=op0, op1=op1), reads=r, writes=w)

    def stt(eng, out, in0, scalar, in1, op0, op1, r, w):
        P.op(eng, lambda e: e.scalar_tensor_tensor(out=out, in0=in0, scalar=scalar, in1=in1, op0=op0, op1=op1),
             reads=r, writes=w)

    def act(out, in_, func, r, w, bias=None, scale=1.0, accum=None):
        kw = {}
        if bias is not None:
            kw['bias'] = bias
        if accum is not None:
            kw['accum_out'] = accum
        P.op('act', lambda e: e.activation(out=out, in_=in_, func=func, scale=scale, **kw), reads=r, writes=w)

    def mm(out, lhsT, rhs, r, w, start=True, stop=True):
        P.op('pe', lambda e: e.matmul(out, lhsT=lhsT, rhs=rhs, start=start, stop=stop), reads=r, writes=w)

    def cp(eng, out, in_, r, w):
        if eng == 'act':
            act(out, in_, AF.Identity, r, w)
        else:
            P.op(eng, lambda e: e.tensor_copy(out=out, in_=in_), reads=r, writes=w)

    def dma(out, in_, r, w, eng='sp'):
        return P.op(eng, lambda e: e.dma_start(out=out, in_=in_), reads=r, writes=w, dma=True)

    def mset(eng, ap, val, w):
        P.op(eng, lambda e: e.memset(ap, val), writes=w)

    def recip(out, in_, r, w):
        P.op('dve', lambda e: e.reciprocal(out=out, in_=in_), reads=r, writes=w)

    def scan(out, d0, d1, r, w):
        P.op('dve', lambda e: e.tensor_tensor_scan(out=out, data0=d0, data1=d1, initial=0.0, op0=ALU.mult, op1=ALU.add),
             reads=r, writes=w)

    def asel(out, in_, pattern, cm, cmpop, r, w, base=0):
        P.op('pool', lambda e: e.affine_select(out=out, in_=in_, pattern=pattern, base=base, channel_multiplier=cm,
                                              compare_op=cmpop, fill=0.0), reads=r, writes=w)

    par = A.alloc("par", [128, NPAR], F32)
    dma(par[:, :], par_d[:, :], [], ['par'])
    PO = {}
    o = 0
    for nm, n in (('gpre', 8), ('mup', 18), ('mun', 18), ('w0', 8), ('a0', 8), ('kk', 4), ('ka', 4), ('rk', 4),
                  ('lnw', 4), ('lnb', 4), ('lb0', 4), ('lb1', 4), ('hgn', 1)):
        PO[nm] = o
        o += n
    assert o <= NPAR

    def pcol(nm, i=0):
        return par[:, PO[nm] + i:PO[nm] + i + 1]

    onesf = A.alloc("onesf", [128, 128], F32)
    negf = A.alloc("negf", [128, 128], F32)
    identf = A.alloc("identf", [128, 128], F32)
    identb = A.alloc("identb", [128, 128], BF16)
    onesb = A.alloc("onesb", [128, 128], BF16)
    bones = A.alloc("bones", [128, 128], BF16)
    epsn = A.alloc("epsn", [128, 1], F32)
    epsg = A.alloc("epsg", [128, 1], F32)
    c0 = A.alloc("c0", [128, 18], F32)
    omka = A.alloc("omka", [128, 4], F32)
    lb = A.alloc("lb", [128, 4], F32)
    oml = A.alloc("oml", [128, 4], F32)
    rmask = A.alloc("rmask", [128, 8, 64], F32)
    mset('pool', onesf[:, :], 1.0, ['onesf'])
    mset('pool', negf[:, :], -1.0, ['negf'])
    mset('pool', onesb[:, :], 1.0, ['onesb'])
    mset('pool', bones[:, :], 0.0, ['bones'])
    mset('pool', bones[0:64, 0:64], 1.0, ['bones'])
    mset('pool', bones[64:128, 64:128], 1.0, ['bones'])
    mset('pool', epsn[:, :], NORM_EPS, ['epsn'])
    mset('pool', epsg[:, :], GN_EPS, ['epsg'])
    mset('pool', rmask[:, :, :], 1.0, ['rmask'])
    mset('pool', rmask[:, :, 0:1], 0.0, ['rmask'])
    asel(identf[:, :], onesf[:, :], [[1, 128]], -1, ALU.is_equal, ['onesf'], ['identf'])
    cp('pool', identb[:, :], identf[:, :], ['identf'], ['identb'])
    tt('pool', c0[:, :], par[:, PO['mup']:PO['mup'] + 18], par[:, PO['mun']:PO['mun'] + 18], ALU.add, ['par'], ['c0'])
    ts('pool', c0[:, :], c0[:, :], -1.0, 1.0, ALU.mult, ALU.add, ['c0'], ['c0'])
    ts('pool', omka[:, :], par[:, PO['ka']:PO['ka'] + 4], -1.0, 1.0, ALU.mult, ALU.add, ['par'], ['omka'])
    tt('pool', lb[:, :], par[:, PO['lb0']:PO['lb0'] + 4], par[:, PO['lb1']:PO['lb1'] + 4], ALU.subtract, ['par'], ['lb'])
    act(lb[:, :], lb[:, :], AF.Sigmoid, ['lb'], ['lb'])
    ts('pool', oml[:, :], lb[:, :], -1.0, 1.0, ALU.mult, ALU.add, ['lb'], ['oml'])

    maskX = [A.alloc("maskX%d" % d, [128, 512], F32) for d in range(2)]
    maskH = [A.alloc("maskH%d" % d, [64, 8, 64], F32) for d in range(2)]
    signs = A.alloc("signs", [128, 512], F32)
    for d in range(2):
        mset('pool', maskX[d][:, :], 0.0, ['maskX%d' % d])
        if d == 0:
            pat, cm = [[1, 64]], -1
        else:
            pat, cm = [[-1, 64]], 1
        patT, cmT = ([[-1, 64]], 1) if d == 0 else ([[1, 64]], -1)
        for h in range(2):
            pr = slice(64 * h, 64 * h + 64)
            asel(maskX[d][pr, 64 * h:64 * h + 64], onesf[pr, 0:64], pat, cm, ALU.is_gt, ['onesf'], ['maskX%d' % d])
            asel(maskX[d][pr, 128:192], onesf[pr, 0:64], pat, cm, ALU.is_ge, ['onesf'], ['maskX%d' % d])
            asel(maskX[d][pr, 192 + 64 * h:192 + 64 * h + 64], negf[pr, 0:64], patT, cmT, ALU.is_gt, ['negf'], ['maskX%d' % d])
            asel(maskX[d][pr, 320 + 64 * h:320 + 64 * h + 64], negf[pr, 0:64], pat, cm, ALU.is_gt, ['negf'], ['maskX%d' % d])
            asel(maskX[d][pr, 448:512], onesf[pr, 0:64], pat, cm, ALU.is_ge, ['onesf'], ['maskX%d' % d])
            if h == 0:
                for q in range(8):
                    asel(maskH[d][0:64, q, :], onesf[0:64, 0:64], pat, cm, ALU.is_ge, ['onesf'], ['maskH%d' % d])
    mset('pool', signs[:, :], 1.0, ['signs'])
    mset('pool', signs[:, 128:256], -1.0, ['signs'])
    mset('pool', signs[:, 384:512], -1.0, ['signs'])

    w2b = A.alloc("w2b", [128, 512], BF16)
    a2b = A.alloc("a2b", [128, 512], BF16)
    mconst = A.mark()
    stg = [A.alloc("stg", [128, 512], F32) for _ in range(2)]
    dma(stg[0][:, :], w2s_d[:, :], [], ['stg0'])
    cp('dve', w2b[:, :], stg[0][:, :], ['stg0'], ['w2b'])
    dma(stg[1][:, :], a2s_d[:, :], [], ['stg1'])
    cp('dve', a2b[:, :], stg[1][:, :], ['stg1'], ['a2b'])

    xTb = A.alloc("xTb", [128, 8, T], BF16)
    rstd = A.alloc("rstd", [128, PW], F32)
    xst = [A.alloc("xst", [128, 8, PW], F32) for _ in range(2)]
    sq = [A.alloc("sq", [128, PW], BF16) for _ in range(2)]
    tmpa = A.alloc("tmpa", [128, PW], F32)
    i_x = 0
    for pc in range(NPC):
        tsl = slice(pc * PW, (pc + 1) * PW)
        s = pc % 2
        bank, bk = PS()
        for kc in range(8):
            s2 = i_x % 2
            i_x += 1
            dma(xst[s][:, kc, :], xT_d[kc * 128:(kc + 1) * 128, tsl], [], ['xst%d_%d' % (s, kc)])
            act(sq[s2][:, :], xst[s][:, kc, :], AF.Square, ['xst%d_%d' % (s, kc)], ['sq%d' % s2])
            mm(bank[:, :], onesb[:, :], sq[s2][:, :], ['onesb', 'sq%d' % s2], [bk], start=(kc == 0), stop=(kc == 7))
        act(tmpa[:, :], bank[:, :], AF.Ln, [bk, 'epsn'], ['tmpa'], bias=epsn[:, 0:1], scale=1.0 / D)
        act(rstd[:, :], tmpa[:, :], AF.Exp, ['tmpa'], ['rstd'], scale=-0.5)
        for kc in range(8):
            stt('dve', xTb[:, kc, tsl], xst[s][:, kc, :], pcol('gpre', kc), rstd[:, :], ALU.mult, ALU.mult,
                ['xst%d_%d' % (s, kc), 'par', 'rstd'], ['xTb%d' % pc])

    wst = [A.alloc("wst", [128, 8, 128], F32) for _ in range(2)]
    wb = [A.alloc("wb", [128, 8, 128], BF16) for _ in range(2)]
    pcb = [A.alloc("pcb", [128, T + 2], F32) for _ in range(3)]
    pout = [A.alloc("pout", [128, PW], F32) for _ in range(6)]
    for s in range(3):
        mset('pool', pcb[s][:, 0:1], 0.0, ['pcbh%d' % s])
        mset('pool', pcb[s][:, T + 1:T + 2], 0.0, ['pcbh%d' % s])
    win_v = win_d.rearrange("(kc p) n -> p kc n", p=128)

    def load_w(cb):
        s = cb % 2
        dma(wst[s][:, :, :], win_v[:, :, cb * 128:(cb + 1) * 128], [], ['wst%d' % s])
        cp('act', wb[s][:, :, :], wst[s][:, :, :], ['wst%d' % s], ['wb%d' % s])

    i_po = 0
    load_w(0)
    for cb in range(NCB):
        s = cb % 2
        sp_ = cb % 3
        if cb + 1 < NCB:
            load_w(cb + 1)
        for pc in range(NPC):
            tsl = slice(pc * PW, (pc + 1) * PW)
            bank, bk = PS()
            for kc in range(8):
                mm(bank[:, :], wb[s][:, kc, :], xTb[:, kc, tsl], ['wb%d' % s, 'xTb%d' % pc], [bk],
                   start=(kc == 0), stop=(kc == 7))
            cp('act', pcb[sp_][:, 1 + pc * PW:1 + (pc + 1) * PW], bank[:, :], [bk], ['pcb%d_%d' % (sp_, pc)])
        for pc in range(NPC):
            rk = ['pcb%d_%d' % (sp_, q) for q in (pc - 1, pc, pc + 1) if 0 <= q < NPC] + ['pcbh%d' % sp_]
            if cb < 18:
                po = pout[i_po % 6]
                pk = 'pout%d' % (i_po % 6)
                i_po += 1
                b0 = 1 + pc * PW
                act(po[:, :], pcb[sp_][:, b0:b0 + PW], AF.Identity, rk + ['c0'], [pk], scale=c0[:, cb:cb + 1])
                stt('dve', po[:, :], pcb[sp_][:, b0 - 1:b0 - 1 + PW], pcol('mup', cb), po[:, :], ALU.mult, ALU.add,
                    rk + ['par', pk], [pk])
                stt('dve', po[:, :], pcb[sp_][:, b0 + 1:b0 + 1 + PW], pcol('mun', cb), po[:, :], ALU.mult, ALU.add,
                    rk + ['par', pk], [pk])
                dma(pT_d[cb * 128:(cb + 1) * 128, pc * PW:(pc + 1) * PW], po[:, :], [pk], [], eng='pool')
            else:
                dma(pT_d[cb * 128:(cb + 1) * 128, pc * PW:(pc + 1) * PW], pcb[sp_][:, 1 + pc * PW:1 + (pc + 1) * PW],
                    rk, [], eng='pool')
    P.barrier()
    A.reset(mconst)
    if stop_after == 'A':
        P.emit()
        return nc

    def pslice(cb, pc):
        return pT_d[cb * 128:(cb + 1) * 128, pc * PW:(pc + 1) * PW]

    mB = A.mark()
    Yacc = A.alloc("Yacc", [128, T], F32)
    Rhat = [A.alloc("Rhat", [128, T], BF16) for _ in range(2)]
    MTS = A.alloc("MTS", [128, NCH, 2, 128], BF16)
    NS = A.alloc("NS", [128, NCH, 2, 128], BF16)
    bonus = A.alloc("bonus", [128, T], BF16)
    sgp = A.alloc("sgp", [128, PW], BF16)
    GC = A.alloc("GC", [128, 2, NCH], F32)
    ld = {nm: A.alloc("ld_" + nm, [128, PW], F32) for nm in ('r', 'k', 'v', 'lw', 'la')}
    th = A.alloc("th", [128, PW], BF16)
    lab = A.alloc("lab", [128, PW], BF16)
    f32t = {nm: A.alloc("t_" + nm, [128, PW], F32) for nm in
            ('sg', 'a', 'S', 'Sd', 'eGx', 'kkm', 'nrm', 'kap', 'kd0', 'kd1')}
    f32t['Sx'] = f32t['sg']
    f32t['t1'] = f32t['nrm']
    f32t['b'] = f32t['kkm']
    for nm in ('eG', 'enG', 'eGd', 'eGxb'):
        f32t[nm] = A.alloc("t_" + nm, [128, PW], BF16)
    sqk = A.alloc("sqk", [128, PW], BF16)
    QR = A.alloc("QR", [128, 8, 192], BF16)
    bbd = A.alloc("bbd", [128, 8, 128], BF16)
    kbd = A.alloc("kbd", [128, 8, 128], BF16)
    Kdbd = A.alloc("Kdbd", [128, 8, 128], BF16)
    Bdbd = A.alloc("Bdbd", [128, 8, 128], BF16)
    vbd = A.alloc("vbd", [128, 8, 128], BF16)
    Vtm = A.alloc("Vtm", [128, 8, 128], BF16)
    Kdtm = A.alloc("Kdtm", [128, 8, 128], BF16)
    GB = []
    for g_ in range(2):
        GB.append(dict(AT=A.alloc("AT", [128, 4, 640], BF16),
                       LX=[A.alloc("LX", [128, 4, 256], BF16) for _ in range(2)],
                       Ms=[A.alloc("Ms", [128, 4, 128], BF16) for _ in range(2)],
                       KA=A.alloc("KA", [128, 4, 256], BF16),
                       KU=A.alloc("KU", [128, 4, 256], BF16)))
    for tns, nm in ((QR, 'QR'), (bbd, 'bbd'), (kbd, 'kbd'), (Kdbd, 'Kdbd'), (Bdbd, 'Bdbd'), (vbd, 'vbd')):
        mset('pool', tns[:, :, :], 0.0, [nm])

    def run_window(gens, width):
        pend = list(gens)
        active = []
        while pend or active:
            while pend and len(active) < width:
                active.append(pend.pop(0))
            for g_ in list(active):
                try:
                    next(g_)
                except StopIteration:
                    active.remove(g_)

    def rw_unit(u):
        r_cb, k_cb, v_cb, g_cb = u, 4 + u, 8 + u, 12 + u
        usl = slice(u * 128, (u + 1) * 128)
        t = f32t
        v3 = lambda ap: ap.rearrange("p (c k) -> p c k", k=64)

        def piece_a(pc):
            tsl = slice(pc * PW, (pc + 1) * PW)
            for nm, cb in (('lw', 16), ('la', 17), ('k', k_cb), ('r', r_cb), ('v', v_cb)):
                dma(ld[nm][:, :], pslice(cb, pc), [], ['ld_' + nm])
            act(th[:, :], ld['lw'][:, :], AF.Tanh, ['ld_lw'], ['th'])
            cp('pool', lab[:, :], ld['la'][:, :], ['ld_la'], ['lab'])
            t = f32t
            act(t['kkm'][:, :], ld['k'][:, :], AF.Identity, ['ld_k', 'par'], ['kkm'], scale=pcol('kk', u))
            act(sqk[:, :], t['kkm'][:, :], AF.Square, ['kkm'], ['sqk'])
            bank, bk = PS()
            mm(bank[:, :], bones[:, :], sqk[:, :], ['bones', 'sqk'], [bk])
            act(t['nrm'][:, :], bank[:, :], AF.Sqrt, [bk], ['nrm'])
            ts('dve', t['nrm'][:, :], t['nrm'][:, :], 1e-12, None, ALU.max, None, ['nrm'], ['nrm'])
            recip(t['nrm'][:, :], t['nrm'][:, :], ['nrm'], ['nrm'])
            tt('pool', t['kap'][:, :], t['kkm'][:, :], t['nrm'][:, :], ALU.mult, ['kkm', 'nrm'], ['kap'])
            yield

        def piece_b(pc):
            for h in range(2):
                pr = slice(64 * h, 64 * h + 64)
                cp('act' if h == 0 else 'dve', vbd[pr, :, 64 * h:64 * h + 64], ld['v'][pr, :].rearrange("p (c k) -> p c k", k=64),
                   ['ld_v'], ['vbd'])
            for half in range(2):
                bank, bk = PS()
                for i in range(4):
                    c = half * 4 + i
                    mm(bank[:, i * 128:(i + 1) * 128], vbd[:, c, :], identb[:, :], ['vbd', 'identb'], [bk])
                cp('act', Vtm[:, half * 4:half * 4 + 4, :], bank[:, :].rearrange("p (c k) -> p c k", k=128), [bk], ['Vtm'])

        def dprep(pc, d):
            dsl = slice(64 * d, 64 * d + 64)
            bw, bkw = PS()
            mm(bw[:, :], w2b[dsl, usl], th[dsl, :], ['w2b', 'th'], [bkw])
            ba, bka = PS()
            mm(ba[:, :], a2b[dsl, usl], lab[dsl, :], ['a2b', 'lab'], [bka])
            act(t['sg'][:, :], bw[:, :], AF.Sigmoid, [bkw, 'par'], ['sg'], bias=pcol('w0', d * 4 + u))
            act(t['a'][:, :], ba[:, :], AF.Sigmoid, [bka, 'par'], ['a'], bias=pcol('a0', d * 4 + u))
            yield
            S3 = t['S'][:, :].rearrange("p (c k) -> p c k", k=64)
            if d == 0:
                scan(t['S'][:, :], rmask[:, :, :].rearrange("p c k -> p (c k)"), t['sg'][:, :], ['rmask', 'sg'], ['S'])
                Stot = S3[:, :, 63:64]
                Stot2 = S3[:, :, 63]
            else:
                scan(t['S'][:, ::-1], rmask[:, :, :].rearrange("p c k -> p (c k)"), t['sg'][:, ::-1], ['rmask', 'sg'], ['S'])
                Stot = S3[:, :, 0:1]
                Stot2 = S3[:, :, 0]
            tt('pool', t['Sx'][:, :], t['S'][:, :], t['sg'][:, :], ALU.subtract, ['S', 'sg'], ['sg'])
            tt('pool', t['Sd'][:, :].rearrange("p (c k) -> p c k", k=64), Stot.broadcast_to([128, 8, 64]), S3,
               ALU.subtract, ['S'], ['Sd'])
            yield
            act(t['eG'][:, :], t['S'][:, :], AF.Exp, ['S'], ['eG'], scale=-CDEC)
            act(t['eGxb'][:, :], t['Sx'][:, :], AF.Exp, ['sg'], ['eGxb'], scale=-CDEC)
            act(t['enG'][:, :], t['S'][:, :], AF.Exp, ['S'], ['enG'], scale=CDEC)
            act(t['eGd'][:, :], t['Sd'][:, :], AF.Exp, ['Sd'], ['eGd'], scale=-CDEC)
            act(GC[:, d, pc * 8:(pc + 1) * 8], Stot2, AF.Exp, ['S'], ['GC%d' % pc], scale=-CDEC)
            yield
            kd = t['kd%d' % d]
            kdk = 'kd%d' % d
            act(t['t1'][:, :], t['a'][:, :], AF.Identity, ['a', 'par', 'omka'], ['nrm'], scale=pcol('ka', u),
                bias=omka[:, u:u + 1])
            tt('pool', kd[:, :], t['nrm'][:, :], ld['k'][:, :], ALU.mult, ['nrm', 'ld_k'], [kdk])
            tt('pool', t['kkm'][:, :], t['kap'][:, :], t['a'][:, :], ALU.mult, ['kap', 'a'], ['kkm'])
            yield

        def dprod(pc, d):
            kd = t['kd%d' % d]
            kdk = 'kd%d' % d
            v3 = lambda ap: ap.rearrange("p (c k) -> p c k", k=64)
            for h in range(2):
                pr = slice(64 * h, 64 * h + 64)
                fs = slice(64 * h, 64 * h + 64)
                tt('dve', Kdbd[pr, :, fs], v3(kd[pr, :]), v3(t['eGd'][pr, :]), ALU.mult, [kdk, 'eGd'], ['Kdbd'])
                tt('dve', Bdbd[pr, :, fs], v3(t['kkm'][pr, :]), v3(t['eGd'][pr, :]), ALU.mult, ['kkm', 'eGd'], ['Bdbd'])
            tt('dve', QR[:, :, 128:192], v3(ld['r'][:, :]), v3(t['eG'][:, :]), ALU.mult, ['ld_r', 'eG'], ['QR'])
            for h in range(2):
                pr = slice(64 * h, 64 * h + 64)
                fs = slice(64 * h, 64 * h + 64)
                e1 = 'pool' if h == 0 else 'dve'
                tt(e1, QR[pr, :, fs], v3(t['kap'][pr, :]), v3(t['eGxb'][pr, :]), ALU.mult, ['kap', 'eGxb'], ['QR'])
                tt(e1, bbd[pr, :, fs], v3(t['kkm'][pr, :]), v3(t['enG'][pr, :]), ALU.mult, ['kkm', 'enG'], ['bbd'])
                tt(e1, kbd[pr, :, fs], v3(kd[pr, :]), v3(t['enG'][pr, :]), ALU.mult, [kdk, 'enG'], ['kbd'])
            for half in range(2):
                G_ = GB[half]
                for src, sk, dst, dk_, eng_ in ((QR, 'QR', G_['KA'][:, :, 0:128], 'KAk%d' % half, 'act'),
                                              (Kdbd, 'Kdbd', Kdtm[:, half * 4:half * 4 + 4, :], 'Kdtm%d' % half, 'dve'),
                                              (Bdbd, 'Bdbd', G_['AT'][:, :, 512:640], 'ATB%d' % half, 'act')):
                    bank, bk = PS()
                    for i in range(4):
                        c = half * 4 + i
                        mm(bank[:, i * 128:(i + 1) * 128], src[:, c, 0:128], identb[:, :], [sk, 'identb'], [bk])
                    cp(eng_, dst, bank[:, :].rearrange("p (c k) -> p c k", k=128), [bk], [dk_])

        def grp_gen(grp, pc, d):
            G_ = GB[grp]
            AT, LX, Ms, KA, KU = G_['AT'], G_['LX'], G_['Ms'], G_['KA'], G_['KU']
            cs = [grp * 4 + i for i in range(4)]
            atk = lambda i: 'AT%d_%d' % (grp, i)
            lxk = lambda q, i: 'LX%d_%d_%d' % (grp, q, i)
            msk = lambda q, i: 'Ms%d_%d_%d' % (grp, q, i)
            for i, c in enumerate(cs):
                bank, bk = PS()
                mm(bank[:, 0:192], kbd[:, c, :], QR[:, c, :], ['kbd', 'QR'], [bk])
                mm(bank[:, 192:320], QR[:, c, 0:128], bbd[:, c, :], ['bbd', 'QR'], [bk])
                mm(bank[:, 320:512], bbd[:, c, :], QR[:, c, :], ['bbd', 'QR'], [bk])
                tt('dve', AT[:, i, 0:512], bank[:, :], maskX[d][:, :], ALU.mult, [bk, 'maskX%d' % d], [atk(i)])
                tt('pool', LX[0][:, i, 128:256], AT[:, i, 320:448], identb[:, :], ALU.add, [atk(i), 'identb'], [lxk(0, i) + 'x'])
            yield
            cur = 0
            for j in range(6):
                nxt = 1 - cur if j >= 1 else 0
                if j == 0:
                    bankA, bkA = PS()
                    for i in range(4):
                        mm(bankA[:, i * 128:(i + 1) * 128], AT[:, i, 192:320], AT[:, i, 320:448], [atk(i)], [bkA])
                    banksA = [(bankA, bkA)]
                elif j <= 3:
                    banksA = []
                    for pair in range(2):
                        bankA, bkA = PS()
                        banksA.append((bankA, bkA))
                        for jj in range(2):
                            i = pair * 2 + jj
                            mm(bankA[:, jj * 256:(jj + 1) * 256], Ms[cur][:, i, :], LX[cur][:, i, :],
                               [msk(cur, i), lxk(cur, i), lxk(cur, i) + 'x'], [bkA])
                else:
                    bankA, bkA = PS()
                    for i in range(4):
                        mm(bankA[:, i * 128:(i + 1) * 128], Ms[cur][:, i, :], LX[cur][:, i, 128:256],
                           [msk(cur, i), lxk(cur, i) + 'x'], [bkA])
                    banksA = [(bankA, bkA)]
                if j <= 4:
                    bankB, bkB = PS()
                    for i in range(4):
                        if j == 0:
                            mm(bankB[:, i * 128:(i + 1) * 128], AT[:, i, 320:448], AT[:, i, 192:320], [atk(i)], [bkB])
                        else:
                            mm(bankB[:, i * 128:(i + 1) * 128], LX[cur][:, i, 0:128], Ms[cur][:, i, :],
                               [lxk(cur, i), msk(cur, i)], [bkB])
                mnx = 0 if j == 0 else nxt
                if j <= 4:
                    cp('act', Ms[mnx][:, :, :], bankB[:, :].rearrange("p (c k) -> p c k", k=128), [bkB],
                       [msk(mnx, i) for i in range(4)])
                if j == 0:
                    cp('act', LX[0][:, :, 0:128], bankA[:, :].rearrange("p (c k) -> p c k", k=128), [bkA],
                       [lxk(0, i) for i in range(4)])
                elif j <= 3:
                    for pair in range(2):
                        bankA, bkA = banksA[pair]
                        b3 = bankA[:, :].rearrange("p (c k) -> p c k", k=256)
                        ii = [pair * 2, pair * 2 + 1]
                        cp('act', LX[nxt][:, pair * 2:pair * 2 + 2, 0:128], b3[:, :, 0:128], [bkA], [lxk(nxt, i) for i in ii])
                        tt('dve', LX[nxt][:, pair * 2:pair * 2 + 2, 128:256], b3[:, :, 128:256],
                           LX[cur][:, pair * 2:pair * 2 + 2, 128:256], ALU.add,
                           [bkA] + [lxk(cur, i) + 'x' for i in ii] + [lxk(nxt, i) for i in ii], [lxk(nxt, i) + 'x' for i in ii])
                else:
                    tt('dve', LX[nxt][:, :, 128:256], bankA[:, :].rearrange("p (c k) -> p c k", k=128),
                       LX[cur][:, :, 128:256], ALU.add, [bkA] + [lxk(cur, i) + 'x' for i in range(4)],
                       [lxk(nxt, i) + 'x' for i in range(4)])
                cur = mnx if j == 0 else nxt
                yield
            xk = lambda i: lxk(cur, i) + 'x'
            bank, bk = PS()
            for i, c in enumerate(cs):
                mm(bank[:, i * 128:(i + 1) * 128], AT[:, i, 0:128], Vtm[:, c, :], [atk(i), 'Vtm'], [bk])
            act(KA[:, :, 128:256], bank[:, :].rearrange("p (c k) -> p c k", k=128), AF.Identity, [bk], ['KAv%d' % grp], scale=-1.0)
            yield
            for pair in range(2):
                bank, bk = PS()
                for jj in range(2):
                    i = pair * 2 + jj
                    mm(bank[:, jj * 256:(jj + 1) * 256], LX[cur][:, i, 128:256], KA[:, i, :],
                       [xk(i), 'KAk%d' % grp, 'KAv%d' % grp], [bk])
                cp('act', KU[:, pair * 2:pair * 2 + 2, :], bank[:, :].rearrange("p (c k) -> p c k", k=256), [bk],
                   ['KU%d_%d' % (grp, pair)])
            kuk = lambda i: 'KU%d_%d' % (grp, i // 2)
            t0 = pc * PW + grp * 256
            yield
            for pair in range(2):
                bank, bk = PS()
                for jj in range(2):
                    i = pair * 2 + jj
                    mm(bank[:, jj * 192:(jj + 1) * 192], KU[:, i, 0:128], AT[:, i, 448:640], [kuk(i), atk(i), 'ATB%d' % grp], [bk])
                b3 = bank[:, 0:384].rearrange("p (c k) -> p c k", k=192)
                ta = t0 + pair * 128
                tt('dve', Rhat[d][:, ta:ta + 128].rearrange("p (c k) -> p c k", k=64),
                   QR[:, grp * 4 + pair * 2:grp * 4 + pair * 2 + 2, 128:192], b3[:, :, 0:64],
                   ALU.subtract, [bk, 'QR'], ['Rhat%d_%d' % (d, pc)])
                for jj in range(2):
                    i = pair * 2 + jj
                    cg = pc * 8 + cs[i]
                    st = cg if d == 0 else NCH - 1 - cg
                    stt('dve', MTS[:, st, d, :], identf[:, :], GC[:, d, cg:cg + 1], bank[:, jj * 192 + 64:(jj + 1) * 192],
                        ALU.mult, ALU.subtract, [bk, 'identf', 'GC%d' % pc], ['MTS%d' % st])
            bank, bk = PS()
            for i, c in enumerate(cs):
                mm(bank[:, i * 64:(i + 1) * 64], Vtm[:, c, :], AT[:, i, 128:192], ['Vtm', atk(i)], [bk], start=True, stop=False)
                mm(bank[:, i * 64:(i + 1) * 64], KU[:, i, 128:256], AT[:, i, 448:512], [kuk(i), atk(i)], [bk], start=False, stop=True)
            if d == 0:
                cp('act', Yacc[:, t0:t0 + 256], bank[:, 0:256], [bk], ['Yacc%d' % pc])
            else:
                tt('dve', Yacc[:, t0:t0 + 256], bank[:, 0:256], Yacc[:, t0:t0 + 256], ALU.add,
                   [bk, 'Yacc%d' % pc], ['Yacc%d' % pc])
            yield
            bank, bk = PS()
            for i, c in enumerate(cs):
                mm(bank[:, i * 128:(i + 1) * 128], Kdtm[:, c, :], Vtm[:, c, :], ['Kdtm%d' % grp, 'Vtm'], [bk], start=True, stop=False)
                mm(bank[:, i * 128:(i + 1) * 128], AT[:, i, 512:640], KU[:, i, 128:256], ['ATB%d' % grp, kuk(i)], [bk],
                   start=False, stop=True)
            for i, c in enumerate(cs):
                cg = pc * 8 + c
                st = cg if d == 0 else NCH - 1 - cg
                cp('act', NS[:, st, d, :], bank[:, i * 128:(i + 1) * 128], [bk], ['NS%d' % st])
            yield

        def bonus_f(pc):
            tsl = slice(pc * PW, (pc + 1) * PW)
            t = f32t
            tt('pool', t['nrm'][:, :], t['kd0'][:, :], t['kd1'][:, :], ALU.add, ['kd0', 'kd1'], ['nrm'])
            tt('pool', t['nrm'][:, :], t['nrm'][:, :], ld['r'][:, :], ALU.mult, ['nrm', 'ld_r'], ['nrm'])
            act(sqk[:, :], t['nrm'][:, :], AF.Identity, ['nrm', 'par'], ['sqk'], scale=pcol('rk', u))
            bank, bk = PS()
            mm(bank[:, :], bones[:, :], sqk[:, :], ['bones', 'sqk'], [bk])
            tt('dve', bonus[:, tsl], bank[:, :], ld['v'][:, :], ALU.mult, [bk, 'ld_v'], ['bonus%d' % pc])

        def interleave(gens, stagger=STAGGER):
            gens = list(gens)
            for _ in range(stagger):
                try:
                    next(gens[0])
                except StopIteration:
                    gens.pop(0)
                    break
            while gens:
                for g_ in list(gens):
                    try:
                        next(g_)
                    except StopIteration:
                        gens.remove(g_)

        def head():
            yield from piece_a(0)
            piece_b(0)
            yield
            yield from dprep(0, 0)
            dprod(0, 0)
            yield

        def body():
            for pc in range(NPC):
                interleave([grp_gen(0, pc, 0), grp_gen(1, pc, 0), dprep(pc, 1)])
                dprod(pc, 1)
                bonus_f(pc)
                nxt = []
                if pc + 1 < NPC:
                    def nxt_gen(pc=pc):
                        yield from piece_a(pc + 1)
                        yield from dprep(pc + 1, 0)
                    nxt = [nxt_gen()]
                interleave([grp_gen(0, pc, 1), grp_gen(1, pc, 1)] + nxt)
                if pc + 1 < NPC:
                    piece_b(pc + 1)
                    dprod(pc + 1, 0)


        def chain_out():
            for s in range(1, NCH):
                bank, bk = PS()
                for d in range(2):
                    mm(bank[:, d * 128:(d + 1) * 128], MTS[:, s, d, :], NS[:, s - 1, d, :], ['MTS%d' % s, 'NS%d' % (s - 1)], [bk])
                tt('dve', NS[:, s, :, :], bank[:, 0:256].rearrange("p (c k) -> p c k", k=128), NS[:, s, :, :], ALU.add,
                   [bk, 'NS%d' % s], ['NS%d' % s])
                if s % 2 == 0:
                    yield
            for d in range(2):
                for pc in range(NPC):
                    bank, bk = PS()
                    n = 0
                    for c in range(8):
                        cg = pc * 8 + c
                        st = cg if d == 0 else NCH - 1 - cg
                        if st == 0:
                            continue
                        mm(bank[:, c * 64:(c + 1) * 64], NS[:, st - 1, d, :], Rhat[d][:, cg * 64:(cg + 1) * 64],
                           ['NS%d' % (st - 1), 'Rhat%d_%d' % (d, pc)], [bk])
                    lo, hi = 0, 8
                    if d == 0 and pc == 0:
                        lo = 1
                    if d == 1 and pc == NPC - 1:
                        hi = 7
                    tsl2 = slice(pc * PW + lo * 64, pc * PW + hi * 64)
                    tt('dve', Yacc[:, tsl2], bank[:, lo * 64:hi * 64], Yacc[:, tsl2], ALU.add, [bk, 'Yacc%d' % pc], ['Yacc%d' % pc])
                    yield

        def epi():
            EY = GB[0]['KU'][:, :, :].rearrange("p a b -> p (a b)").rearrange("p (a b) -> p a b", b=PW)
            t = f32t
            for pc in range(NPC):
                tsl = slice(pc * PW, (pc + 1) * PW)
                cp('pool', EY[:, 0, :], Yacc[:, tsl], ['Yacc%d' % pc], ['KU0_0', 'KU0_1'])
                act(EY[:, 1, :], Yacc[:, tsl], AF.Square, ['Yacc%d' % pc], ['KU0_0', 'KU0_1'])
                b1, bk1 = PS()
                mm(b1[:, :], bones[:, :], EY[:, 0, :], ['bones', 'KU0_0', 'KU0_1'], [bk1])
                b2, bk2 = PS()
                mm(b2[:, :], bones[:, :], EY[:, 1, :], ['bones', 'KU0_0', 'KU0_1'], [bk2])
                act(t['S'][:, :], b1[:, :], AF.Identity, [bk1], ['S'], scale=1.0 / 64)
                tt('pool', t['sg'][:, :], t['S'][:, :], t['S'][:, :], ALU.mult, ['S'], ['sg'])
                stt('dve', t['Sd'][:, :], b2[:, :], 1.0 / 64, t['sg'][:, :], ALU.mult, ALU.subtract, [bk2, 'sg'], ['Sd'])
                act(t['a'][:, :], t['Sd'][:, :], AF.Ln, ['Sd', 'epsg'], ['a'], bias=epsg[:, 0:1])
                act(t['a'][:, :], t['a'][:, :], AF.Exp, ['a'], ['a'], scale=-0.5)
                tt('pool', t['eGx'][:, :], Yacc[:, tsl], t['S'][:, :], ALU.subtract, ['Yacc%d' % pc, 'S'], ['eGx'])
                tt('pool', t['eGx'][:, :], t['eGx'][:, :], t['a'][:, :], ALU.mult, ['eGx', 'a'], ['eGx'])
                act(t['eGx'][:, :], t['eGx'][:, :], AF.Identity, ['eGx', 'par'], ['eGx'], scale=pcol('lnw', u), bias=pcol('lnb', u))
                tt('dve', t['eGx'][:, :], t['eGx'][:, :], bonus[:, tsl], ALU.add, ['eGx', 'bonus%d' % pc], ['eGx'])
                dma(ld['lw'][:, :], pslice(g_cb, pc), [], ['ld_lw'])
                act(sgp[:, :], ld['lw'][:, :], AF.Silu, ['ld_lw'], ['sgp'])
                tt('dve', sqk[:, :], t['eGx'][:, :], sgp[:, :], ALU.mult, ['eGx', 'sgp'], ['sqk'])
                dma(mixT_d[u * 128:(u + 1) * 128, tsl], sqk[:, :], ['sqk'], [])

        return head, body, chain_out, epi

    units = [rw_unit(u) for u in range(rw_units)]
    if units:
        for _ in units[0][0]():
            pass
    for u in range(rw_units):
        head_, body_, chain_, epi_ = units[u]
        body_()
        gl = [chain_()]
        if u + 1 < rw_units:
            gl.append(units[u + 1][0]())
        run_window(gl, 2)
        epi_()
    P.barrier()
    A.reset(mB)
    if stop_after == 'B1':
        P.emit()
        return nc

    Yacc = A.alloc("hYacc", [128, T], F32)
    Qt = [A.alloc("Qt", [128, T], BF16) for _ in range(2)]
    NS = A.alloc("hNS", [128, NCH, 2, 128], BF16)
    GCS = [A.alloc("GCS", [128, NCH], F32) for _ in range(2)]
    sgate = A.alloc("hsgate", [128, T], BF16)
    HP = []
    for q_ in range(2):
        HP.append(dict(ld={nm: A.alloc("hld_" + nm, [128, PW], F32) for nm in ('q', 'f0', 'f1', 'i', 'g')},
                       vb=A.alloc("hvb", [128, PW], BF16), Vtm=A.alloc("hVtm", [64, 8, 128], BF16)))
    HS = []
    for q_ in range(4):
        HS.append(dict(t={nm: A.alloc("ht_" + nm, [128, PW], F32) for nm in ('f', 'lf', 'kf', 'G', 'Gd')},
                       eG=A.alloc("heG", [128, PW], BF16), enG=A.alloc("henG", [128, PW], BF16),
                       eGd=A.alloc("heGd", [128, PW], BF16),
                       kb=A.alloc("hkb", [128, PW], BF16), Kd=A.alloc("hKd", [128, PW], BF16),
                       Kdtm=A.alloc("hKdtm", [64, 8, 128], BF16), AT=A.alloc("hAT", [64, 8, 64], BF16)))
    r1 = A.alloc("hr1", [128, PW], F32)
    r2 = A.alloc("hr2", [128, PW], F32)
    osq = A.alloc("osq", [128, PW], BF16)
    obf = A.alloc("obf", [128, PW], BF16)

    def hg_unit(h):
        cbs = {'q': 18 + h, 'f0': 22 + h, 'f1': 26 + h, 'i': 30 + h, 'g': 34 + h}

        def piece_gen(pc):
            q_ = pc % 2
            B_ = HP[q_]
            ld = B_['ld']
            tsl = slice(pc * PW, (pc + 1) * PW)
            for nm in ('f0', 'f1', 'q', 'i', 'g'):
                dma(ld[nm][:, :], pslice(cbs[nm], pc), [], ['hld%d_%s' % (q_, nm)])
            cp('pool', B_['vb'][:, :], ld['i'][:, :], ['hld%d_i' % q_], ['hvb%d' % q_])
            act(sgate[:, tsl], ld['g'][:, :], AF.Silu, ['hld%d_g' % q_], ['hsg%d' % pc])
            yield
            for half in range(2):
                bank, bk = PS()
                for i in range(4):
                    c = half * 4 + i
                    mm(bank[0:64, i * 128:(i + 1) * 128], B_['vb'][:, c * 64:(c + 1) * 64], identb[:, :],
                       ['hvb%d' % q_, 'identb'], [bk])
                cp('dve', B_['Vtm'][:, half * 4:half * 4 + 4, :], bank[0:64, :].rearrange("p (c k) -> p c k", k=128),
                   [bk], ['hVtm%d' % q_])
            yield

        def pair_gen(pc):
            q_ = pc % 2
            B_ = HP[q_]
            ld = B_['ld']
            tsl = slice(pc * PW, (pc + 1) * PW)
            SS = [HS[q_ * 2 + d] for d in range(2)]
            KK = [(lambda nm, sid=q_ * 2 + d: 'h%s_%d' % (nm, sid)) for d in range(2)]
            for d in range(2):
                t, K = SS[d]['t'], KK[d]
                fk = 'f%d' % d
                act(t['f'][:, :], ld[fk][:, :], AF.Sigmoid, ['hld%d_%s' % (q_, fk)], [K('f')])
            for d in range(2):
                t, K = SS[d]['t'], KK[d]
                ts('dve', t['f'][:, :], t['f'][:, :], oml[:, h:h + 1], lb[:, h:h + 1], ALU.mult, ALU.add, [K('f'), 'oml', 'lb'], [K('f')])
            for d in range(2):
                t, K = SS[d]['t'], KK[d]
                act(t['lf'][:, :], t['f'][:, :], AF.Ln, [K('f')], [K('lf')])
                ts('pool', t['kf'][:, :], t['f'][:, :], -1.0, 1.0, ALU.mult, ALU.add, [K('f')], [K('kf')])
            yield
            GT = []
            for d in range(2):
                t, K = SS[d]['t'], KK[d]
                G3 = t['G'][:, :].rearrange("p (c k) -> p c k", k=64)
                if d == 0:
                    scan(t['G'][:, :], rmask[:, :, :].rearrange("p c k -> p (c k)"), t['lf'][:, :], ['rmask', K('lf')], [K('G')])
                    Gtot = G3[:, :, 63:64]
                    Gtot2 = G3[:, :, 63]
                else:
                    scan(t['G'][:, ::-1], rmask[:, :, :].rearrange("p c k -> p (c k)"), t['lf'][:, ::-1], ['rmask', K('lf')], [K('G')])
                    Gtot = G3[:, :, 0:1]
                    Gtot2 = G3[:, :, 0]
                GT.append(Gtot2)
                tt('pool', t['Gd'][:, :].rearrange("p (c k) -> p c k", k=64), Gtot.broadcast_to([128, 8, 64]), G3,
                   ALU.subtract, [K('G')], [K('Gd')])
            yield
            for d in range(2):
                S_, t, K = SS[d], SS[d]['t'], KK[d]
                act(S_['eG'][:, :], t['G'][:, :], AF.Exp, [K('G')], [K('eG')])
                act(S_['enG'][:, :], t['G'][:, :], AF.Exp, [K('G')], [K('enG')], scale=-1.0)
                act(S_['eGd'][:, :], t['Gd'][:, :], AF.Exp, [K('Gd')], [K('eGd')])
                if d == 0:
                    act(GCS[0][:, pc * 8:(pc + 1) * 8], GT[d], AF.Exp, [K('G')], ['GCS%d_%d' % (d, s_) for s_ in range(pc * 8, pc * 8 + 8)])
                else:
                    lo_ = NCH - 8 - pc * 8
                    act(GCS[1][:, lo_:lo_ + 8][:, ::-1], GT[d], AF.Exp, [K('G')], ['GCS%d_%d' % (d, s_) for s_ in range(lo_, lo_ + 8)])
            yield
            for d in range(2):
                S_, t, K = SS[d], SS[d]['t'], KK[d]
                tt('dve', Qt[d][:, tsl], ld['q'][:, :], S_['eG'][:, :], ALU.mult, ['hld%d_q' % q_, K('eG')], ['Qt%d_%d' % (d, pc)])
                tt('pool', S_['kb'][:, :], t['kf'][:, :], S_['enG'][:, :], ALU.mult, [K('kf'), K('enG')], [K('kb')])
                tt('dve', S_['Kd'][:, :], t['kf'][:, :], S_['eGd'][:, :], ALU.mult, [K('kf'), K('eGd')], [K('Kd')])
            yield
            for d in range(2):
                S_, t, K = SS[d], SS[d]['t'], KK[d]
                for half in range(2):
                    bank, bk = PS()
                    for i in range(4):
                        c = half * 4 + i
                        mm(bank[0:64, i * 128:(i + 1) * 128], S_['Kd'][:, c * 64:(c + 1) * 64], identb[:, :], [K('Kd'), 'identb'], [bk])
                    cp('act' if half == 0 else 'dve', S_['Kdtm'][:, half * 4:half * 4 + 4, :],
                       bank[0:64, :].rearrange("p (c k) -> p c k", k=128), [bk], [K('Kdtm')])
                bank, bk = PS()
                for c in range(8):
                    mm(bank[0:64, c * 64:(c + 1) * 64], S_['kb'][:, c * 64:(c + 1) * 64],
                       Qt[d][:, pc * PW + c * 64:pc * PW + (c + 1) * 64], [K('kb'), 'Qt%d_%d' % (d, pc)], [bk])
                tt('dve', S_['AT'][:, :, :], bank[0:64, :].rearrange("p (c k) -> p c k", k=64), maskH[d][:, :, :], ALU.mult,
                   [bk, 'maskH%d' % d], [K('AT')])
            yield
            for d in range(2):
                S_, t, K = SS[d], SS[d]['t'], KK[d]
                bank, bk = PS()
                for c in range(8):
                    mm(bank[:, c * 64:(c + 1) * 64], B_['Vtm'][:, c, :], S_['AT'][:, c, :], ['hVtm%d' % q_, K('AT')], [bk])
                if d == 0:
                    cp('act', Yacc[:, tsl], bank[:, :], [bk], ['hY%d' % pc])
                else:
                    tt('dve', Yacc[:, tsl], bank[:, :], Yacc[:, tsl], ALU.add, [bk, 'hY%d' % pc], ['hY%d' % pc])
                for half in range(2):
                    bank, bk = PS()
                    for i in range(4):
                        c = half * 4 + i
                        mm(bank[:, i * 128:(i + 1) * 128], S_['Kdtm'][:, c, :], B_['Vtm'][:, c, :], [K('Kdtm'), 'hVtm%d' % q_], [bk])
                    for i in range(4):
                        c = half * 4 + i
                        cg = pc * 8 + c
                        st = cg if d == 0 else NCH - 1 - cg
                        cp('act' if half == 0 else 'dve', NS[:, st, d, :], bank[:, i * 128:(i + 1) * 128], [bk], ['hNS%d_%d' % (d, st)])
            yield

        gens = []
        for pc in range(NPC):
            def work(pc=pc):
                yield from piece_gen(pc)
                yield from pair_gen(pc)
            gens.append(work())
        run_window(gens, 2)
        for s in range(1, NCH):
            for d in range(2):
                stt('dve', NS[:, s, d, :], NS[:, s - 1, d, :], GCS[d][:, s:s + 1], NS[:, s, d, :],
                    ALU.mult, ALU.add, ['hNS%d_%d' % (d, s - 1), 'hNS%d_%d' % (d, s), 'GCS%d_%d' % (d, s)], ['hNS%d_%d' % (d, s)])
        for d in range(2):
            for pc in range(NPC):
                bank, bk = PS()
                lo, hi = 0, 8
                for c in range(8):
                    cg = pc * 8 + c
                    st = cg if d == 0 else NCH - 1 - cg
                    if st == 0:
                        if d == 0:
                            lo = 1
                        else:
                            hi = 7
                        continue
                    mm(bank[:, c * 64:(c + 1) * 64], NS[:, st - 1, d, :], Qt[d][:, cg * 64:(cg + 1) * 64],
                       ['hNS%d_%d' % (d, st - 1), 'Qt%d_%d' % (d, pc)], [bk])
                tsl2 = slice(pc * PW + lo * 64, pc * PW + hi * 64)
                tt('dve', Yacc[:, tsl2], bank[:, lo * 64:hi * 64], Yacc[:, tsl2], ALU.add, [bk, 'hY%d' % pc], ['hY%d' % pc])
        for pc in range(NPC):
            tsl = slice(pc * PW, (pc + 1) * PW)
            act(osq[:, :], Yacc[:, tsl], AF.Square, ['hY%d' % pc], ['osq'])
            bank, bk = PS()
            mm(bank[:, :], onesb[:, :], osq[:, :], ['onesb', 'osq'], [bk])
            act(r1[:, :], bank[:, :], AF.Ln, [bk, 'epsn'], ['hr1'], bias=epsn[:, 0:1], scale=1.0 / 128)
            act(r1[:, :], r1[:, :], AF.Exp, ['hr1'], ['hr1'], scale=-0.5)
            tt('dve', r2[:, :], Yacc[:, tsl], r1[:, :], ALU.mult, ['hY%d' % pc, 'hr1'], ['hr2'])
            stt('dve', obf[:, :], r2[:, :], pcol('hgn', 0), sgate[:, tsl], ALU.mult, ALU.mult,
                ['hr2', 'par', 'hsg%d' % pc], ['obf'])
            dma(mixT_d[512 + h * 128:512 + (h + 1) * 128, tsl], obf[:, :], ['obf'], [], eng='pool')

    for h in range(hg_units):
        hg_unit(h)
    P.barrier()
    A.reset(mB)
    if stop_after == 'B':
        P.emit()
        return nc

    wob = A.alloc("wob", [128, 8, D], BF16)
    gpost = A.alloc("gpost", [128, D], F32)
    wost = [A.alloc("wost", [128, D], F32) for _ in range(2)]
    dma(gpost[:, :], gpost_d[:, :], [], ['gpost'])
    wout_v = wout_d.rearrange("(kc p) n -> p kc n", p=128)
    for kc in range(8):
        dma(wost[kc % 2][:, :], wout_v[:, kc, :], [], ['wost%d' % (kc % 2)])
        cp('dve' if kc % 2 == 0 else 'act', wob[:, kc, :], wost[kc % 2][:, :], ['wost%d' % (kc % 2)], ['wob'])
    NSL = 4
    mx = [A.alloc("mx", [128, 8, 128], BF16) for _ in range(NSL)]
    xt = [A.alloc("xt", [128, D], F32) for _ in range(NSL)]
    ot = [A.alloc("ot", [128, D], F32) for _ in range(NSL)]
    junk = A.alloc("junk", [128, 512], BF16)
    ss = [A.alloc("ss", [128, 4], F32) for _ in range(NSL)]
    mix_v = mixT_d.rearrange("(kc p) t -> p kc t", p=128)
    NT = T // 128

    def c_load(tt_i):
        s = tt_i % NSL
        rows = slice(tt_i * 128, (tt_i + 1) * 128)
        dma(mx[s][:, :, :], mix_v[:, :, rows], [], ['mx%d' % s])
        dma(xt[s][:, :], x_d[rows, :], [], ['xt%d' % s])

    for i in range(min(NSL - 1, NT)):
        c_load(i)
    for tt_i in range(NT):
        s = tt_i % NSL
        rows = slice(tt_i * 128, (tt_i + 1) * 128)
        if tt_i + NSL - 1 < NT:
            c_load(tt_i + NSL - 1)
        banks = []
        for half in range(2):
            bank, bk = PS()
            banks.append((bank, bk))
            for kc in range(8):
                mm(bank[:, :], mx[s][:, kc, :], wob[:, kc, half * 512:(half + 1) * 512], ['mx%d' % s, 'wob'], [bk],
                   start=(kc == 0), stop=(kc == 7))
            act(junk[:, :], bank[:, :], AF.Square, [bk], ['junk', 'ss%d' % s], accum=ss[s][:, half:half + 1])
        act(ss[s][:, 2:3], ss[s][:, 0:1], AF.Identity, ['ss%d' % s], ['ss%d' % s], bias=ss[s][:, 1:2])
        act(ss[s][:, 3:4], ss[s][:, 2:3], AF.Sqrt, ['ss%d' % s, 'epsn'], ['ss%d' % s], bias=epsn[:, 0:1], scale=1.0 / D)
        recip(ss[s][:, 3:4], ss[s][:, 3:4], ['ss%d' % s], ['ss%d' % s])
        for half in range(2):
            bank, bk = banks[half]
            hs = slice(half * 512, (half + 1) * 512)
            stt('dve', ot[s][:, hs], bank[:, :], ss[s][:, 3:4], gpost[:, hs], ALU.mult, ALU.mult,
                [bk, 'ss%d' % s, 'gpost'], ['ot%d' % s])
        tt('dve', ot[s][:, :], ot[s][:, :], xt[s][:, :], ALU.add, ['ot%d' % s, 'xt%d' % s], ['ot%d' % s])
        dma(out_d[rows, :], ot[s][:, :], ['ot%d' % s], [], eng='pool')
    P.emit()
    return nc


def prep_inputs(inputs, b):
    f = lambda a: np.ascontiguousarray(np.asarray(a, dtype=np.float32))
    x = f(inputs['x'])
    par = np.zeros((128, NPAR), np.float32)
    cols = []
    cols.append(f(inputs['pre_norm_g'])[0].reshape(8, 128).T)
    cols.append(f(inputs['rw_shift_prev'])[0].reshape(18, 128).T)
    cols.append(f(inputs['rw_shift_next'])[0].reshape(18, 128).T)
    cols.append(f(inputs['rw_w0'])[0].reshape(8, 128).T)
    cols.append(f(inputs['rw_a0'])[0].reshape(8, 128).T)
    cols.append(f(inputs['rw_k_k'])[0].reshape(4, 128).T)
    cols.append(f(inputs['rw_k_a'])[0].reshape(4, 128).T)
    cols.append(f(inputs['rw_r_k'])[0].reshape(4, 128).T)
    cols.append(f(inputs['rw_ln_w'])[0].reshape(4, 128).T)
    cols.append(f(inputs['rw_ln_b'])[0].reshape(4, 128).T)
    lbl = f(inputs['hg_lb_logits'])
    cols.append(lbl[0].reshape(4, 128).T)
    cols.append(lbl[1].reshape(4, 128).T)
    cols.append(f(inputs['hg_norm_g'])[0].reshape(1, 128).T)
    allc = np.concatenate(cols, axis=1)
    par[:, :allc.shape[1]] = allc
    m = {
        "xT": np.ascontiguousarray(x[b].T),
        "x": np.ascontiguousarray(x[b]),
        "w_in": f(inputs['w_in'])[0],
        "w_out": f(inputs['w_out'])[0],
        "par": par,
        "w2s": f(inputs['rw_w2'])[0].reshape(128, 512),
        "a2s": f(inputs['rw_a2'])[0].reshape(128, 512),
        "gpost": np.ascontiguousarray(np.broadcast_to(f(inputs['post_norm_g'])[0][None, :], (128, D))),
    }
    return m


def kernel(**inputs):
    nc = build()
    in_maps = [prep_inputs(inputs, c % 4) for c in range(8)]
    res = run_bass_kernel_spmd(nc, in_maps, core_ids=list(range(8)))
    out = np.stack([np.asarray(res.results[c]["out"], dtype=np.float32) for c in range(4)], axis=0)
    return out
```

```python
import numpy as np
import concourse.bass as bass
import concourse.mybir as mybir
from concourse.bass_utils import run_bass_kernel_spmd

F32 = mybir.dt.float32
BF16 = mybir.dt.bfloat16
AF = mybir.ActivationFunctionType
ALU = mybir.AluOpType

T = 4096
D = 1024
NPC = 8
PW = 512
C = 64
NCH = 64
NCB = 38
INC = 4864
CDEC = 0.6065306597126334
NORM_EPS = 1e-6
GN_EPS = 64e-5
HG_LVL = 99
STAGGER = 0
TRANSITIVE = True
NPAR = 96


class Prog:
    ENG = ('pe', 'act', 'dve', 'pool', 'sp')

    def __init__(self, nc, n_dma_sems=10):
        self.nc = nc
        self.ops = {e: [] for e in self.ENG}
        self.sems = {e: nc.alloc_semaphore(name='s_' + e) for e in self.ENG}
        self.dma_sems = [nc.alloc_semaphore(name='d%d' % i) for i in range(n_dma_sems)]
        self.dma_val = [0] * n_dma_sems
        self.dma_rr = 0
        self.count = {e: 0 for e in self.ENG}
        self.seen = {e: {} for e in self.ENG}
        self.pending = {e: [] for e in self.ENG}
        self.lastw = {}
        self.readers = {}
        self.clock = {}

    def _need(self, eng, ev, waits):
        if ev is None:
            return
        sid, val, src = ev
        if src == eng and eng == 'pe':
            return
        if self.seen[eng].get(sid, 0) >= val:
            return
        self.seen[eng][sid] = val
        waits.append((sid, val))
        if TRANSITIVE:
            for k2, v2 in self.clock.get((sid, val), {}).items():
                if self.seen[eng].get(k2, 0) < v2:
                    self.seen[eng][k2] = v2

    def op(self, eng, fn, reads=(), writes=(), dma=False):
        waits = []
        for ev in self.pending[eng]:
            self._need(eng, ev, waits)
        self.pending[eng] = []
        for k in reads:
            self._need(eng, self.lastw.get(k), waits)
        for k in writes:
            self._need(eng, self.lastw.get(k), waits)
            for ev in self.readers.get(k, ()):
                self._need(eng, ev, waits)
        if dma:
            di = self.dma_rr
            self.dma_rr = (self.dma_rr + 1) % len(self.dma_sems)
            if self.dma_val[di] > 0:
                self._need(eng, (('d', di), self.dma_val[di], 'dma'), waits)
            self.dma_val[di] += 16
            ev = (('d', di), self.dma_val[di], 'dma')
            inc = (self.dma_sems[di], 16)
        else:
            self.count[eng] += 1
            ev = (('e', eng), self.count[eng], eng)
            inc = (self.sems[eng], 1)
        if TRANSITIVE:
            self.clock[(ev[0], ev[1])] = dict(self.seen[eng])
        for k in writes:
            self.lastw[k] = ev
            self.readers[k] = []
        for k in reads:
            if k not in writes:
                self.readers.setdefault(k, []).append(ev)
        self.ops[eng].append((fn, waits, inc))
        return ev

    def all_events(self):
        evs = [(('e', e), self.count[e], e) for e in self.ENG if self.count[e] > 0]
        evs += [(('d', i), v, 'dma') for i, v in enumerate(self.dma_val) if v > 0]
        return evs

    def barrier(self):
        evs = self.all_events()
        for e in self.ENG:
            self.pending[e] = list(evs)

    def _semh(self, sid):
        return self.dma_sems[sid[1]] if sid[0] == 'd' else self.sems[sid[1]]

    def emit(self):
        nc = self.nc
        final_events = self.all_events()
        with nc.Block() as block:
            def run(engname, engobj):
                for fn, waits, inc in self.ops[engname]:
                    for sid, val in waits:
                        engobj.wait_ge(self._semh(sid), val)
                    fn(engobj).then_inc(inc[0], inc[1])
                if engname == 'sp':
                    for sid, val, _ in final_events:
                        engobj.wait_ge(self._semh(sid), val)

            @block.tensor
            def _(e): run('pe', e)

            @block.scalar
            def _(e): run('act', e)

            @block.vector
            def _(e): run('dve', e)

            @block.gpsimd
            def _(e): run('pool', e)

            @block.sync
            def _(e): run('sp', e)


class Arena:
    def __init__(self, nc, cap):
        self.nc = nc
        self.cur = 16512
        self.cap = cap
        self.n = 0

    def alloc(self, name, shape, dtype):
        sz = 1
        for s in shape[1:]:
            sz *= s
        sz *= 2 if dtype == BF16 else 4
        sz = (sz + 63) // 64 * 64
        off = self.cur
        self.cur += sz
        assert self.cur <= self.cap, (name, self.cur, self.cap)
        self.n += 1
        return self.nc.alloc_sbuf_tensor_at("%s_%d" % (name, self.n), list(shape), dtype, offset=off)

    def mark(self):
        return self.cur

    def reset(self, m):
        print('arena peak', self.cur, 'of', self.cap)
        self.cur = m


def build(stop_after=None, rw_units=4, hg_units=4):
    nc = bass.Bass("TRN2", target_bir_lowering=False)
    xT_d = nc.dram_tensor("xT", [D, T], F32, kind="ExternalInput").ap()
    x_d = nc.dram_tensor("x", [T, D], F32, kind="ExternalInput").ap()
    win_d = nc.dram_tensor("w_in", [D, INC], F32, kind="ExternalInput").ap()
    wout_d = nc.dram_tensor("w_out", [D, D], F32, kind="ExternalInput").ap()
    par_d = nc.dram_tensor("par", [128, NPAR], F32, kind="ExternalInput").ap()
    w2s_d = nc.dram_tensor("w2s", [128, 512], F32, kind="ExternalInput").ap()
    a2s_d = nc.dram_tensor("a2s", [128, 512], F32, kind="ExternalInput").ap()
    gpost_d = nc.dram_tensor("gpost", [128, D], F32, kind="ExternalInput").ap()
    out_d = nc.dram_tensor("out", [T, D], F32, kind="ExternalOutput").ap()
    dbg = stop_after is not None
    pT_d = nc.dram_tensor("pT", [INC, T], F32, kind="ExternalOutput" if dbg else "Internal").ap()
    mixT_d = nc.dram_tensor("mixT", [D, T], BF16, kind="ExternalOutput" if dbg else "Internal").ap()

    P = Prog(nc)
    A = Arena(nc, 229344)
    psb = [nc.alloc_psum_tensor("psb%d" % i, [128, 512], F32) for i in range(8)]
    ps_rr = [0]

    def PS():
        b = ps_rr[0]
        ps_rr[0] = (b + 1) % 8
        return psb[b], "ps%d" % b

    def tt(eng, out, in0, in1, op, r, w):
        P.op(eng, lambda e: e.tensor_tensor(out=out, in0=in0, in1=in1, op=op), reads=r, writes=w)

    def ts(eng, out, in0, s1, s2, op0, op1, r, w):
        if s2 is None:
            P.op(eng, lambda e: e.tensor_scalar(out=out, in0=in0, scalar1=s1, scalar2=None, op0=op0), reads=r, writes=w)
        else:
            P.op(eng, lambda e: e.tensor_scalar(out=out, in0=in0, scalar1=s1, scalar2=s2, op0=op0, op1=op1), reads=r, writes=w)

    def stt(eng, out, in0, scalar, in1, op0, op1, r, w):
        P.op(eng, lambda e: e.scalar_tensor_tensor(out=out, in0=in0, scalar=scalar, in1=in1, op0=op0, op1=op1),
             reads=r, writes=w)

    def act(out, in_, func, r, w, bias=None, scale=1.0, accum=None):
        kw = {}
        if bias is not None:
            kw['bias'] = bias
        if accum is not None:
            kw['accum_out'] = accum
        P.op('act', lambda e: e.activation(out=out, in_=in_, func=func, scale=scale, **kw), reads=r, writes=w)

    def mm(out, lhsT, rhs, r, w, start=True, stop=True):
        P.op('pe', lambda e: e.matmul(out, lhsT=lhsT, rhs=rhs, start=start, stop=stop), reads=r, writes=w)

    def cp(eng, out, in_, r, w):
        if eng == 'act':
            act(out, in_, AF.Identity, r, w)
        else:
            P.op(eng, lambda e: e.tensor_copy(out=out, in_=in_), reads=r, writes=w)

    def dma(out, in_, r, w, eng='sp'):
        return P.op(eng, lambda e: e.dma_start(out=out, in_=in_), reads=r, writes=w, dma=True)

    def mset(eng, ap, val, w):
        P.op(eng, lambda e: e.memset(ap, val), writes=w)

    def recip(out, in_, r, w):
        P.op('dve', lambda e: e.reciprocal(out=out, in_=in_), reads=r, writes=w)

    def scan(out, d0, d1, r, w):
        P.op('dve', lambda e: e.tensor_tensor_scan(out=out, data0=d0, data1=d1, initial=0.0, op0=ALU.mult, op1=ALU.add),
             reads=r, writes=w)

    def asel(out, in_, pattern, cm, cmpop, r, w, base=0):
        P.op('pool', lambda e: e.affine_select(out=out, in_=in_, pattern=pattern, base=base, channel_multiplier=cm,
                                              compare_op=cmpop, fill=0.0), reads=r, writes=w)

    par = A.alloc("par", [128, NPAR], F32)
    dma(par[:, :], par_d[:, :], [], ['par'])
    PO = {}
    o = 0
    for nm, n in (('gpre', 8), ('mup', 18), ('mun', 18), ('w0', 8), ('a0', 8), ('kk', 4), ('ka', 4), ('rk', 4),
                  ('lnw', 4), ('lnb', 4), ('lb0', 4), ('lb1', 4), ('hgn', 1)):
        PO[nm] = o
        o += n
    assert o <= NPAR

    def pcol(nm, i=0):
        return par[:, PO[nm] + i:PO[nm] + i + 1]

    onesf = A.alloc("onesf", [128, 128], F32)
    negf = A.alloc("negf", [128, 128], F32)
    identf = A.alloc("identf", [128, 128], F32)
    identb = A.alloc("identb", [128, 128], BF16)
    onesb = A.alloc("onesb", [128, 128], BF16)
    bones = A.alloc("bones", [128, 128], BF16)
    epsn = A.alloc("epsn", [128, 1], F32)
    epsg = A.alloc("epsg", [128, 1], F32)
    c0 = A.alloc("c0", [128, 18], F32)
    omka = A.alloc("omka", [128, 4], F32)
    lb = A.alloc("lb", [128, 4], F32)
    oml = A.alloc("oml", [128, 4], F32)
    rmask = A.alloc("rmask", [128, 8, 64], F32)
    mset('pool', onesf[:, :], 1.0, ['onesf'])
    mset('pool', negf[:, :], -1.0, ['negf'])
    mset('pool', onesb[:, :], 1.0, ['onesb'])
    mset('pool', bones[:, :], 0.0, ['bones'])
    mset('pool', bones[0:64, 0:64], 1.0, ['bones'])
    mset('pool', bones[64:128, 64:128], 1.0, ['bones'])
    mset('pool', epsn[:, :], NORM_EPS, ['epsn'])
    mset('pool', epsg[:, :], GN_EPS, ['epsg'])
    mset('pool', rmask[:, :, :], 1.0, ['rmask'])
    mset('pool', rmask[:, :, 0:1], 0.0, ['rmask'])
    asel(identf[:, :], onesf[:, :], [[1, 128]], -1, ALU.is_equal, ['onesf'], ['identf'])
    cp('pool', identb[:, :], identf[:, :], ['identf'], ['identb'])
    tt('pool', c0[:, :], par[:, PO['mup']:PO['mup'] + 18], par[:, PO['mun']:PO['mun'] + 18], ALU.add, ['par'], ['c0'])
    ts('pool', c0[:, :], c0[:, :], -1.0, 1.0, ALU.mult, ALU.add, ['c0'], ['c0'])
    ts('pool', omka[:, :], par[:, PO['ka']:PO['ka'] + 4], -1.0, 1.0, ALU.mult, ALU.add, ['par'], ['omka'])
    w0h = A.alloc("w0h", [128, 8], F32)
    a0h = A.alloc("a0h", [128, 8], F32)
    halfc = A.alloc("halfc", [128, 1], F32)
    ts('pool', w0h[:, :], par[:, PO['w0']:PO['w0'] + 8], 0.5, None, ALU.mult, None, ['par'], ['w0h'])
    ts('pool', a0h[:, :], par[:, PO['a0']:PO['a0'] + 8], 0.5, None, ALU.mult, None, ['par'], ['a0h'])
    mset('pool', halfc[:, :], 0.5, ['halfc'])
    tt('pool', lb[:, :], par[:, PO['lb0']:PO['lb0'] + 4], par[:, PO['lb1']:PO['lb1'] + 4], ALU.subtract, ['par'], ['lb'])
    act(lb[:, :], lb[:, :], AF.Sigmoid, ['lb'], ['lb'])
    ts('pool', oml[:, :], lb[:, :], -1.0, 1.0, ALU.mult, ALU.add, ['lb'], ['oml'])

    maskX = [A.alloc("maskX%d" % d, [128, 512], BF16) for d in range(2)]
    maskH = [A.alloc("maskH%d" % d, [64, 8, 64], F32) for d in range(2)]
    signs = A.alloc("signs", [128, 512], F32)
    for d in range(2):
        mset('pool', maskX[d][:, :], 0.0, ['maskX%d' % d])
        if d == 0:
            pat, cm = [[1, 64]], -1
        else:
            pat, cm = [[-1, 64]], 1
        patT, cmT = ([[-1, 64]], 1) if d == 0 else ([[1, 64]], -1)
        for h in range(2):
            pr = slice(64 * h, 64 * h + 64)
            asel(maskX[d][pr, 64 * h:64 * h + 64], onesf[pr, 0:64], pat, cm, ALU.is_gt, ['onesf'], ['maskX%d' % d])
            asel(maskX[d][pr, 128:192], onesf[pr, 0:64], pat, cm, ALU.is_ge, ['onesf'], ['maskX%d' % d])
            asel(maskX[d][pr, 192 + 64 * h:192 + 64 * h + 64], negf[pr, 0:64], patT, cmT, ALU.is_gt, ['negf'], ['maskX%d' % d])
            asel(maskX[d][pr, 320 + 64 * h:320 + 64 * h + 64], negf[pr, 0:64], pat, cm, ALU.is_gt, ['negf'], ['maskX%d' % d])
            asel(maskX[d][pr, 448:512], onesf[pr, 0:64], pat, cm, ALU.is_ge, ['onesf'], ['maskX%d' % d])
            if h == 0:
                for q in range(8):
                    asel(maskH[d][0:64, q, :], onesf[0:64, 0:64], pat, cm, ALU.is_ge, ['onesf'], ['maskH%d' % d])
    mset('pool', signs[:, :], 1.0, ['signs'])
    mset('pool', signs[:, 128:256], -1.0, ['signs'])
    mset('pool', signs[:, 384:512], -1.0, ['signs'])

    w2b = A.alloc("w2b", [128, 512], BF16)
    a2b = A.alloc("a2b", [128, 512], BF16)
    mconst = A.mark()
    stg = [A.alloc("stg", [128, 512], F32) for _ in range(2)]
    dma(stg[0][:, :], w2s_d[:, :], [], ['stg0'])
    cp('dve', w2b[:, :], stg[0][:, :], ['stg0'], ['w2b'])
    dma(stg[1][:, :], a2s_d[:, :], [], ['stg1'])
    cp('dve', a2b[:, :], stg[1][:, :], ['stg1'], ['a2b'])

    xTb = A.alloc("xTb", [128, 8, T], BF16)
    rstd = A.alloc("rstd", [128, PW], F32)
    xst = [A.alloc("xst", [128, 8, PW], F32) for _ in range(2)]
    sq = [A.alloc("sq", [128, PW], BF16) for _ in range(2)]
    tmpa = A.alloc("tmpa", [128, PW], F32)
    i_x = 0
    for pc in range(NPC):
        tsl = slice(pc * PW, (pc + 1) * PW)
        s = pc % 2
        bank, bk = PS()
        for kc in range(8):
            s2 = i_x % 2
            i_x += 1
            dma(xst[s][:, kc, :], xT_d[kc * 128:(kc + 1) * 128, tsl], [], ['xst%d_%d' % (s, kc)])
            act(sq[s2][:, :], xst[s][:, kc, :], AF.Square, ['xst%d_%d' % (s, kc)], ['sq%d' % s2])
            mm(bank[:, :], onesb[:, :], sq[s2][:, :], ['onesb', 'sq%d' % s2], [bk], start=(kc == 0), stop=(kc == 7))
        act(tmpa[:, :], bank[:, :], AF.Ln, [bk, 'epsn'], ['tmpa'], bias=epsn[:, 0:1], scale=1.0 / D)
        act(rstd[:, :], tmpa[:, :], AF.Exp, ['tmpa'], ['rstd'], scale=-0.5)
        for kc in range(8):
            stt('dve', xTb[:, kc, tsl], xst[s][:, kc, :], pcol('gpre', kc), rstd[:, :], ALU.mult, ALU.mult,
                ['xst%d_%d' % (s, kc), 'par', 'rstd'], ['xTb%d' % pc])

    wst = [A.alloc("wst", [128, 8, 128], F32) for _ in range(2)]
    wb = [A.alloc("wb", [128, 8, 128], BF16) for _ in range(2)]
    pcb = [A.alloc("pcb", [128, T + 2], F32) for _ in range(3)]
    pout = [A.alloc("pout", [128, PW], F32) for _ in range(6)]
    for s in range(3):
        mset('pool', pcb[s][:, 0:1], 0.0, ['pcbh%d' % s])
        mset('pool', pcb[s][:, T + 1:T + 2], 0.0, ['pcbh%d' % s])
    win_v = win_d.rearrange("(kc p) n -> p kc n", p=128)

    def load_w(cb):
        s = cb % 2
        dma(wst[s][:, :, :], win_v[:, :, cb * 128:(cb + 1) * 128], [], ['wst%d' % s])
        cp('act', wb[s][:, :, :], wst[s][:, :, :], ['wst%d' % s], ['wb%d' % s])

    i_po = 0
    load_w(0)
    for cb in range(NCB):
        s = cb % 2
        sp_ = cb % 3
        if cb + 1 < NCB:
            load_w(cb + 1)
        for pc in range(NPC):
            tsl = slice(pc * PW, (pc + 1) * PW)
            bank, bk = PS()
            for kc in range(8):
                mm(bank[:, :], wb[s][:, kc, :], xTb[:, kc, tsl], ['wb%d' % s, 'xTb%d' % pc], [bk],
                   start=(kc == 0), stop=(kc == 7))
            cp('act', pcb[sp_][:, 1 + pc * PW:1 + (pc + 1) * PW], bank[:, :], [bk], ['pcb%d_%d' % (sp_, pc)])
        for pc in range(NPC):
            rk = ['pcb%d_%d' % (sp_, q) for q in (pc - 1, pc, pc + 1) if 0 <= q < NPC] + ['pcbh%d' % sp_]
            if cb < 18:
                po = pout[i_po % 6]
                pk = 'pout%d' % (i_po % 6)
                i_po += 1
                b0 = 1 + pc * PW
                act(po[:, :], pcb[sp_][:, b0:b0 + PW], AF.Identity, rk + ['c0'], [pk], scale=c0[:, cb:cb + 1])
                stt('dve', po[:, :], pcb[sp_][:, b0 - 1:b0 - 1 + PW], pcol('mup', cb), po[:, :], ALU.mult, ALU.add,
                    rk + ['par', pk], [pk])
                stt('dve', po[:, :], pcb[sp_][:, b0 + 1:b0 + 1 + PW], pcol('mun', cb), po[:, :], ALU.mult, ALU.add,
                    rk + ['par', pk], [pk])
                dma(pT_d[cb * 128:(cb + 1) * 128, pc * PW:(pc + 1) * PW], po[:, :], [pk], [], eng='pool')
            else:
                dma(pT_d[cb * 128:(cb + 1) * 128, pc * PW:(pc + 1) * PW], pcb[sp_][:, 1 + pc * PW:1 + (pc + 1) * PW],
                    rk, [], eng='pool')
    P.barrier()
    A.reset(mconst)
    if stop_after == 'A':
        P.emit()
        return nc

    def pslice(cb, pc):
        return pT_d[cb * 128:(cb + 1) * 128, pc * PW:(pc + 1) * PW]

    mB = A.mark()
    Yacc = A.alloc("Yacc", [128, T], F32)
    Rhat = [A.alloc("Rhat", [128, T], BF16) for _ in range(2)]
    MTS = A.alloc("MTS", [128, NCH, 2, 128], BF16)
    NS = A.alloc("NS", [128, NCH, 2, 128], BF16)
    bonus = A.alloc("bonus", [128, T], BF16)
    sgp = A.alloc("sgp", [128, PW], BF16)
    GC = A.alloc("GC", [128, 2, NCH], F32)
    ld = {nm: A.alloc("ld_" + nm, [128, PW], F32) for nm in ('r', 'k', 'v', 'lw', 'la')}
    th = A.alloc("th", [128, PW], BF16)
    lab = A.alloc("lab", [128, PW], BF16)
    f32t = {nm: A.alloc("t_" + nm, [128, PW], F32) for nm in
            ('sg', 'a', 'S', 'Sd', 'eGx', 'kkm', 'nrm', 'kap', 'kd0', 'kd1')}
    f32t['Sx'] = f32t['sg']
    f32t['t1'] = f32t['nrm']
    f32t['b'] = f32t['kkm']
    for nm in ('eG', 'enG', 'eGd', 'eGxb'):
        f32t[nm] = A.alloc("t_" + nm, [128, PW], BF16)
    sqk = A.alloc("sqk", [128, PW], BF16)
    QR = A.alloc("QR", [128, 8, 192], BF16)
    bbd = A.alloc("bbd", [128, 8, 128], BF16)
    kbd = A.alloc("kbd", [128, 8, 128], BF16)
    Kdbd = A.alloc("Kdbd", [128, 8, 128], BF16)
    Bdbd = A.alloc("Bdbd", [128, 8, 128], BF16)
    vbd = A.alloc("vbd", [128, 8, 128], BF16)
    Vtm = A.alloc("Vtm", [128, 8, 128], BF16)
    Kdtm = A.alloc("Kdtm", [128, 8, 128], BF16)
    GB = []
    for g_ in range(2):
        GB.append(dict(AT=A.alloc("AT", [128, 4, 640], BF16),
                       LX=[A.alloc("LX", [128, 4, 256], BF16) for _ in range(2)],
                       Ms=[A.alloc("Ms", [128, 4, 128], BF16) for _ in range(2)],
                       KA=A.alloc("KA", [128, 4, 256], BF16),
                       KU=A.alloc("KU", [128, 4, 256], BF16)))
    for tns, nm in ((QR, 'QR'), (bbd, 'bbd'), (kbd, 'kbd'), (Kdbd, 'Kdbd'), (Bdbd, 'Bdbd'), (vbd, 'vbd')):
        mset('pool', tns[:, :, :], 0.0, [nm])

    def run_window(gens, width):
        pend = list(gens)
        active = []
        while pend or active:
            while pend and len(active) < width:
                active.append(pend.pop(0))
            for g_ in list(active):
                try:
                    next(g_)
                except StopIteration:
                    active.remove(g_)

    def rw_unit(u):
        r_cb, k_cb, v_cb, g_cb = u, 4 + u, 8 + u, 12 + u
        usl = slice(u * 128, (u + 1) * 128)
        t = f32t
        v3 = lambda ap: ap.rearrange("p (c k) -> p c k", k=64)

        def piece_a(pc):
            tsl = slice(pc * PW, (pc + 1) * PW)
            for nm, cb in (('lw', 16), ('la', 17), ('k', k_cb), ('r', r_cb), ('v', v_cb)):
                dma(ld[nm][:, :], pslice(cb, pc), [], ['ld_' + nm])
            act(th[:, :], ld['lw'][:, :], AF.Tanh, ['ld_lw'], ['th'])
            cp('pool', lab[:, :], ld['la'][:, :], ['ld_la'], ['lab'])
            t = f32t
            act(t['kkm'][:, :], ld['k'][:, :], AF.Identity, ['ld_k', 'par'], ['kkm'], scale=pcol('kk', u))
            act(sqk[:, :], t['kkm'][:, :], AF.Square, ['kkm'], ['sqk'])
            bank, bk = PS()
            mm(bank[:, :], bones[:, :], sqk[:, :], ['bones', 'sqk'], [bk])
            act(t['nrm'][:, :], bank[:, :], AF.Sqrt, [bk], ['nrm'])
            ts('dve', t['nrm'][:, :], t['nrm'][:, :], 1e-12, None, ALU.max, None, ['nrm'], ['nrm'])
            recip(t['nrm'][:, :], t['nrm'][:, :], ['nrm'], ['nrm'])
            tt('pool', t['kap'][:, :], t['kkm'][:, :], t['nrm'][:, :], ALU.mult, ['kkm', 'nrm'], ['kap'])
            yield

        def piece_b(pc):
            for h in range(2):
                pr = slice(64 * h, 64 * h + 64)
                cp('act' if h == 0 else 'dve', vbd[pr, :, 64 * h:64 * h + 64], ld['v'][pr, :].rearrange("p (c k) -> p c k", k=64),
                   ['ld_v'], ['vbd'])
            for half in range(2):
                bank, bk = PS()
                for i in range(4):
                    c = half * 4 + i
                    mm(bank[:, i * 128:(i + 1) * 128], vbd[:, c, :], identb[:, :], ['vbd', 'identb'], [bk])
                cp('act', Vtm[:, half * 4:half * 4 + 4, :], bank[:, :].rearrange("p (c k) -> p c k", k=128), [bk], ['Vtm'])

        def dprep(pc, d):
            dsl = slice(64 * d, 64 * d + 64)
            bw, bkw = PS()
            mm(bw[:, :], w2b[dsl, usl], th[dsl, :], ['w2b', 'th'], [bkw])
            ba, bka = PS()
            mm(ba[:, :], a2b[dsl, usl], lab[dsl, :], ['a2b', 'lab'], [bka])
            act(t['sg'][:, :], bw[:, :], AF.Tanh, [bkw, 'w0h'], ['sg'], scale=0.5, bias=w0h[:, d * 4 + u:d * 4 + u + 1])
            act(t['a'][:, :], ba[:, :], AF.Tanh, [bka, 'a0h'], ['a'], scale=0.5, bias=a0h[:, d * 4 + u:d * 4 + u + 1])
            act(t['sg'][:, :], t['sg'][:, :], AF.Identity, ['sg', 'halfc'], ['sg'], scale=0.5, bias=halfc[:, 0:1])
            act(t['a'][:, :], t['a'][:, :], AF.Identity, ['a', 'halfc'], ['a'], scale=0.5, bias=halfc[:, 0:1])
            yield
            S3 = t['S'][:, :].rearrange("p (c k) -> p c k", k=64)
            if d == 0:
                scan(t['S'][:, :], rmask[:, :, :].rearrange("p c k -> p (c k)"), t['sg'][:, :], ['rmask', 'sg'], ['S'])
                Stot = S3[:, :, 63:64]
                Stot2 = S3[:, :, 63]
            else:
                scan(t['S'][:, ::-1], rmask[:, :, :].rearrange("p c k -> p (c k)"), t['sg'][:, ::-1], ['rmask', 'sg'], ['S'])
                Stot = S3[:, :, 0:1]
                Stot2 = S3[:, :, 0]
            tt('pool', t['Sx'][:, :], t['S'][:, :], t['sg'][:, :], ALU.subtract, ['S', 'sg'], ['sg'])
            tt('pool', t['Sd'][:, :].rearrange("p (c k) -> p c k", k=64), Stot.broadcast_to([128, 8, 64]), S3,
               ALU.subtract, ['S'], ['Sd'])
            yield
            act(t['eG'][:, :], t['S'][:, :], AF.Exp, ['S'], ['eG'], scale=-CDEC)
            act(t['eGxb'][:, :], t['Sx'][:, :], AF.Exp, ['sg'], ['eGxb'], scale=-CDEC)
            act(t['enG'][:, :], t['S'][:, :], AF.Exp, ['S'], ['enG'], scale=CDEC)
            act(t['eGd'][:, :], t['Sd'][:, :], AF.Exp, ['Sd'], ['eGd'], scale=-CDEC)
            act(GC[:, d, pc * 8:(pc + 1) * 8], Stot2, AF.Exp, ['S'], ['GC%d' % pc], scale=-CDEC)
            yield
            kd = t['kd%d' % d]
            kdk = 'kd%d' % d
            act(t['t1'][:, :], t['a'][:, :], AF.Identity, ['a', 'par', 'omka'], ['nrm'], scale=pcol('ka', u),
                bias=omka[:, u:u + 1])
            tt('pool', kd[:, :], t['nrm'][:, :], ld['k'][:, :], ALU.mult, ['nrm', 'ld_k'], [kdk])
            tt('pool', t['kkm'][:, :], t['kap'][:, :], t['a'][:, :], ALU.mult, ['kap', 'a'], ['kkm'])
            yield

        def dprod(pc, d):
            kd = t['kd%d' % d]
            kdk = 'kd%d' % d
            v3 = lambda ap: ap.rearrange("p (c k) -> p c k", k=64)
            for h in range(2):
                pr = slice(64 * h, 64 * h + 64)
                fs = slice(64 * h, 64 * h + 64)
                tt('dve', Kdbd[pr, :, fs], v3(kd[pr, :]), v3(t['eGd'][pr, :]), ALU.mult, [kdk, 'eGd'], ['Kdbd'])
                tt('dve', Bdbd[pr, :, fs], v3(t['kkm'][pr, :]), v3(t['eGd'][pr, :]), ALU.mult, ['kkm', 'eGd'], ['Bdbd'])
            tt('dve', QR[:, :, 128:192], v3(ld['r'][:, :]), v3(t['eG'][:, :]), ALU.mult, ['ld_r', 'eG'], ['QR'])
            for h in range(2):
                pr = slice(64 * h, 64 * h + 64)
                fs = slice(64 * h, 64 * h + 64)
                e1 = 'pool' if h == 0 else 'dve'
                tt(e1, QR[pr, :, fs], v3(t['kap'][pr, :]), v3(t['eGxb'][pr, :]), ALU.mult, ['kap', 'eGxb'], ['QR'])
                tt(e1, bbd[pr, :, fs], v3(t['kkm'][pr, :]), v3(t['enG'][pr, :]), ALU.mult, ['kkm', 'enG'], ['bbd'])
                tt(e1, kbd[pr, :, fs], v3(kd[pr, :]), v3(t['enG'][pr, :]), ALU.mult, [kdk, 'enG'], ['kbd'])
            for half in range(2):
                G_ = GB[half]
                for src, sk, dst, dk_, eng_ in ((QR, 'QR', G_['KA'][:, :, 0:128], 'KAk%d' % half, 'act'),
                                              (Kdbd, 'Kdbd', Kdtm[:, half * 4:half * 4 + 4, :], 'Kdtm%d' % half, 'dve'),
                                              (Bdbd, 'Bdbd', G_['AT'][:, :, 512:640], 'ATB%d' % half, 'act')):
                    bank, bk = PS()
                    for i in range(4):
                        c = half * 4 + i
                        mm(bank[:, i * 128:(i + 1) * 128], src[:, c, 0:128], identb[:, :], [sk, 'identb'], [bk])
                    cp(eng_, dst, bank[:, :].rearrange("p (c k) -> p c k", k=128), [bk], [dk_])

        def grp_gen(grp, pc, d):
            G_ = GB[grp]
            AT, LX, Ms, KA, KU = G_['AT'], G_['LX'], G_['Ms'], G_['KA'], G_['KU']
            cs = [grp * 4 + i for i in range(4)]
            atk = lambda i: 'AT%d_%d' % (grp, i)
            lxk = lambda q, i: 'LX%d_%d_%d' % (grp, q, i)
            msk = lambda q, i: 'Ms%d_%d_%d' % (grp, q, i)
            for i, c in enumerate(cs):
                bank, bk = PS()
                mm(bank[:, 0:192], kbd[:, c, :], QR[:, c, :], ['kbd', 'QR'], [bk])
                mm(bank[:, 192:320], QR[:, c, 0:128], bbd[:, c, :], ['bbd', 'QR'], [bk])
                mm(bank[:, 320:512], bbd[:, c, :], QR[:, c, :], ['bbd', 'QR'], [bk])
                tt('dve', AT[:, i, 0:512], bank[:, :], maskX[d][:, :], ALU.mult, [bk, 'maskX%d' % d], [atk(i)])
                tt('pool', LX[0][:, i, 128:256], AT[:, i, 320:448], identb[:, :], ALU.add, [atk(i), 'identb'], [lxk(0, i) + 'x'])
            yield
            cur = 0
            for j in range(6):
                nxt = 1 - cur if j >= 1 else 0
                if j == 0:
                    bankA, bkA = PS()
                    for i in range(4):
                        mm(bankA[:, i * 128:(i + 1) * 128], AT[:, i, 192:320], AT[:, i, 320:448], [atk(i)], [bkA])
                    banksA = [(bankA, bkA)]
                elif j <= 3:
                    banksA = []
                    for pair in range(2):
                        bankA, bkA = PS()
                        banksA.append((bankA, bkA))
                        for jj in range(2):
                            i = pair * 2 + jj
                            mm(bankA[:, jj * 256:(jj + 1) * 256], Ms[cur][:, i, :], LX[cur][:, i, :],
                               [msk(cur, i), lxk(cur, i), lxk(cur, i) + 'x'], [bkA])
                else:
                    bankA, bkA = PS()
                    for i in range(4):
                        mm(bankA[:, i * 128:(i + 1) * 128], Ms[cur][:, i, :], LX[cur][:, i, 128:256],
                           [msk(cur, i), lxk(cur, i) + 'x'], [bkA])
                    banksA = [(bankA, bkA)]
                if j <= 4:
                    bankB, bkB = PS()
                    for i in range(4):
                        if j == 0:
                            mm(bankB[:, i * 128:(i + 1) * 128], AT[:, i, 320:448], AT[:, i, 192:320], [atk(i)], [bkB])
                        else:
                            mm(bankB[:, i * 128:(i + 1) * 128], LX[cur][:, i, 0:128], Ms[cur][:, i, :],
                               [lxk(cur, i), msk(cur, i)], [bkB])
                mnx = 0 if j == 0 else nxt
                if j <= 4:
                    cp('act', Ms[mnx][:, :, :], bankB[:, :].rearrange("p (c k) -> p c k", k=128), [bkB],
                       [msk(mnx, i) for i in range(4)])
                if j == 0:
                    cp('act', LX[0][:, :, 0:128], bankA[:, :].rearrange("p (c k) -> p c k", k=128), [bkA],
                       [lxk(0, i) for i in range(4)])
                elif j <= 3:
                    for pair in range(2):
                        bankA, bkA = banksA[pair]
                        b3 = bankA[:, :].rearrange("p (c k) -> p c k", k=256)
                        ii = [pair * 2, pair * 2 + 1]
                        cp('act', LX[nxt][:, pair * 2:pair * 2 + 2, 0:128], b3[:, :, 0:128], [bkA], [lxk(nxt, i) for i in ii])
                        tt('dve', LX[nxt][:, pair * 2:pair * 2 + 2, 128:256], b3[:, :, 128:256],
                           LX[cur][:, pair * 2:pair * 2 + 2, 128:256], ALU.add,
                           [bkA] + [lxk(cur, i) + 'x' for i in ii] + [lxk(nxt, i) for i in ii], [lxk(nxt, i) + 'x' for i in ii])
                else:
                    tt('dve', LX[nxt][:, :, 128:256], bankA[:, :].rearrange("p (c k) -> p c k", k=128),
                       LX[cur][:, :, 128:256], ALU.add, [bkA] + [lxk(cur, i) + 'x' for i in range(4)],
                       [lxk(nxt, i) + 'x' for i in range(4)])
                cur = mnx if j == 0 else nxt
                yield
            xk = lambda i: lxk(cur, i) + 'x'
            bank, bk = PS()
            for i, c in enumerate(cs):
                mm(bank[:, i * 128:(i + 1) * 128], AT[:, i, 0:128], Vtm[:, c, :], [atk(i), 'Vtm'], [bk])
            act(KA[:, :, 128:256], bank[:, :].rearrange("p (c k) -> p c k", k=128), AF.Identity, [bk], ['KAv%d' % grp], scale=-1.0)
            yield
            for pair in range(2):
                bank, bk = PS()
                for jj in range(2):
                    i = pair * 2 + jj
                    mm(bank[:, jj * 256:(jj + 1) * 256], LX[cur][:, i, 128:256], KA[:, i, :],
                       [xk(i), 'KAk%d' % grp, 'KAv%d' % grp], [bk])
                cp('act', KU[:, pair * 2:pair * 2 + 2, :], bank[:, :].rearrange("p (c k) -> p c k", k=256), [bk],
                   ['KU%d_%d' % (grp, pair)])
            kuk = lambda i: 'KU%d_%d' % (grp, i // 2)
            t0 = pc * PW + grp * 256
            yield
            for pair in range(2):
                bank, bk = PS()
                for jj in range(2):
                    i = pair * 2 + jj
                    mm(bank[:, jj * 192:(jj + 1) * 192], KU[:, i, 0:128], AT[:, i, 448:640], [kuk(i), atk(i), 'ATB%d' % grp], [bk])
                b3 = bank[:, 0:384].rearrange("p (c k) -> p c k", k=192)
                ta = t0 + pair * 128
                tt('dve', Rhat[d][:, ta:ta + 128].rearrange("p (c k) -> p c k", k=64),
                   QR[:, grp * 4 + pair * 2:grp * 4 + pair * 2 + 2, 128:192], b3[:, :, 0:64],
                   ALU.subtract, [bk, 'QR'], ['Rhat%d_%d' % (d, pc)])
                for jj in range(2):
                    i = pair * 2 + jj
                    cg = pc * 8 + cs[i]
                    st = cg if d == 0 else NCH - 1 - cg
                    stt('dve', MTS[:, st, d, :], identf[:, :], GC[:, d, cg:cg + 1], bank[:, jj * 192 + 64:(jj + 1) * 192],
                        ALU.mult, ALU.subtract, [bk, 'identf', 'GC%d' % pc], ['MTS%d' % st])
            bank, bk = PS()
            for i, c in enumerate(cs):
                mm(bank[:, i * 64:(i + 1) * 64], Vtm[:, c, :], AT[:, i, 128:192], ['Vtm', atk(i)], [bk], start=True, stop=False)
                mm(bank[:, i * 64:(i + 1) * 64], KU[:, i, 128:256], AT[:, i, 448:512], [kuk(i), atk(i)], [bk], start=False, stop=True)
            if d == 0:
                cp('act', Yacc[:, t0:t0 + 256], bank[:, 0:256], [bk], ['Yacc%d' % pc])
            else:
                tt('dve', Yacc[:, t0:t0 + 256], bank[:, 0:256], Yacc[:, t0:t0 + 256], ALU.add,
                   [bk, 'Yacc%d' % pc], ['Yacc%d' % pc])
            yield
            bank, bk = PS()
            for i, c in enumerate(cs):
                mm(bank[:, i * 128:(i + 1) * 128], Kdtm[:, c, :], Vtm[:, c, :], ['Kdtm%d' % grp, 'Vtm'], [bk], start=True, stop=False)
                mm(bank[:, i * 128:(i + 1) * 128], AT[:, i, 512:640], KU[:, i, 128:256], ['ATB%d' % grp, kuk(i)], [bk],
                   start=False, stop=True)
            for i, c in enumerate(cs):
                cg = pc * 8 + c
                st = cg if d == 0 else NCH - 1 - cg
                cp('act', NS[:, st, d, :], bank[:, i * 128:(i + 1) * 128], [bk], ['NS%d' % st])
            yield

        def bonus_f(pc):
            tsl = slice(pc * PW, (pc + 1) * PW)
            t = f32t
            tt('pool', t['nrm'][:, :], t['kd0'][:, :], t['kd1'][:, :], ALU.add, ['kd0', 'kd1'], ['nrm'])
            tt('pool', t['nrm'][:, :], t['nrm'][:, :], ld['r'][:, :], ALU.mult, ['nrm', 'ld_r'], ['nrm'])
            act(sqk[:, :], t['nrm'][:, :], AF.Identity, ['nrm', 'par'], ['sqk'], scale=pcol('rk', u))
            bank, bk = PS()
            mm(bank[:, :], bones[:, :], sqk[:, :], ['bones', 'sqk'], [bk])
            tt('dve', bonus[:, tsl], bank[:, :], ld['v'][:, :], ALU.mult, [bk, 'ld_v'], ['bonus%d' % pc])

        def interleave(gens, stagger=STAGGER):
            gens = list(gens)
            for _ in range(stagger):
                try:
                    next(gens[0])
                except StopIteration:
                    gens.pop(0)
                    break
            while gens:
                for g_ in list(gens):
                    try:
                        next(g_)
                    except StopIteration:
                        gens.remove(g_)

        def head():
            yield from piece_a(0)
            piece_b(0)
            yield
            yield from dprep(0, 0)
            dprod(0, 0)
            yield

        def body():
            for pc in range(NPC):
                interleave([grp_gen(0, pc, 0), grp_gen(1, pc, 0), dprep(pc, 1)])
                dprod(pc, 1)
                bonus_f(pc)
                nxt = []
                if pc + 1 < NPC:
                    def nxt_gen(pc=pc):
                        yield from piece_a(pc + 1)
                        yield from dprep(pc + 1, 0)
                    nxt = [nxt_gen()]
                interleave([grp_gen(0, pc, 1), grp_gen(1, pc, 1)] + nxt)
                if pc + 1 < NPC:
                    piece_b(pc + 1)
                    dprod(pc + 1, 0)


        def chain_out():
            for s in range(1, NCH):
                bank, bk = PS()
                for d in range(2):
                    mm(bank[:, d * 128:(d + 1) * 128], MTS[:, s, d, :], NS[:, s - 1, d, :], ['MTS%d' % s, 'NS%d' % (s - 1)], [bk])
                tt('dve', NS[:, s, :, :], bank[:, 0:256].rearrange("p (c k) -> p c k", k=128), NS[:, s, :, :], ALU.add,
                   [bk, 'NS%d' % s], ['NS%d' % s])
                if s % 2 == 0:
                    yield
            for d in range(2):
                for pc in range(NPC):
                    bank, bk = PS()
                    n = 0
                    for c in range(8):
                        cg = pc * 8 + c
                        st = cg if d == 0 else NCH - 1 - cg
                        if st == 0:
                            continue
                        mm(bank[:, c * 64:(c + 1) * 64], NS[:, st - 1, d, :], Rhat[d][:, cg * 64:(cg + 1) * 64],
                           ['NS%d' % (st - 1), 'Rhat%d_%d' % (d, pc)], [bk])
                    lo, hi = 0, 8
                    if d == 0 and pc == 0:
                        lo = 1
                    if d == 1 and pc == NPC - 1:
                        hi = 7
                    tsl2 = slice(pc * PW + lo * 64, pc * PW + hi * 64)
                    tt('dve', Yacc[:, tsl2], bank[:, lo * 64:hi * 64], Yacc[:, tsl2], ALU.add, [bk, 'Yacc%d' % pc], ['Yacc%d' % pc])
                    yield

        def epi():
            EY = GB[0]['KU'][:, :, :].rearrange("p a b -> p (a b)").rearrange("p (a b) -> p a b", b=PW)
            t = f32t
            for pc in range(NPC):
                tsl = slice(pc * PW, (pc + 1) * PW)
                cp('pool', EY[:, 0, :], Yacc[:, tsl], ['Yacc%d' % pc], ['KU0_0', 'KU0_1'])
                act(EY[:, 1, :], Yacc[:, tsl], AF.Square, ['Yacc%d' % pc], ['KU0_0', 'KU0_1'])
                b1, bk1 = PS()
                mm(b1[:, :], bones[:, :], EY[:, 0, :], ['bones', 'KU0_0', 'KU0_1'], [bk1])
                b2, bk2 = PS()
                mm(b2[:, :], bones[:, :], EY[:, 1, :], ['bones', 'KU0_0', 'KU0_1'], [bk2])
                act(t['S'][:, :], b1[:, :], AF.Identity, [bk1], ['S'], scale=1.0 / 64)
                tt('pool', t['sg'][:, :], t['S'][:, :], t['S'][:, :], ALU.mult, ['S'], ['sg'])
                stt('dve', t['Sd'][:, :], b2[:, :], 1.0 / 64, t['sg'][:, :], ALU.mult, ALU.subtract, [bk2, 'sg'], ['Sd'])
                act(t['a'][:, :], t['Sd'][:, :], AF.Ln, ['Sd', 'epsg'], ['a'], bias=epsg[:, 0:1])
                act(t['a'][:, :], t['a'][:, :], AF.Exp, ['a'], ['a'], scale=-0.5)
                tt('pool', t['eGx'][:, :], Yacc[:, tsl], t['S'][:, :], ALU.subtract, ['Yacc%d' % pc, 'S'], ['eGx'])
                tt('pool', t['eGx'][:, :], t['eGx'][:, :], t['a'][:, :], ALU.mult, ['eGx', 'a'], ['eGx'])
                act(t['eGx'][:, :], t['eGx'][:, :], AF.Identity, ['eGx', 'par'], ['eGx'], scale=pcol('lnw', u), bias=pcol('lnb', u))
                tt('dve', t['eGx'][:, :], t['eGx'][:, :], bonus[:, tsl], ALU.add, ['eGx', 'bonus%d' % pc], ['eGx'])
                dma(ld['lw'][:, :], pslice(g_cb, pc), [], ['ld_lw'])
                act(sgp[:, :], ld['lw'][:, :], AF.Silu, ['ld_lw'], ['sgp'])
                tt('dve', sqk[:, :], t['eGx'][:, :], sgp[:, :], ALU.mult, ['eGx', 'sgp'], ['sqk'])
                dma(mixT_d[u * 128:(u + 1) * 128, tsl], sqk[:, :], ['sqk'], [])

        return head, body, chain_out, epi

    units = [rw_unit(u) for u in range(rw_units)]
    if units:
        for _ in units[0][0]():
            pass
    for u in range(rw_units):
        head_, body_, chain_, epi_ = units[u]
        body_()
        gl = [chain_()]
        if u + 1 < rw_units:
            gl.append(units[u + 1][0]())
        run_window(gl, 2)
        epi_()
    P.barrier()
    A.reset(mB)
    if stop_after == 'B1':
        P.emit()
        return nc

    Yacc = A.alloc("hYacc", [128, T], F32)
    Qt = [A.alloc("Qt", [128, T], BF16) for _ in range(2)]
    NS = A.alloc("hNS", [128, NCH, 2, 128], BF16)
    GCS = [A.alloc("GCS", [128, NCH], F32) for _ in range(2)]
    sgate = A.alloc("hsgate", [128, T], BF16)
    HP = []
    for q_ in range(2):
        HP.append(dict(ld={nm: A.alloc("hld_" + nm, [128, PW], F32) for nm in ('q', 'f0', 'f1', 'i', 'g')},
                       vb=A.alloc("hvb", [128, PW], BF16), Vtm=A.alloc("hVtm", [64, 8, 128], BF16)))
    HS = []
    for q_ in range(4):
        HS.append(dict(t={nm: A.alloc("ht_" + nm, [128, PW], F32) for nm in ('f', 'lf', 'kf', 'G', 'Gd')},
                       eG=A.alloc("heG", [128, PW], BF16), enG=A.alloc("henG", [128, PW], BF16),
                       eGd=A.alloc("heGd", [128, PW], BF16),
                       kb=A.alloc("hkb", [128, PW], BF16), Kd=A.alloc("hKd", [128, PW], BF16),
                       Kdtm=A.alloc("hKdtm", [64, 8, 128], BF16), AT=A.alloc("hAT", [64, 8, 64], BF16)))
    r1 = A.alloc("hr1", [128, PW], F32)
    r2 = A.alloc("hr2", [128, PW], F32)
    osq = A.alloc("osq", [128, PW], BF16)
    obf = A.alloc("obf", [128, PW], BF16)

    def hg_unit(h):
        cbs = {'q': 18 + h, 'f0': 22 + h, 'f1': 26 + h, 'i': 30 + h, 'g': 34 + h}

        def piece_gen(pc):
            q_ = pc % 2
            B_ = HP[q_]
            ld = B_['ld']
            tsl = slice(pc * PW, (pc + 1) * PW)
            for nm in ('f0', 'f1', 'q', 'i', 'g'):
                dma(ld[nm][:, :], pslice(cbs[nm], pc), [], ['hld%d_%s' % (q_, nm)])
            cp('pool', B_['vb'][:, :], ld['i'][:, :], ['hld%d_i' % q_], ['hvb%d' % q_])
            act(sgate[:, tsl], ld['g'][:, :], AF.Silu, ['hld%d_g' % q_], ['hsg%d' % pc])
            yield
            for half in range(2):
                bank, bk = PS()
                for i in range(4):
                    c = half * 4 + i
                    mm(bank[0:64, i * 128:(i + 1) * 128], B_['vb'][:, c * 64:(c + 1) * 64], identb[:, :],
                       ['hvb%d' % q_, 'identb'], [bk])
                cp('dve', B_['Vtm'][:, half * 4:half * 4 + 4, :], bank[0:64, :].rearrange("p (c k) -> p c k", k=128),
                   [bk], ['hVtm%d' % q_])
            yield

        def pair_gen(pc):
            q_ = pc % 2
            B_ = HP[q_]
            ld = B_['ld']
            tsl = slice(pc * PW, (pc + 1) * PW)
            SS = [HS[q_ * 2 + d] for d in range(2)]
            KK = [(lambda nm, sid=q_ * 2 + d: 'h%s_%d' % (nm, sid)) for d in range(2)]
            for d in range(2):
                t, K = SS[d]['t'], KK[d]
                fk = 'f%d' % d
                act(t['f'][:, :], ld[fk][:, :], AF.Sigmoid, ['hld%d_%s' % (q_, fk)], [K('f')])
            for d in range(2):
                t, K = SS[d]['t'], KK[d]
                ts('dve', t['f'][:, :], t['f'][:, :], oml[:, h:h + 1], lb[:, h:h + 1], ALU.mult, ALU.add, [K('f'), 'oml', 'lb'], [K('f')])
            for d in range(2):
                t, K = SS[d]['t'], KK[d]
                act(t['lf'][:, :], t['f'][:, :], AF.Ln, [K('f')], [K('lf')])
                ts('pool', t['kf'][:, :], t['f'][:, :], -1.0, 1.0, ALU.mult, ALU.add, [K('f')], [K('kf')])
            yield
            GT = []
            for d in range(2):
                t, K = SS[d]['t'], KK[d]
                G3 = t['G'][:, :].rearrange("p (c k) -> p c k", k=64)
                if d == 0:
                    scan(t['G'][:, :], rmask[:, :, :].rearrange("p c k -> p (c k)"), t['lf'][:, :], ['rmask', K('lf')], [K('G')])
                    Gtot = G3[:, :, 63:64]
                    Gtot2 = G3[:, :, 63]
                else:
                    scan(t['G'][:, ::-1], rmask[:, :, :].rearrange("p c k -> p (c k)"), t['lf'][:, ::-1], ['rmask', K('lf')], [K('G')])
                    Gtot = G3[:, :, 0:1]
                    Gtot2 = G3[:, :, 0]
                GT.append(Gtot2)
                tt('pool', t['Gd'][:, :].rearrange("p (c k) -> p c k", k=64), Gtot.broadcast_to([128, 8, 64]), G3,
                   ALU.subtract, [K('G')], [K('Gd')])
            yield
            for d in range(2):
                S_, t, K = SS[d], SS[d]['t'], KK[d]
                act(S_['eG'][:, :], t['G'][:, :], AF.Exp, [K('G')], [K('eG')])
                act(S_['enG'][:, :], t['G'][:, :], AF.Exp, [K('G')], [K('enG')], scale=-1.0)
                act(S_['eGd'][:, :], t['Gd'][:, :], AF.Exp, [K('Gd')], [K('eGd')])
                if d == 0:
                    act(GCS[0][:, pc * 8:(pc + 1) * 8], GT[d], AF.Exp, [K('G')], ['GCS%d_%d' % (d, s_) for s_ in range(pc * 8, pc * 8 + 8)])
                else:
                    lo_ = NCH - 8 - pc * 8
                    act(GCS[1][:, lo_:lo_ + 8][:, ::-1], GT[d], AF.Exp, [K('G')], ['GCS%d_%d' % (d, s_) for s_ in range(lo_, lo_ + 8)])
            yield
            for d in range(2):
                S_, t, K = SS[d], SS[d]['t'], KK[d]
                tt('dve', Qt[d][:, tsl], ld['q'][:, :], S_['eG'][:, :], ALU.mult, ['hld%d_q' % q_, K('eG')], ['Qt%d_%d' % (d, pc)])
                tt('pool', S_['kb'][:, :], t['kf'][:, :], S_['enG'][:, :], ALU.mult, [K('kf'), K('enG')], [K('kb')])
                tt('dve', S_['Kd'][:, :], t['kf'][:, :], S_['eGd'][:, :], ALU.mult, [K('kf'), K('eGd')], [K('Kd')])
            yield
            for d in range(2):
                S_, t, K = SS[d], SS[d]['t'], KK[d]
                for half in range(2):
                    bank, bk = PS()
                    for i in range(4):
                        c = half * 4 + i
                        mm(bank[0:64, i * 128:(i + 1) * 128], S_['Kd'][:, c * 64:(c + 1) * 64], identb[:, :], [K('Kd'), 'identb'], [bk])
                    cp('act' if half == 0 else 'dve', S_['Kdtm'][:, half * 4:half * 4 + 4, :],
                       bank[0:64, :].rearrange("p (c k) -> p c k", k=128), [bk], [K('Kdtm')])
                bank, bk = PS()
                for c in range(8):
                    mm(bank[0:64, c * 64:(c + 1) * 64], S_['kb'][:, c * 64:(c + 1) * 64],
                       Qt[d][:, pc * PW + c * 64:pc * PW + (c + 1) * 64], [K('kb'), 'Qt%d_%d' % (d, pc)], [bk])
                tt('dve', S_['AT'][:, :, :], bank[0:64, :].rearrange("p (c k) -> p c k", k=64), maskH[d][:, :, :], ALU.mult,
                   [bk, 'maskH%d' % d], [K('AT')])
            yield
            for d in range(2):
                S_, t, K = SS[d], SS[d]['t'], KK[d]
                bank, bk = PS()
                for c in range(8):
                    mm(bank[:, c * 64:(c + 1) * 64], B_['Vtm'][:, c, :], S_['AT'][:, c, :], ['hVtm%d' % q_, K('AT')], [bk])
                if d == 0:
                    cp('act', Yacc[:, tsl], bank[:, :], [bk], ['hY%d' % pc])
                else:
                    tt('dve', Yacc[:, tsl], bank[:, :], Yacc[:, tsl], ALU.add, [bk, 'hY%d' % pc], ['hY%d' % pc])
                for half in range(2):
                    bank, bk = PS()
                    for i in range(4):
                        c = half * 4 + i
                        mm(bank[:, i * 128:(i + 1) * 128], S_['Kdtm'][:, c, :], B_['Vtm'][:, c, :], [K('Kdtm'), 'hVtm%d' % q_], [bk])
                    for i in range(4):
                        c = half * 4 + i
                        cg = pc * 8 + c
                        st = cg if d == 0 else NCH - 1 - cg
                        cp('act' if half == 0 else 'dve', NS[:, st, d, :], bank[:, i * 128:(i + 1) * 128], [bk], ['hNS%d_%d' % (d, st)])
            yield

        gens = []
        for pc in range(NPC):
            def work(pc=pc):
                yield from piece_gen(pc)
                yield from pair_gen(pc)
            gens.append(work())
        run_window(gens, 2)
        for s in range(1, NCH):
            for d in range(2):
                stt('dve', NS[:, s, d, :], NS[:, s - 1, d, :], GCS[d][:, s:s + 1], NS[:, s, d, :],
                    ALU.mult, ALU.add, ['hNS%d_%d' % (d, s - 1), 'hNS%d_%d' % (d, s), 'GCS%d_%d' % (d, s)], ['hNS%d_%d' % (d, s)])
        for d in range(2):
            for pc in range(NPC):
                bank, bk = PS()
                lo, hi = 0, 8
                for c in range(8):
                    cg = pc * 8 + c
                    st = cg if d == 0 else NCH - 1 - cg
                    if st == 0:
                        if d == 0:
                            lo = 1
                        else:
                            hi = 7
                        continue
                    mm(bank[:, c * 64:(c + 1) * 64], NS[:, st - 1, d, :], Qt[d][:, cg * 64:(cg + 1) * 64],
                       ['hNS%d_%d' % (d, st - 1), 'Qt%d_%d' % (d, pc)], [bk])
                tsl2 = slice(pc * PW + lo * 64, pc * PW + hi * 64)
                tt('dve', Yacc[:, tsl2], bank[:, lo * 64:hi * 64], Yacc[:, tsl2], ALU.add, [bk, 'hY%d' % pc], ['hY%d' % pc])
        for pc in range(NPC):
            tsl = slice(pc * PW, (pc + 1) * PW)
            act(osq[:, :], Yacc[:, tsl], AF.Square, ['hY%d' % pc], ['osq'])
            bank, bk = PS()
            mm(bank[:, :], onesb[:, :], osq[:, :], ['onesb', 'osq'], [bk])
            act(r1[:, :], bank[:, :], AF.Ln, [bk, 'epsn'], ['hr1'], bias=epsn[:, 0:1], scale=1.0 / 128)
            act(r1[:, :], r1[:, :], AF.Exp, ['hr1'], ['hr1'], scale=-0.5)
            tt('dve', r2[:, :], Yacc[:, tsl], r1[:, :], ALU.mult, ['hY%d' % pc, 'hr1'], ['hr2'])
            stt('dve', obf[:, :], r2[:, :], pcol('hgn', 0), sgate[:, tsl], ALU.mult, ALU.mult,
                ['hr2', 'par', 'hsg%d' % pc], ['obf'])
            dma(mixT_d[512 + h * 128:512 + (h + 1) * 128, tsl], obf[:, :], ['obf'], [], eng='pool')

    for h in range(hg_units):
        hg_unit(h)
    P.barrier()
    A.reset(mB)
    if stop_after == 'B':
        P.emit()
        return nc

    wob = A.alloc("wob", [128, 8, D], BF16)
    gpost = A.alloc("gpost", [128, D], F32)
    wost = [A.alloc("wost", [128, D], F32) for _ in range(2)]
    dma(gpost[:, :], gpost_d[:, :], [], ['gpost'])
    wout_v = wout_d.rearrange("(kc p) n -> p kc n", p=128)
    for kc in range(8):
        dma(wost[kc % 2][:, :], wout_v[:, kc, :], [], ['wost%d' % (kc % 2)])
        cp('dve' if kc % 2 == 0 else 'act', wob[:, kc, :], wost[kc % 2][:, :], ['wost%d' % (kc % 2)], ['wob'])
    NSL = 4
    mx = [A.alloc("mx", [128, 8, 128], BF16) for _ in range(NSL)]
    xt = [A.alloc("xt", [128, D], F32) for _ in range(NSL)]
    ot = [A.alloc("ot", [128, D], F32) for _ in range(NSL)]
    junk = A.alloc("junk", [128, 512], BF16)
    ss = [A.alloc("ss", [128, 4], F32) for _ in range(NSL)]
    mix_v = mixT_d.rearrange("(kc p) t -> p kc t", p=128)
    NT = T // 128

    def c_load(tt_i):
        s = tt_i % NSL
        rows = slice(tt_i * 128, (tt_i + 1) * 128)
        dma(mx[s][:, :, :], mix_v[:, :, rows], [], ['mx%d' % s])
        dma(xt[s][:, :], x_d[rows, :], [], ['xt%d' % s])

    for i in range(min(NSL - 1, NT)):
        c_load(i)
    for tt_i in range(NT):
        s = tt_i % NSL
        rows = slice(tt_i * 128, (tt_i + 1) * 128)
        if tt_i + NSL - 1 < NT:
            c_load(tt_i + NSL - 1)
        banks = []
        for half in range(2):
            bank, bk = PS()
            banks.append((bank, bk))
            for kc in range(8):
                mm(bank[:, :], mx[s][:, kc, :], wob[:, kc, half * 512:(half + 1) * 512], ['mx%d' % s, 'wob'], [bk],
                   start=(kc == 0), stop=(kc == 7))
            act(junk[:, :], bank[:, :], AF.Square, [bk], ['junk', 'ss%d' % s], accum=ss[s][:, half:half + 1])
        act(ss[s][:, 2:3], ss[s][:, 0:1], AF.Identity, ['ss%d' % s], ['ss%d' % s], bias=ss[s][:, 1:2])
        act(ss[s][:, 3:4], ss[s][:, 2:3], AF.Sqrt, ['ss%d' % s, 'epsn'], ['ss%d' % s], bias=epsn[:, 0:1], scale=1.0 / D)
        recip(ss[s][:, 3:4], ss[s][:, 3:4], ['ss%d' % s], ['ss%d' % s])
        for half in range(2):
            bank, bk = banks[half]
            hs = slice(half * 512, (half + 1) * 512)
            stt('dve', ot[s][:, hs], bank[:, :], ss[s][:, 3:4], gpost[:, hs], ALU.mult, ALU.mult,
                [bk, 'ss%d' % s, 'gpost'], ['ot%d' % s])
        tt('dve', ot[s][:, :], ot[s][:, :], xt[s][:, :], ALU.add, ['ot%d' % s, 'xt%d' % s], ['ot%d' % s])
        dma(out_d[rows, :], ot[s][:, :], ['ot%d' % s], [], eng='pool')
    P.emit()
    return nc


def prep_inputs(inputs, b):
    f = lambda a: np.ascontiguousarray(np.asarray(a, dtype=np.float32))
    x = f(inputs['x'])
    par = np.zeros((128, NPAR), np.float32)
    cols = []
    cols.append(f(inputs['pre_norm_g'])[0].reshape(8, 128).T)
    cols.append(f(inputs['rw_shift_prev'])[0].reshape(18, 128).T)
    cols.append(f(inputs['rw_shift_next'])[0].reshape(18, 128).T)
    cols.append(f(inputs['rw_w0'])[0].reshape(8, 128).T)
    cols.append(f(inputs['rw_a0'])[0].reshape(8, 128).T)
    cols.append(f(inputs['rw_k_k'])[0].reshape(4, 128).T)
    cols.append(f(inputs['rw_k_a'])[0].reshape(4, 128).T)
    cols.append(f(inputs['rw_r_k'])[0].reshape(4, 128).T)
    cols.append(f(inputs['rw_ln_w'])[0].reshape(4, 128).T)
    cols.append(f(inputs['rw_ln_b'])[0].reshape(4, 128).T)
    lbl = f(inputs['hg_lb_logits'])
    cols.append(lbl[0].reshape(4, 128).T)
    cols.append(lbl[1].reshape(4, 128).T)
    cols.append(f(inputs['hg_norm_g'])[0].reshape(1, 128).T)
    allc = np.concatenate(cols, axis=1)
    par[:, :allc.shape[1]] = allc
    m = {
        "xT": np.ascontiguousarray(x[b].T),
        "x": np.ascontiguousarray(x[b]),
        "w_in": f(inputs['w_in'])[0],
        "w_out": f(inputs['w_out'])[0],
        "par": par,
        "w2s": f(inputs['rw_w2'])[0].reshape(128, 512),
        "a2s": f(inputs['rw_a2'])[0].reshape(128, 512),
        "gpost": np.ascontiguousarray(np.broadcast_to(f(inputs['post_norm_g'])[0][None, :], (128, D))),
    }
    return m


def kernel(**inputs):
    nc = build()
    in_maps = [prep_inputs(inputs, c % 4) for c in range(8)]
    res = run_bass_kernel_spmd(nc, in_maps, core_ids=list(range(8)))
    out = np.stack([np.asarray(res.results[c]["out"], dtype=np.float32) for c in range(4)], axis=0)
    return out
```

```python
import numpy as np
import concourse.bass as bass
import concourse.mybir as mybir
from concourse.bass_utils import run_bass_kernel_spmd

F32 = mybir.dt.float32
BF16 = mybir.dt.bfloat16
AF = mybir.ActivationFunctionType
ALU = mybir.AluOpType

T = 4096
D = 1024
NPC = 8
PW = 512
C = 64
NCH = 64
NCB = 38
INC = 4864
CDEC = 0.6065306597126334
NORM_EPS = 1e-6
GN_EPS = 64e-5
HG_LVL = 99
STAGGER = 0
TRANSITIVE = True
NPAR = 96


class Prog:
    ENG = ('pe', 'act', 'dve', 'pool', 'sp')

    def __init__(self, nc, n_dma_sems=10):
        self.nc = nc
        self.ops = {e: [] for e in self.ENG}
        self.sems = {e: nc.alloc_semaphore(name='s_' + e) for e in self.ENG}
        self.dma_sems = [nc.alloc_semaphore(name='d%d' % i) for i in range(n_dma_sems)]
        self.dma_val = [0] * n_dma_sems
        self.dma_rr = 0
        self.count = {e: 0 for e in self.ENG}
        self.seen = {e: {} for e in self.ENG}
        self.pending = {e: [] for e in self.ENG}
        self.lastw = {}
        self.readers = {}
        self.clock = {}

    def _need(self, eng, ev, waits):
        if ev is None:
            return
        sid, val, src = ev
        if src == eng and eng == 'pe':
            return
        if self.seen[eng].get(sid, 0) >= val:
            return
        self.seen[eng][sid] = val
        waits.append((sid, val))
        if TRANSITIVE:
            for k2, v2 in self.clock.get((sid, val), {}).items():
                if self.seen[eng].get(k2, 0) < v2:
                    self.seen[eng][k2] = v2

    def op(self, eng, fn, reads=(), writes=(), dma=False):
        waits = []
        for ev in self.pending[eng]:
            self._need(eng, ev, waits)
        self.pending[eng] = []
        for k in reads:
            self._need(eng, self.lastw.get(k), waits)
        for k in writes:
            self._need(eng, self.lastw.get(k), waits)
            for ev in self.readers.get(k, ()):
                self._need(eng, ev, waits)
        if dma:
            di = self.dma_rr
            self.dma_rr = (self.dma_rr + 1) % len(self.dma_sems)
            if self.dma_val[di] > 0:
                self._need(eng, (('d', di), self.dma_val[di], 'dma'), waits)
            self.dma_val[di] += 16
            ev = (('d', di), self.dma_val[di], 'dma')
            inc = (self.dma_sems[di], 16)
        else:
            self.count[eng] += 1
            ev = (('e', eng), self.count[eng], eng)
            inc = (self.sems[eng], 1)
        if TRANSITIVE:
            self.clock[(ev[0], ev[1])] = dict(self.seen[eng])
        for k in writes:
            self.lastw[k] = ev
            self.readers[k] = []
        for k in reads:
            if k not in writes:
                self.readers.setdefault(k, []).append(ev)
        self.ops[eng].append((fn, waits, inc))
        return ev

    def all_events(self):
        evs = [(('e', e), self.count[e], e) for e in self.ENG if self.count[e] > 0]
        evs += [(('d', i), v, 'dma') for i, v in enumerate(self.dma_val) if v > 0]
        return evs

    def barrier(self):
        evs = self.all_events()
        for e in self.ENG:
            self.pending[e] = list(evs)

    def _semh(self, sid):
        return self.dma_sems[sid[1]] if sid[0] == 'd' else self.sems[sid[1]]

    def emit(self):
        nc = self.nc
        final_events = self.all_events()
        with nc.Block() as block:
            def run(engname, engobj):
                for fn, waits, inc in self.ops[engname]:
                    for sid, val in waits:
                        engobj.wait_ge(self._semh(sid), val)
                    fn(engobj).then_inc(inc[0], inc[1])
                if engname == 'sp':
                    for sid, val, _ in final_events:
                        engobj.wait_ge(self._semh(sid), val)

            @block.tensor
            def _(e): run('pe', e)

            @block.scalar
            def _(e): run('act', e)

            @block.vector
            def _(e): run('dve', e)

            @block.gpsimd
            def _(e): run('pool', e)

            @block.sync
            def _(e): run('sp', e)


class Arena:
    def __init__(self, nc, cap):
        self.nc = nc
        self.cur = 16512
        self.cap = cap
        self.n = 0

    def alloc(self, name, shape, dtype):
        sz = 1
        for s in shape[1:]:
            sz *= s
        sz *= 2 if dtype == BF16 else 4
        sz = (sz + 63) // 64 * 64
        off = self.cur
        self.cur += sz
        assert self.cur <= self.cap, (name, self.cur, self.cap)
        self.n += 1
        return self.nc.alloc_sbuf_tensor_at("%s_%d" % (name, self.n), list(shape), dtype, offset=off)

    def mark(self):
        return self.cur

    def reset(self, m):
        print('arena peak', self.cur, 'of', self.cap)
        self.cur = m


def build(stop_after=None, rw_units=4, hg_units=4):
    nc = bass.Bass("TRN2", target_bir_lowering=False)
    xT_d = nc.dram_tensor("xT", [D, T], F32, kind="ExternalInput").ap()
    x_d = nc.dram_tensor("x", [T, D], F32, kind="ExternalInput").ap()
    win_d = nc.dram_tensor("w_in", [D, INC], F32, kind="ExternalInput").ap()
    wout_d = nc.dram_tensor("w_out", [D, D], F32, kind="ExternalInput").ap()
    par_d = nc.dram_tensor("par", [128, NPAR], F32, kind="ExternalInput").ap()
    w2s_d = nc.dram_tensor("w2s", [128, 512], F32, kind="ExternalInput").ap()
    a2s_d = nc.dram_tensor("a2s", [128, 512], F32, kind="ExternalInput").ap()
    gpost_d = nc.dram_tensor("gpost", [128, D], F32, kind="ExternalInput").ap()
    out_d = nc.dram_tensor("out", [T, D], F32, kind="ExternalOutput").ap()
    dbg = stop_after is not None
    pT_d = nc.dram_tensor("pT", [INC, T], F32, kind="ExternalOutput" if dbg else "Internal").ap()
    mixT_d = nc.dram_tensor("mixT", [D, T], BF16, kind="ExternalOutput" if dbg else "Internal").ap()

    P = Prog(nc)
    A = Arena(nc, 229344)
    psb = [nc.alloc_psum_tensor("psb%d" % i, [128, 512], F32) for i in range(8)]
    ps_rr = [0]

    def PS():
        b = ps_rr[0]
        ps_rr[0] = (b + 1) % 8
        return psb[b], "ps%d" % b

    def tt(eng, out, in0, in1, op, r, w):
        P.op(eng, lambda e: e.tensor_tensor(out=out, in0=in0, in1=in1, op=op), reads=r, writes=w)

    def ts(eng, out, in0, s1, s2, op0, op1, r, w):
        if s2 is None:
            P.op(eng, lambda e: e.tensor_scalar(out=out, in0=in0, scalar1=s1, scalar2=None, op0=op0), reads=r, writes=w)
        else:
            P.op(eng, lambda e: e.tensor_scalar(out=out, in0=in0, scalar1=s1, scalar2=s2, op0=op0, op1=op1), reads=r, writes=w)

    def stt(eng, out, in0, scalar, in1, op0, op1, r, w):
        P.op(eng, lambda e: e.scalar_tensor_tensor(out=out, in0=in0, scalar=scalar, in1=in1, op0=op0, op1=op1),
             reads=r, writes=w)

    def act(out, in_, func, r, w, bias=None, scale=1.0, accum=None):
        kw = {}
        if bias is not None:
            kw['bias'] = bias
        if accum is not None:
            kw['accum_out'] = accum
        P.op('act', lambda e: e.activation(out=out, in_=in_, func=func, scale=scale, **kw), reads=r, writes=w)

    def mm(out, lhsT, rhs, r, w, start=True, stop=True):
        P.op('pe', lambda e: e.matmul(out, lhsT=lhsT, rhs=rhs, start=start, stop=stop), reads=r, writes=w)

    def cp(eng, out, in_, r, w):
        if eng == 'act':
            act(out, in_, AF.Identity, r, w)
        else:
            P.op(eng, lambda e: e.tensor_copy(out=out, in_=in_), reads=r, writes=w)

    def dma(out, in_, r, w, eng='sp'):
        return P.op(eng, lambda e: e.dma_start(out=out, in_=in_), reads=r, writes=w, dma=True)

    def mset(eng, ap, val, w):
        P.op(eng, lambda e: e.memset(ap, val), writes=w)

    def recip(out, in_, r, w):
        P.op('dve', lambda e: e.reciprocal(out=out, in_=in_), reads=r, writes=w)

    def scan(out, d0, d1, r, w):
        P.op('dve', lambda e: e.tensor_tensor_scan(out=out, data0=d0, data1=d1, initial=0.0, op0=ALU.mult, op1=ALU.add),
             reads=r, writes=w)

    def asel(out, in_, pattern, cm, cmpop, r, w, base=0):
        P.op('pool', lambda e: e.affine_select(out=out, in_=in_, pattern=pattern, base=base, channel_multiplier=cm,
                                              compare_op=cmpop, fill=0.0), reads=r, writes=w)

    par = A.alloc("par", [128, NPAR], F32)
    dma(par[:, :], par_d[:, :], [], ['par'])
    PO = {}
    o = 0
    for nm, n in (('gpre', 8), ('mup', 18), ('mun', 18), ('w0', 8), ('a0', 8), ('kk', 4), ('ka', 4), ('rk', 4),
                  ('lnw', 4), ('lnb', 4), ('lb0', 4), ('lb1', 4), ('hgn', 1)):
        PO[nm] = o
        o += n
    assert o <= NPAR

    def pcol(nm, i=0):
        return par[:, PO[nm] + i:PO[nm] + i + 1]

    onesf = A.alloc("onesf", [128, 128], F32)
    negf = A.alloc("negf", [128, 128], F32)
    identf = A.alloc("identf", [128, 128], F32)
    identb = A.alloc("identb", [128, 128], BF16)
    onesb = A.alloc("onesb", [128, 128], BF16)
    bones = A.alloc("bones", [128, 128], BF16)
    epsn = A.alloc("epsn", [128, 1], F32)
    epsg = A.alloc("epsg", [128, 1], F32)
    c0 = A.alloc("c0", [128, 18], F32)
    omka = A.alloc("omka", [128, 4], F32)
    lb = A.alloc("lb", [128, 4], F32)
    oml = A.alloc("oml", [128, 4], F32)
    rmask = A.alloc("rmask", [128, 8, 64], F32)
    mset('pool', onesf[:, :], 1.0, ['onesf'])
    mset('pool', negf[:, :], -1.0, ['negf'])
    mset('pool', onesb[:, :], 1.0, ['onesb'])
    mset('pool', bones[:, :], 0.0, ['bones'])
    mset('pool', bones[0:64, 0:64], 1.0, ['bones'])
    mset('pool', bones[64:128, 64:128], 1.0, ['bones'])
    mset('pool', epsn[:, :], NORM_EPS, ['epsn'])
    mset('pool', epsg[:, :], GN_EPS, ['epsg'])
    mset('pool', rmask[:, :, :], 1.0, ['rmask'])
    mset('pool', rmask[:, :, 0:1], 0.0, ['rmask'])
    asel(identf[:, :], onesf[:, :], [[1, 128]], -1, ALU.is_equal, ['onesf'], ['identf'])
    cp('pool', identb[:, :], identf[:, :], ['identf'], ['identb'])
    tt('pool', c0[:, :], par[:, PO['mup']:PO['mup'] + 18], par[:, PO['mun']:PO['mun'] + 18], ALU.add, ['par'], ['c0'])
    ts('pool', c0[:, :], c0[:, :], -1.0, 1.0, ALU.mult, ALU.add, ['c0'], ['c0'])
    ts('pool', omka[:, :], par[:, PO['ka']:PO['ka'] + 4], -1.0, 1.0, ALU.mult, ALU.add, ['par'], ['omka'])
    w0h = A.alloc("w0h", [128, 8], F32)
    a0h = A.alloc("a0h", [128, 8], F32)
    halfc = A.alloc("halfc", [128, 1], F32)
    ts('pool', w0h[:, :], par[:, PO['w0']:PO['w0'] + 8], 0.5, None, ALU.mult, None, ['par'], ['w0h'])
    ts('pool', a0h[:, :], par[:, PO['a0']:PO['a0'] + 8], 0.5, None, ALU.mult, None, ['par'], ['a0h'])
    mset('pool', halfc[:, :], 0.5, ['halfc'])
    tt('pool', lb[:, :], par[:, PO['lb0']:PO['lb0'] + 4], par[:, PO['lb1']:PO['lb1'] + 4], ALU.subtract, ['par'], ['lb'])
    act(lb[:, :], lb[:, :], AF.Sigmoid, ['lb'], ['lb'])
    ts('pool', oml[:, :], lb[:, :], -1.0, 1.0, ALU.mult, ALU.add, ['lb'], ['oml'])
    hc1 = A.alloc("hc1", [128, 4], F32)
    hc2 = A.alloc("hc2", [128, 4], F32)
    ts('pool', hc1[:, :], oml[:, :], 0.5, None, ALU.mult, None, ['oml'], ['hc1'])
    tt('pool', hc2[:, :], lb[:, :], hc1[:, :], ALU.add, ['lb', 'hc1'], ['hc2'])

    maskX = [A.alloc("maskX%d" % d, [128, 512], BF16) for d in range(2)]
    maskH = [A.alloc("maskH%d" % d, [64, 8, 64], F32) for d in range(2)]
    signs = A.alloc("signs", [128, 512], F32)
    for d in range(2):
        mset('pool', maskX[d][:, :], 0.0, ['maskX%d' % d])
        if d == 0:
            pat, cm = [[1, 64]], -1
        else:
            pat, cm = [[-1, 64]], 1
        patT, cmT = ([[-1, 64]], 1) if d == 0 else ([[1, 64]], -1)
        for h in range(2):
            pr = slice(64 * h, 64 * h + 64)
            asel(maskX[d][pr, 64 * h:64 * h + 64], onesf[pr, 0:64], pat, cm, ALU.is_gt, ['onesf'], ['maskX%d' % d])
            asel(maskX[d][pr, 128:192], onesf[pr, 0:64], pat, cm, ALU.is_ge, ['onesf'], ['maskX%d' % d])
            asel(maskX[d][pr, 192 + 64 * h:192 + 64 * h + 64], negf[pr, 0:64], patT, cmT, ALU.is_gt, ['negf'], ['maskX%d' % d])
            asel(maskX[d][pr, 320 + 64 * h:320 + 64 * h + 64], negf[pr, 0:64], pat, cm, ALU.is_gt, ['negf'], ['maskX%d' % d])
            asel(maskX[d][pr, 448:512], onesf[pr, 0:64], pat, cm, ALU.is_ge, ['onesf'], ['maskX%d' % d])
            if h == 0:
                for q in range(8):
                    asel(maskH[d][0:64, q, :], onesf[0:64, 0:64], pat, cm, ALU.is_ge, ['onesf'], ['maskH%d' % d])
    mset('pool', signs[:, :], 1.0, ['signs'])
    mset('pool', signs[:, 128:256], -1.0, ['signs'])
    mset('pool', signs[:, 384:512], -1.0, ['signs'])

    w2b = A.alloc("w2b", [128, 512], BF16)
    a2b = A.alloc("a2b", [128, 512], BF16)
    mconst = A.mark()
    stg = [A.alloc("stg", [128, 512], F32) for _ in range(2)]
    dma(stg[0][:, :], w2s_d[:, :], [], ['stg0'])
    cp('dve', w2b[:, :], stg[0][:, :], ['stg0'], ['w2b'])
    dma(stg[1][:, :], a2s_d[:, :], [], ['stg1'])
    cp('dve', a2b[:, :], stg[1][:, :], ['stg1'], ['a2b'])

    xTb = A.alloc("xTb", [128, 8, T], BF16)
    rstd = A.alloc("rstd", [128, PW], F32)
    xst = [A.alloc("xst", [128, 8, PW], F32) for _ in range(2)]
    sq = [A.alloc("sq", [128, PW], BF16) for _ in range(2)]
    tmpa = A.alloc("tmpa", [128, PW], F32)
    i_x = 0
    for pc in range(NPC):
        tsl = slice(pc * PW, (pc + 1) * PW)
        s = pc % 2
        bank, bk = PS()
        for kc in range(8):
            s2 = i_x % 2
            i_x += 1
            dma(xst[s][:, kc, :], xT_d[kc * 128:(kc + 1) * 128, tsl], [], ['xst%d_%d' % (s, kc)])
            act(sq[s2][:, :], xst[s][:, kc, :], AF.Square, ['xst%d_%d' % (s, kc)], ['sq%d' % s2])
            mm(bank[:, :], onesb[:, :], sq[s2][:, :], ['onesb', 'sq%d' % s2], [bk], start=(kc == 0), stop=(kc == 7))
        act(tmpa[:, :], bank[:, :], AF.Ln, [bk, 'epsn'], ['tmpa'], bias=epsn[:, 0:1], scale=1.0 / D)
        act(rstd[:, :], tmpa[:, :], AF.Exp, ['tmpa'], ['rstd'], scale=-0.5)
        for kc in range(8):
            stt('dve', xTb[:, kc, tsl], xst[s][:, kc, :], pcol('gpre', kc), rstd[:, :], ALU.mult, ALU.mult,
                ['xst%d_%d' % (s, kc), 'par', 'rstd'], ['xTb%d' % pc])

    wst = [A.alloc("wst", [128, 8, 128], F32) for _ in range(2)]
    wb = [A.alloc("wb", [128, 8, 128], BF16) for _ in range(2)]
    pcb = [A.alloc("pcb", [128, T + 2], F32) for _ in range(3)]
    pout = [A.alloc("pout", [128, PW], F32) for _ in range(6)]
    for s in range(3):
        mset('pool', pcb[s][:, 0:1], 0.0, ['pcbh%d' % s])
        mset('pool', pcb[s][:, T + 1:T + 2], 0.0, ['pcbh%d' % s])
    win_v = win_d.rearrange("(kc p) n -> p kc n", p=128)

    def load_w(cb):
        s = cb % 2
        dma(wst[s][:, :, :], win_v[:, :, cb * 128:(cb + 1) * 128], [], ['wst%d' % s])
        cp('act', wb[s][:, :, :], wst[s][:, :, :], ['wst%d' % s], ['wb%d' % s])

    i_po = 0
    load_w(0)
    for cb in range(NCB):
        s = cb % 2
        sp_ = cb % 3
        if cb + 1 < NCB:
            load_w(cb + 1)
        for pc in range(NPC):
            tsl = slice(pc * PW, (pc + 1) * PW)
            bank, bk = PS()
            for kc in range(8):
                mm(bank[:, :], wb[s][:, kc, :], xTb[:, kc, tsl], ['wb%d' % s, 'xTb%d' % pc], [bk],
                   start=(kc == 0), stop=(kc == 7))
            cp('act', pcb[sp_][:, 1 + pc * PW:1 + (pc + 1) * PW], bank[:, :], [bk], ['pcb%d_%d' % (sp_, pc)])
        for pc in range(NPC):
            rk = ['pcb%d_%d' % (sp_, q) for q in (pc - 1, pc, pc + 1) if 0 <= q < NPC] + ['pcbh%d' % sp_]
            if cb < 18:
                po = pout[i_po % 6]
                pk = 'pout%d' % (i_po % 6)
                i_po += 1
                b0 = 1 + pc * PW
                act(po[:, :], pcb[sp_][:, b0:b0 + PW], AF.Identity, rk + ['c0'], [pk], scale=c0[:, cb:cb + 1])
                stt('dve', po[:, :], pcb[sp_][:, b0 - 1:b0 - 1 + PW], pcol('mup', cb), po[:, :], ALU.mult, ALU.add,
                    rk + ['par', pk], [pk])
                stt('dve', po[:, :], pcb[sp_][:, b0 + 1:b0 + 1 + PW], pcol('mun', cb), po[:, :], ALU.mult, ALU.add,
                    rk + ['par', pk], [pk])
                dma(pT_d[cb * 128:(cb + 1) * 128, pc * PW:(pc + 1) * PW], po[:, :], [pk], [], eng='pool')
            else:
                dma(pT_d[cb * 128:(cb + 1) * 128, pc * PW:(pc + 1) * PW], pcb[sp_][:, 1 + pc * PW:1 + (pc + 1) * PW],
                    rk, [], eng='pool')
    P.barrier()
    A.reset(mconst)
    if stop_after == 'A':
        P.emit()
        return nc

    def pslice(cb, pc):
        return pT_d[cb * 128:(cb + 1) * 128, pc * PW:(pc + 1) * PW]

    mB = A.mark()
    Yacc = A.alloc("Yacc", [128, T], F32)
    Rhat = [A.alloc("Rhat", [128, T], BF16) for _ in range(2)]
    MTS = A.alloc("MTS", [128, NCH, 2, 128], BF16)
    NS = A.alloc("NS", [128, NCH, 2, 128], BF16)
    bonus = A.alloc("bonus", [128, T], BF16)
    sgp = A.alloc("sgp", [128, PW], BF16)
    GC = A.alloc("GC", [128, 2, NCH], F32)
    ld = {nm: A.alloc("ld_" + nm, [128, PW], F32) for nm in ('r', 'k', 'v', 'lw', 'la')}
    th = A.alloc("th", [128, PW], BF16)
    lab = A.alloc("lab", [128, PW], BF16)
    f32t = {nm: A.alloc("t_" + nm, [128, PW], F32) for nm in
            ('sg', 'a', 'S', 'Sd', 'eGx', 'kkm', 'nrm', 'kap', 'kd0', 'kd1')}
    f32t['Sx'] = f32t['sg']
    f32t['t1'] = f32t['nrm']
    f32t['b'] = f32t['kkm']
    for nm in ('eG', 'enG', 'eGd', 'eGxb'):
        f32t[nm] = A.alloc("t_" + nm, [128, PW], BF16)
    sqk = A.alloc("sqk", [128, PW], BF16)
    QR = A.alloc("QR", [128, 8, 192], BF16)
    bbd = A.alloc("bbd", [128, 8, 128], BF16)
    kbd = A.alloc("kbd", [128, 8, 128], BF16)
    Kdbd = A.alloc("Kdbd", [128, 8, 128], BF16)
    Bdbd = A.alloc("Bdbd", [128, 8, 128], BF16)
    vbd = A.alloc("vbd", [128, 8, 128], BF16)
    Vtm = A.alloc("Vtm", [128, 8, 128], BF16)
    Kdtm = A.alloc("Kdtm", [128, 8, 128], BF16)
    GB = []
    for g_ in range(2):
        GB.append(dict(AT=A.alloc("AT", [128, 4, 640], BF16),
                       LX=[A.alloc("LX", [128, 4, 256], BF16) for _ in range(2)],
                       Ms=[A.alloc("Ms", [128, 4, 128], BF16) for _ in range(2)],
                       KA=A.alloc("KA", [128, 4, 256], BF16),
                       KU=A.alloc("KU", [128, 4, 256], BF16)))
    for tns, nm in ((QR, 'QR'), (bbd, 'bbd'), (kbd, 'kbd'), (Kdbd, 'Kdbd'), (Bdbd, 'Bdbd'), (vbd, 'vbd')):
        mset('pool', tns[:, :, :], 0.0, [nm])

    def run_window(gens, width):
        pend = list(gens)
        active = []
        while pend or active:
            while pend and len(active) < width:
                active.append(pend.pop(0))
            for g_ in list(active):
                try:
                    next(g_)
                except StopIteration:
                    active.remove(g_)

    def rw_unit(u):
        r_cb, k_cb, v_cb, g_cb = u, 4 + u, 8 + u, 12 + u
        usl = slice(u * 128, (u + 1) * 128)
        t = f32t
        v3 = lambda ap: ap.rearrange("p (c k) -> p c k", k=64)

        def piece_a(pc):
            tsl = slice(pc * PW, (pc + 1) * PW)
            for nm, cb in (('lw', 16), ('la', 17), ('k', k_cb), ('r', r_cb), ('v', v_cb)):
                dma(ld[nm][:, :], pslice(cb, pc), [], ['ld_' + nm])
            act(th[:, :], ld['lw'][:, :], AF.Tanh, ['ld_lw'], ['th'])
            cp('pool', lab[:, :], ld['la'][:, :], ['ld_la'], ['lab'])
            t = f32t
            act(t['kkm'][:, :], ld['k'][:, :], AF.Identity, ['ld_k', 'par'], ['kkm'], scale=pcol('kk', u))
            act(sqk[:, :], t['kkm'][:, :], AF.Square, ['kkm'], ['sqk'])
            bank, bk = PS()
            mm(bank[:, :], bones[:, :], sqk[:, :], ['bones', 'sqk'], [bk])
            act(t['nrm'][:, :], bank[:, :], AF.Sqrt, [bk], ['nrm'])
            ts('dve', t['nrm'][:, :], t['nrm'][:, :], 1e-12, None, ALU.max, None, ['nrm'], ['nrm'])
            recip(t['nrm'][:, :], t['nrm'][:, :], ['nrm'], ['nrm'])
            tt('pool', t['kap'][:, :], t['kkm'][:, :], t['nrm'][:, :], ALU.mult, ['kkm', 'nrm'], ['kap'])
            yield

        def piece_b(pc):
            for h in range(2):
                pr = slice(64 * h, 64 * h + 64)
                cp('act' if h == 0 else 'dve', vbd[pr, :, 64 * h:64 * h + 64], ld['v'][pr, :].rearrange("p (c k) -> p c k", k=64),
                   ['ld_v'], ['vbd'])
            for half in range(2):
                bank, bk = PS()
                for i in range(4):
                    c = half * 4 + i
                    mm(bank[:, i * 128:(i + 1) * 128], vbd[:, c, :], identb[:, :], ['vbd', 'identb'], [bk])
                cp('act', Vtm[:, half * 4:half * 4 + 4, :], bank[:, :].rearrange("p (c k) -> p c k", k=128), [bk], ['Vtm'])

        def dprep(pc, d):
            dsl = slice(64 * d, 64 * d + 64)
            bw, bkw = PS()
            mm(bw[:, :], w2b[dsl, usl], th[dsl, :], ['w2b', 'th'], [bkw])
            ba, bka = PS()
            mm(ba[:, :], a2b[dsl, usl], lab[dsl, :], ['a2b', 'lab'], [bka])
            act(t['sg'][:, :], bw[:, :], AF.Tanh, [bkw, 'w0h'], ['sg'], scale=0.5, bias=w0h[:, d * 4 + u:d * 4 + u + 1])
            act(t['a'][:, :], ba[:, :], AF.Tanh, [bka, 'a0h'], ['a'], scale=0.5, bias=a0h[:, d * 4 + u:d * 4 + u + 1])
            act(t['sg'][:, :], t['sg'][:, :], AF.Identity, ['sg', 'halfc'], ['sg'], scale=0.5, bias=halfc[:, 0:1])
            act(t['a'][:, :], t['a'][:, :], AF.Identity, ['a', 'halfc'], ['a'], scale=0.5, bias=halfc[:, 0:1])
            yield
            S3 = t['S'][:, :].rearrange("p (c k) -> p c k", k=64)
            if d == 0:
                scan(t['S'][:, :], rmask[:, :, :].rearrange("p c k -> p (c k)"), t['sg'][:, :], ['rmask', 'sg'], ['S'])
                Stot = S3[:, :, 63:64]
                Stot2 = S3[:, :, 63]
            else:
                scan(t['S'][:, ::-1], rmask[:, :, :].rearrange("p c k -> p (c k)"), t['sg'][:, ::-1], ['rmask', 'sg'], ['S'])
                Stot = S3[:, :, 0:1]
                Stot2 = S3[:, :, 0]
            tt('pool', t['Sx'][:, :], t['S'][:, :], t['sg'][:, :], ALU.subtract, ['S', 'sg'], ['sg'])
            tt('pool', t['Sd'][:, :].rearrange("p (c k) -> p c k", k=64), Stot.broadcast_to([128, 8, 64]), S3,
               ALU.subtract, ['S'], ['Sd'])
            yield
            act(t['eG'][:, :], t['S'][:, :], AF.Exp, ['S'], ['eG'], scale=-CDEC)
            act(t['eGxb'][:, :], t['Sx'][:, :], AF.Exp, ['sg'], ['eGxb'], scale=-CDEC)
            act(t['enG'][:, :], t['S'][:, :], AF.Exp, ['S'], ['enG'], scale=CDEC)
            act(t['eGd'][:, :], t['Sd'][:, :], AF.Exp, ['Sd'], ['eGd'], scale=-CDEC)
            act(GC[:, d, pc * 8:(pc + 1) * 8], Stot2, AF.Exp, ['S'], ['GC%d' % pc], scale=-CDEC)
            yield
            kd = t['kd%d' % d]
            kdk = 'kd%d' % d
            act(t['t1'][:, :], t['a'][:, :], AF.Identity, ['a', 'par', 'omka'], ['nrm'], scale=pcol('ka', u),
                bias=omka[:, u:u + 1])
            tt('pool', kd[:, :], t['nrm'][:, :], ld['k'][:, :], ALU.mult, ['nrm', 'ld_k'], [kdk])
            tt('pool', t['kkm'][:, :], t['kap'][:, :], t['a'][:, :], ALU.mult, ['kap', 'a'], ['kkm'])
            yield

        def dprod(pc, d):
            kd = t['kd%d' % d]
            kdk = 'kd%d' % d
            v3 = lambda ap: ap.rearrange("p (c k) -> p c k", k=64)
            for h in range(2):
                pr = slice(64 * h, 64 * h + 64)
                fs = slice(64 * h, 64 * h + 64)
                tt('dve', Kdbd[pr, :, fs], v3(kd[pr, :]), v3(t['eGd'][pr, :]), ALU.mult, [kdk, 'eGd'], ['Kdbd'])
                tt('dve', Bdbd[pr, :, fs], v3(t['kkm'][pr, :]), v3(t['eGd'][pr, :]), ALU.mult, ['kkm', 'eGd'], ['Bdbd'])
            tt('dve', QR[:, :, 128:192], v3(ld['r'][:, :]), v3(t['eG'][:, :]), ALU.mult, ['ld_r', 'eG'], ['QR'])
            for h in range(2):
                pr = slice(64 * h, 64 * h + 64)
                fs = slice(64 * h, 64 * h + 64)
                e1 = 'pool' if h == 0 else 'dve'
                tt(e1, QR[pr, :, fs], v3(t['kap'][pr, :]), v3(t['eGxb'][pr, :]), ALU.mult, ['kap', 'eGxb'], ['QR'])
                tt(e1, bbd[pr, :, fs], v3(t['kkm'][pr, :]), v3(t['enG'][pr, :]), ALU.mult, ['kkm', 'enG'], ['bbd'])
                tt(e1, kbd[pr, :, fs], v3(kd[pr, :]), v3(t['enG'][pr, :]), ALU.mult, [kdk, 'enG'], ['kbd'])
            for half in range(2):
                G_ = GB[half]
                for src, sk, dst, dk_, eng_ in ((QR, 'QR', G_['KA'][:, :, 0:128], 'KAk%d' % half, 'act'),
                                              (Kdbd, 'Kdbd', Kdtm[:, half * 4:half * 4 + 4, :], 'Kdtm%d' % half, 'dve'),
                                              (Bdbd, 'Bdbd', G_['AT'][:, :, 512:640], 'ATB%d' % half, 'act')):
                    bank, bk = PS()
                    for i in range(4):
                        c = half * 4 + i
                        mm(bank[:, i * 128:(i + 1) * 128], src[:, c, 0:128], identb[:, :], [sk, 'identb'], [bk])
                    cp(eng_, dst, bank[:, :].rearrange("p (c k) -> p c k", k=128), [bk], [dk_])

        def grp_gen(grp, pc, d):
            G_ = GB[grp]
            AT, LX, Ms, KA, KU = G_['AT'], G_['LX'], G_['Ms'], G_['KA'], G_['KU']
            cs = [grp * 4 + i for i in range(4)]
            atk = lambda i: 'AT%d_%d' % (grp, i)
            lxk = lambda q, i: 'LX%d_%d_%d' % (grp, q, i)
            msk = lambda q, i: 'Ms%d_%d_%d' % (grp, q, i)
            for i, c in enumerate(cs):
                bank, bk = PS()
                mm(bank[:, 0:192], kbd[:, c, :], QR[:, c, :], ['kbd', 'QR'], [bk])
                mm(bank[:, 192:320], QR[:, c, 0:128], bbd[:, c, :], ['bbd', 'QR'], [bk])
                mm(bank[:, 320:512], bbd[:, c, :], QR[:, c, :], ['bbd', 'QR'], [bk])
                tt('dve', AT[:, i, 0:512], bank[:, :], maskX[d][:, :], ALU.mult, [bk, 'maskX%d' % d], [atk(i)])
                tt('pool', LX[0][:, i, 128:256], AT[:, i, 320:448], identb[:, :], ALU.add, [atk(i), 'identb'], [lxk(0, i) + 'x'])
            yield
            cur = 0
            for j in range(6):
                nxt = 1 - cur if j >= 1 else 0
                if j == 0:
                    bankA, bkA = PS()
                    for i in range(4):
                        mm(bankA[:, i * 128:(i + 1) * 128], AT[:, i, 192:320], AT[:, i, 320:448], [atk(i)], [bkA])
                    banksA = [(bankA, bkA)]
                elif j <= 3:
                    banksA = []
                    for pair in range(2):
                        bankA, bkA = PS()
                        banksA.append((bankA, bkA))
                        for jj in range(2):
                            i = pair * 2 + jj
                            mm(bankA[:, jj * 256:(jj + 1) * 256], Ms[cur][:, i, :], LX[cur][:, i, :],
                               [msk(cur, i), lxk(cur, i), lxk(cur, i) + 'x'], [bkA])
                else:
                    bankA, bkA = PS()
                    for i in range(4):
                        mm(bankA[:, i * 128:(i + 1) * 128], Ms[cur][:, i, :], LX[cur][:, i, 128:256],
                           [msk(cur, i), lxk(cur, i) + 'x'], [bkA])
                    banksA = [(bankA, bkA)]
                if j <= 4:
                    bankB, bkB = PS()
                    for i in range(4):
                        if j == 0:
                            mm(bankB[:, i * 128:(i + 1) * 128], AT[:, i, 320:448], AT[:, i, 192:320], [atk(i)], [bkB])
                        else:
                            mm(bankB[:, i * 128:(i + 1) * 128], LX[cur][:, i, 0:128], Ms[cur][:, i, :],
                               [lxk(cur, i), msk(cur, i)], [bkB])
                mnx = 0 if j == 0 else nxt
                if j <= 4:
                    cp('act', Ms[mnx][:, :, :], bankB[:, :].rearrange("p (c k) -> p c k", k=128), [bkB],
                       [msk(mnx, i) for i in range(4)])
                if j == 0:
                    cp('act', LX[0][:, :, 0:128], bankA[:, :].rearrange("p (c k) -> p c k", k=128), [bkA],
                       [lxk(0, i) for i in range(4)])
                elif j <= 3:
                    for pair in range(2):
                        bankA, bkA = banksA[pair]
                        b3 = bankA[:, :].rearrange("p (c k) -> p c k", k=256)
                        ii = [pair * 2, pair * 2 + 1]
                        cp('act', LX[nxt][:, pair * 2:pair * 2 + 2, 0:128], b3[:, :, 0:128], [bkA], [lxk(nxt, i) for i in ii])
                        tt('dve', LX[nxt][:, pair * 2:pair * 2 + 2, 128:256], b3[:, :, 128:256],
                           LX[cur][:, pair * 2:pair * 2 + 2, 128:256], ALU.add,
                           [bkA] + [lxk(cur, i) + 'x' for i in ii] + [lxk(nxt, i) for i in ii], [lxk(nxt, i) + 'x' for i in ii])
                else:
                    tt('dve', LX[nxt][:, :, 128:256], bankA[:, :].rearrange("p (c k) -> p c k", k=128),
                       LX[cur][:, :, 128:256], ALU.add, [bkA] + [lxk(cur, i) + 'x' for i in range(4)],
                       [lxk(nxt, i) + 'x' for i in range(4)])
                cur = mnx if j == 0 else nxt
                yield
            xk = lambda i: lxk(cur, i) + 'x'
            bank, bk = PS()
            for i, c in enumerate(cs):
                mm(bank[:, i * 128:(i + 1) * 128], AT[:, i, 0:128], Vtm[:, c, :], [atk(i), 'Vtm'], [bk])
            act(KA[:, :, 128:256], bank[:, :].rearrange("p (c k) -> p c k", k=128), AF.Identity, [bk], ['KAv%d' % grp], scale=-1.0)
            yield
            for pair in range(2):
                bank, bk = PS()
                for jj in range(2):
                    i = pair * 2 + jj
                    mm(bank[:, jj * 256:(jj + 1) * 256], LX[cur][:, i, 128:256], KA[:, i, :],
                       [xk(i), 'KAk%d' % grp, 'KAv%d' % grp], [bk])
                cp('act', KU[:, pair * 2:pair * 2 + 2, :], bank[:, :].rearrange("p (c k) -> p c k", k=256), [bk],
                   ['KU%d_%d' % (grp, pair)])
            kuk = lambda i: 'KU%d_%d' % (grp, i // 2)
            t0 = pc * PW + grp * 256
            yield
            for pair in range(2):
                bank, bk = PS()
                for jj in range(2):
                    i = pair * 2 + jj
                    mm(bank[:, jj * 192:(jj + 1) * 192], KU[:, i, 0:128], AT[:, i, 448:640], [kuk(i), atk(i), 'ATB%d' % grp], [bk])
                b3 = bank[:, 0:384].rearrange("p (c k) -> p c k", k=192)
                ta = t0 + pair * 128
                tt('dve', Rhat[d][:, ta:ta + 128].rearrange("p (c k) -> p c k", k=64),
                   QR[:, grp * 4 + pair * 2:grp * 4 + pair * 2 + 2, 128:192], b3[:, :, 0:64],
                   ALU.subtract, [bk, 'QR'], ['Rhat%d_%d' % (d, pc)])
                for jj in range(2):
                    i = pair * 2 + jj
                    cg = pc * 8 + cs[i]
                    st = cg if d == 0 else NCH - 1 - cg
                    stt('dve', MTS[:, st, d, :], identf[:, :], GC[:, d, cg:cg + 1], bank[:, jj * 192 + 64:(jj + 1) * 192],
                        ALU.mult, ALU.subtract, [bk, 'identf', 'GC%d' % pc], ['MTS%d' % st])
            bank, bk = PS()
            for i, c in enumerate(cs):
                mm(bank[:, i * 64:(i + 1) * 64], Vtm[:, c, :], AT[:, i, 128:192], ['Vtm', atk(i)], [bk], start=True, stop=False)
                mm(bank[:, i * 64:(i + 1) * 64], KU[:, i, 128:256], AT[:, i, 448:512], [kuk(i), atk(i)], [bk], start=False, stop=True)
            if d == 0:
                cp('act', Yacc[:, t0:t0 + 256], bank[:, 0:256], [bk], ['Yacc%d' % pc])
            else:
                tt('dve', Yacc[:, t0:t0 + 256], bank[:, 0:256], Yacc[:, t0:t0 + 256], ALU.add,
                   [bk, 'Yacc%d' % pc], ['Yacc%d' % pc])
            yield
            bank, bk = PS()
            for i, c in enumerate(cs):
                mm(bank[:, i * 128:(i + 1) * 128], Kdtm[:, c, :], Vtm[:, c, :], ['Kdtm%d' % grp, 'Vtm'], [bk], start=True, stop=False)
                mm(bank[:, i * 128:(i + 1) * 128], AT[:, i, 512:640], KU[:, i, 128:256], ['ATB%d' % grp, kuk(i)], [bk],
                   start=False, stop=True)
            for i, c in enumerate(cs):
                cg = pc * 8 + c
                st = cg if d == 0 else NCH - 1 - cg
                cp('act', NS[:, st, d, :], bank[:, i * 128:(i + 1) * 128], [bk], ['NS%d' % st])
            yield

        def bonus_f(pc):
            tsl = slice(pc * PW, (pc + 1) * PW)
            t = f32t
            tt('pool', t['nrm'][:, :], t['kd0'][:, :], t['kd1'][:, :], ALU.add, ['kd0', 'kd1'], ['nrm'])
            tt('pool', t['nrm'][:, :], t['nrm'][:, :], ld['r'][:, :], ALU.mult, ['nrm', 'ld_r'], ['nrm'])
            act(sqk[:, :], t['nrm'][:, :], AF.Identity, ['nrm', 'par'], ['sqk'], scale=pcol('rk', u))
            bank, bk = PS()
            mm(bank[:, :], bones[:, :], sqk[:, :], ['bones', 'sqk'], [bk])
            tt('dve', bonus[:, tsl], bank[:, :], ld['v'][:, :], ALU.mult, [bk, 'ld_v'], ['bonus%d' % pc])

        def interleave(gens, stagger=STAGGER):
            gens = list(gens)
            for _ in range(stagger):
                try:
                    next(gens[0])
                except StopIteration:
                    gens.pop(0)
                    break
            while gens:
                for g_ in list(gens):
                    try:
                        next(g_)
                    except StopIteration:
                        gens.remove(g_)

        def head():
            yield from piece_a(0)
            piece_b(0)
            yield
            yield from dprep(0, 0)
            dprod(0, 0)
            yield

        def body():
            for pc in range(NPC):
                interleave([grp_gen(0, pc, 0), grp_gen(1, pc, 0), dprep(pc, 1)])
                dprod(pc, 1)
                bonus_f(pc)
                nxt = []
                if pc + 1 < NPC:
                    def nxt_gen(pc=pc):
                        yield from piece_a(pc + 1)
                        yield from dprep(pc + 1, 0)
                    nxt = [nxt_gen()]
                interleave([grp_gen(0, pc, 1), grp_gen(1, pc, 1)] + nxt)
                if pc + 1 < NPC:
                    piece_b(pc + 1)
                    dprod(pc + 1, 0)


        def chain_out():
            for s in range(1, NCH):
                bank, bk = PS()
                for d in range(2):
                    mm(bank[:, d * 128:(d + 1) * 128], MTS[:, s, d, :], NS[:, s - 1, d, :], ['MTS%d' % s, 'NS%d' % (s - 1)], [bk])
                tt('dve', NS[:, s, :, :], bank[:, 0:256].rearrange("p (c k) -> p c k", k=128), NS[:, s, :, :], ALU.add,
                   [bk, 'NS%d' % s], ['NS%d' % s])
                if s % 2 == 0:
                    yield
            for d in range(2):
                for pc in range(NPC):
                    bank, bk = PS()
                    n = 0
                    for c in range(8):
                        cg = pc * 8 + c
                        st = cg if d == 0 else NCH - 1 - cg
                        if st == 0:
                            continue
                        mm(bank[:, c * 64:(c + 1) * 64], NS[:, st - 1, d, :], Rhat[d][:, cg * 64:(cg + 1) * 64],
                           ['NS%d' % (st - 1), 'Rhat%d_%d' % (d, pc)], [bk])
                    lo, hi = 0, 8
                    if d == 0 and pc == 0:
                        lo = 1
                    if d == 1 and pc == NPC - 1:
                        hi = 7
                    tsl2 = slice(pc * PW + lo * 64, pc * PW + hi * 64)
                    tt('dve', Yacc[:, tsl2], bank[:, lo * 64:hi * 64], Yacc[:, tsl2], ALU.add, [bk, 'Yacc%d' % pc], ['Yacc%d' % pc])
                    yield

        def epi():
            EY = GB[0]['KU'][:, :, :].rearrange("p a b -> p (a b)").rearrange("p (a b) -> p a b", b=PW)
            t = f32t
            for pc in range(NPC):
                tsl = slice(pc * PW, (pc + 1) * PW)
                cp('pool', EY[:, 0, :], Yacc[:, tsl], ['Yacc%d' % pc], ['KU0_0', 'KU0_1'])
                act(EY[:, 1, :], Yacc[:, tsl], AF.Square, ['Yacc%d' % pc], ['KU0_0', 'KU0_1'])
                b1, bk1 = PS()
                mm(b1[:, :], bones[:, :], EY[:, 0, :], ['bones', 'KU0_0', 'KU0_1'], [bk1])
                b2, bk2 = PS()
                mm(b2[:, :], bones[:, :], EY[:, 1, :], ['bones', 'KU0_0', 'KU0_1'], [bk2])
                act(t['S'][:, :], b1[:, :], AF.Identity, [bk1], ['S'], scale=1.0 / 64)
                tt('pool', t['sg'][:, :], t['S'][:, :], t['S'][:, :], ALU.mult, ['S'], ['sg'])
                stt('dve', t['Sd'][:, :], b2[:, :], 1.0 / 64, t['sg'][:, :], ALU.mult, ALU.subtract, [bk2, 'sg'], ['Sd'])
                act(t['a'][:, :], t['Sd'][:, :], AF.Ln, ['Sd', 'epsg'], ['a'], bias=epsg[:, 0:1])
                act(t['a'][:, :], t['a'][:, :], AF.Exp, ['a'], ['a'], scale=-0.5)
                tt('pool', t['eGx'][:, :], Yacc[:, tsl], t['S'][:, :], ALU.subtract, ['Yacc%d' % pc, 'S'], ['eGx'])
                tt('pool', t['eGx'][:, :], t['eGx'][:, :], t['a'][:, :], ALU.mult, ['eGx', 'a'], ['eGx'])
                act(t['eGx'][:, :], t['eGx'][:, :], AF.Identity, ['eGx', 'par'], ['eGx'], scale=pcol('lnw', u), bias=pcol('lnb', u))
                tt('dve', t['eGx'][:, :], t['eGx'][:, :], bonus[:, tsl], ALU.add, ['eGx', 'bonus%d' % pc], ['eGx'])
                dma(ld['lw'][:, :], pslice(g_cb, pc), [], ['ld_lw'])
                act(sgp[:, :], ld['lw'][:, :], AF.Silu, ['ld_lw'], ['sgp'])
                tt('dve', sqk[:, :], t['eGx'][:, :], sgp[:, :], ALU.mult, ['eGx', 'sgp'], ['sqk'])
                dma(mixT_d[u * 128:(u + 1) * 128, tsl], sqk[:, :], ['sqk'], [])

        return head, body, chain_out, epi

    units = [rw_unit(u) for u in range(rw_units)]
    if units:
        for _ in units[0][0]():
            pass
    for u in range(rw_units):
        head_, body_, chain_, epi_ = units[u]
        body_()
        gl = [chain_()]
        if u + 1 < rw_units:
            gl.append(units[u + 1][0]())
        run_window(gl, 2)
        epi_()
    P.barrier()
    A.reset(mB)
    if stop_after == 'B1':
        P.emit()
        return nc

    Yacc = A.alloc("hYacc", [128, T], F32)
    Qt = [A.alloc("Qt", [128, T], BF16) for _ in range(2)]
    NS = A.alloc("hNS", [128, NCH, 2, 128], BF16)
    GCS = [A.alloc("GCS", [128, NCH], F32) for _ in range(2)]
    sgate = A.alloc("hsgate", [128, T], BF16)
    HP = []
    for q_ in range(2):
        HP.append(dict(ld={nm: A.alloc("hld_" + nm, [128, PW], F32) for nm in ('q', 'f0', 'f1', 'i', 'g')},
                       vb=A.alloc("hvb", [128, PW], BF16), Vtm=A.alloc("hVtm", [64, 8, 128], BF16)))
    HS = []
    for q_ in range(4):
        HS.append(dict(t={nm: A.alloc("ht_" + nm, [128, PW], F32) for nm in ('f', 'lf', 'kf', 'G', 'Gd')},
                       eG=A.alloc("heG", [128, PW], BF16), enG=A.alloc("henG", [128, PW], BF16),
                       eGd=A.alloc("heGd", [128, PW], BF16),
                       kb=A.alloc("hkb", [128, PW], BF16), Kd=A.alloc("hKd", [128, PW], BF16),
                       Kdtm=A.alloc("hKdtm", [64, 8, 128], BF16), AT=A.alloc("hAT", [64, 8, 64], BF16)))
    r1 = A.alloc("hr1", [128, PW], F32)
    r2 = A.alloc("hr2", [128, PW], F32)
    osq = A.alloc("osq", [128, PW], BF16)
    obf = A.alloc("obf", [128, PW], BF16)

    def hg_unit(h):
        cbs = {'q': 18 + h, 'f0': 22 + h, 'f1': 26 + h, 'i': 30 + h, 'g': 34 + h}

        def piece_gen(pc):
            q_ = pc % 2
            B_ = HP[q_]
            ld = B_['ld']
            tsl = slice(pc * PW, (pc + 1) * PW)
            for nm in ('f0', 'f1', 'q', 'i', 'g'):
                dma(ld[nm][:, :], pslice(cbs[nm], pc), [], ['hld%d_%s' % (q_, nm)])
            cp('pool', B_['vb'][:, :], ld['i'][:, :], ['hld%d_i' % q_], ['hvb%d' % q_])
            act(sgate[:, tsl], ld['g'][:, :], AF.Silu, ['hld%d_g' % q_], ['hsg%d' % pc])
            yield
            for half in range(2):
                bank, bk = PS()
                for i in range(4):
                    c = half * 4 + i
                    mm(bank[0:64, i * 128:(i + 1) * 128], B_['vb'][:, c * 64:(c + 1) * 64], identb[:, :],
                       ['hvb%d' % q_, 'identb'], [bk])
                cp('dve', B_['Vtm'][:, half * 4:half * 4 + 4, :], bank[0:64, :].rearrange("p (c k) -> p c k", k=128),
                   [bk], ['hVtm%d' % q_])
            yield

        def pair_gen(pc):
            q_ = pc % 2
            B_ = HP[q_]
            ld = B_['ld']
            tsl = slice(pc * PW, (pc + 1) * PW)
            SS = [HS[q_ * 2 + d] for d in range(2)]
            KK = [(lambda nm, sid=q_ * 2 + d: 'h%s_%d' % (nm, sid)) for d in range(2)]
            for d in range(2):
                t, K = SS[d]['t'], KK[d]
                fk = 'f%d' % d
                act(t['f'][:, :], ld[fk][:, :], AF.Tanh, ['hld%d_%s' % (q_, fk)], [K('f')], scale=0.5)
            for d in range(2):
                t, K = SS[d]['t'], KK[d]
                ts('dve', t['f'][:, :], t['f'][:, :], hc1[:, h:h + 1], hc2[:, h:h + 1], ALU.mult, ALU.add, [K('f'), 'hc1', 'hc2'], [K('f')])
            for d in range(2):
                t, K = SS[d]['t'], KK[d]
                act(t['lf'][:, :], t['f'][:, :], AF.Ln, [K('f')], [K('lf')])
                ts('pool', t['kf'][:, :], t['f'][:, :], -1.0, 1.0, ALU.mult, ALU.add, [K('f')], [K('kf')])
            yield
            GT = []
            for d in range(2):
                t, K = SS[d]['t'], KK[d]
                G3 = t['G'][:, :].rearrange("p (c k) -> p c k", k=64)
                if d == 0:
                    scan(t['G'][:, :], rmask[:, :, :].rearrange("p c k -> p (c k)"), t['lf'][:, :], ['rmask', K('lf')], [K('G')])
                    Gtot = G3[:, :, 63:64]
                    Gtot2 = G3[:, :, 63]
                else:
                    scan(t['G'][:, ::-1], rmask[:, :, :].rearrange("p c k -> p (c k)"), t['lf'][:, ::-1], ['rmask', K('lf')], [K('G')])
                    Gtot = G3[:, :, 0:1]
                    Gtot2 = G3[:, :, 0]
                GT.append(Gtot2)
                tt('pool', t['Gd'][:, :].rearrange("p (c k) -> p c k", k=64), Gtot.broadcast_to([128, 8, 64]), G3,
                   ALU.subtract, [K('G')], [K('Gd')])
            yield
            for d in range(2):
                S_, t, K = SS[d], SS[d]['t'], KK[d]
                act(S_['eG'][:, :], t['G'][:, :], AF.Exp, [K('G')], [K('eG')])
                act(S_['enG'][:, :], t['G'][:, :], AF.Exp, [K('G')], [K('enG')], scale=-1.0)
                act(S_['eGd'][:, :], t['Gd'][:, :], AF.Exp, [K('Gd')], [K('eGd')])
                if d == 0:
                    act(GCS[0][:, pc * 8:(pc + 1) * 8], GT[d], AF.Exp, [K('G')], ['GCS%d_%d' % (d, s_) for s_ in range(pc * 8, pc * 8 + 8)])
                else:
                    lo_ = NCH - 8 - pc * 8
                    act(GCS[1][:, lo_:lo_ + 8][:, ::-1], GT[d], AF.Exp, [K('G')], ['GCS%d_%d' % (d, s_) for s_ in range(lo_, lo_ + 8)])
            yield
            for d in range(2):
                S_, t, K = SS[d], SS[d]['t'], KK[d]
                tt('dve', Qt[d][:, tsl], ld['q'][:, :], S_['eG'][:, :], ALU.mult, ['hld%d_q' % q_, K('eG')], ['Qt%d_%d' % (d, pc)])
                tt('pool', S_['kb'][:, :], t['kf'][:, :], S_['enG'][:, :], ALU.mult, [K('kf'), K('enG')], [K('kb')])
                tt('dve', S_['Kd'][:, :], t['kf'][:, :], S_['eGd'][:, :], ALU.mult, [K('kf'), K('eGd')], [K('Kd')])
            yield
            for d in range(2):
                S_, t, K = SS[d], SS[d]['t'], KK[d]
                for half in range(2):
                    bank, bk = PS()
                    for i in range(4):
                        c = half * 4 + i
                        mm(bank[0:64, i * 128:(i + 1) * 128], S_['Kd'][:, c * 64:(c + 1) * 64], identb[:, :], [K('Kd'), 'identb'], [bk])
                    cp('act' if half == 0 else 'dve', S_['Kdtm'][:, half * 4:half * 4 + 4, :],
                       bank[0:64, :].rearrange("p (c k) -> p c k", k=128), [bk], [K('Kdtm')])
                bank, bk = PS()
                for c in range(8):
                    mm(bank[0:64, c * 64:(c + 1) * 64], S_['kb'][:, c * 64:(c + 1) * 64],
                       Qt[d][:, pc * PW + c * 64:pc * PW + (c + 1) * 64], [K('kb'), 'Qt%d_%d' % (d, pc)], [bk])
                tt('dve', S_['AT'][:, :, :], bank[0:64, :].rearrange("p (c k) -> p c k", k=64), maskH[d][:, :, :], ALU.mult,
                   [bk, 'maskH%d' % d], [K('AT')])
            yield
            for d in range(2):
                S_, t, K = SS[d], SS[d]['t'], KK[d]
                bank, bk = PS()
                for c in range(8):
                    mm(bank[:, c * 64:(c + 1) * 64], B_['Vtm'][:, c, :], S_['AT'][:, c, :], ['hVtm%d' % q_, K('AT')], [bk])
                if d == 0:
                    cp('act', Yacc[:, tsl], bank[:, :], [bk], ['hY%d' % pc])
                else:
                    tt('dve', Yacc[:, tsl], bank[:, :], Yacc[:, tsl], ALU.add, [bk, 'hY%d' % pc], ['hY%d' % pc])
                for half in range(2):
                    bank, bk = PS()
                    for i in range(4):
                        c = half * 4 + i
                        mm(bank[:, i * 128:(i + 1) * 128], S_['Kdtm'][:, c, :], B_['Vtm'][:, c, :], [K('Kdtm'), 'hVtm%d' % q_], [bk])
                    for i in range(4):
                        c = half * 4 + i
                        cg = pc * 8 + c
                        st = cg if d == 0 else NCH - 1 - cg
                        cp('act' if half == 0 else 'dve', NS[:, st, d, :], bank[:, i * 128:(i + 1) * 128], [bk], ['hNS%d_%d' % (d, st)])
            yield

        gens = []
        for pc in range(NPC):
            def work(pc=pc):
                yield from piece_gen(pc)
                yield from pair_gen(pc)
            gens.append(work())
        run_window(gens, 2)
        for s in range(1, NCH):
            for d in range(2):
                stt('dve', NS[:, s, d, :], NS[:, s - 1, d, :], GCS[d][:, s:s + 1], NS[:, s, d, :],
                    ALU.mult, ALU.add, ['hNS%d_%d' % (d, s - 1), 'hNS%d_%d' % (d, s), 'GCS%d_%d' % (d, s)], ['hNS%d_%d' % (d, s)])
        for d in range(2):
            for pc in range(NPC):
                bank, bk = PS()
                lo, hi = 0, 8
                for c in range(8):
                    cg = pc * 8 + c
                    st = cg if d == 0 else NCH - 1 - cg
                    if st == 0:
                        if d == 0:
                            lo = 1
                        else:
                            hi = 7
                        continue
                    mm(bank[:, c * 64:(c + 1) * 64], NS[:, st - 1, d, :], Qt[d][:, cg * 64:(cg + 1) * 64],
                       ['hNS%d_%d' % (d, st - 1), 'Qt%d_%d' % (d, pc)], [bk])
                tsl2 = slice(pc * PW + lo * 64, pc * PW + hi * 64)
                tt('dve', Yacc[:, tsl2], bank[:, lo * 64:hi * 64], Yacc[:, tsl2], ALU.add, [bk, 'hY%d' % pc], ['hY%d' % pc])
        for pc in range(NPC):
            tsl = slice(pc * PW, (pc + 1) * PW)
            act(osq[:, :], Yacc[:, tsl], AF.Square, ['hY%d' % pc], ['osq'])
            bank, bk = PS()
            mm(bank[:, :], onesb[:, :], osq[:, :], ['onesb', 'osq'], [bk])
            act(r1[:, :], bank[:, :], AF.Ln, [bk, 'epsn'], ['hr1'], bias=epsn[:, 0:1], scale=1.0 / 128)
            act(r1[:, :], r1[:, :], AF.Exp, ['hr1'], ['hr1'], scale=-0.5)
            tt('dve', r2[:, :], Yacc[:, tsl], r1[:, :], ALU.mult, ['hY%d' % pc, 'hr1'], ['hr2'])
            stt('dve', obf[:, :], r2[:, :], pcol('hgn', 0), sgate[:, tsl], ALU.mult, ALU.mult,
                ['hr2', 'par', 'hsg%d' % pc], ['obf'])
            dma(mixT_d[512 + h * 128:512 + (h + 1) * 128, tsl], obf[:, :], ['obf'], [], eng='pool')

    for h in range(hg_units):
        hg_unit(h)
    P.barrier()
    A.reset(mB)
    if stop_after == 'B':
        P.emit()
        return nc

    wob = A.alloc("wob", [128, 8, D], BF16)
    gpost = A.alloc("gpost", [128, D], F32)
    wost = [A.alloc("wost", [128, D], F32) for _ in range(2)]
    dma(gpost[:, :], gpost_d[:, :], [], ['gpost'])
    wout_v = wout_d.rearrange("(kc p) n -> p kc n", p=128)
    for kc in range(8):
        dma(wost[kc % 2][:, :], wout_v[:, kc, :], [], ['wost%d' % (kc % 2)])
        cp('dve' if kc % 2 == 0 else 'act', wob[:, kc, :], wost[kc % 2][:, :], ['wost%d' % (kc % 2)], ['wob'])
    NSL = 4
    mx = [A.alloc("mx", [128, 8, 128], BF16) for _ in range(NSL)]
    xt = [A.alloc("xt", [128, D], F32) for _ in range(NSL)]
    ot = [A.alloc("ot", [128, D], F32) for _ in range(NSL)]
    junk = A.alloc("junk", [128, 512], BF16)
    ss = [A.alloc("ss", [128, 4], F32) for _ in range(NSL)]
    mix_v = mixT_d.rearrange("(kc p) t -> p kc t", p=128)
    NT = T // 128

    def c_load(tt_i):
        s = tt_i % NSL
        rows = slice(tt_i * 128, (tt_i + 1) * 128)
        dma(mx[s][:, :, :], mix_v[:, :, rows], [], ['mx%d' % s])
        dma(xt[s][:, :], x_d[rows, :], [], ['xt%d' % s])

    for i in range(min(NSL - 1, NT)):
        c_load(i)
    for tt_i in range(NT):
        s = tt_i % NSL
        rows = slice(tt_i * 128, (tt_i + 1) * 128)
        if tt_i + NSL - 1 < NT:
            c_load(tt_i + NSL - 1)
        banks = []
        for half in range(2):
            bank, bk = PS()
            banks.append((bank, bk))
            for kc in range(8):
                mm(bank[:, :], mx[s][:, kc, :], wob[:, kc, half * 512:(half + 1) * 512], ['mx%d' % s, 'wob'], [bk],
                   start=(kc == 0), stop=(kc == 7))
            act(junk[:, :], bank[:, :], AF.Square, [bk], ['junk', 'ss%d' % s], accum=ss[s][:, half:half + 1])
        act(ss[s][:, 2:3], ss[s][:, 0:1], AF.Identity, ['ss%d' % s], ['ss%d' % s], bias=ss[s][:, 1:2])
        act(ss[s][:, 3:4], ss[s][:, 2:3], AF.Sqrt, ['ss%d' % s, 'epsn'], ['ss%d' % s], bias=epsn[:, 0:1], scale=1.0 / D)
        recip(ss[s][:, 3:4], ss[s][:, 3:4], ['ss%d' % s], ['ss%d' % s])
        for half in range(2):
            bank, bk = banks[half]
            hs = slice(half * 512, (half + 1) * 512)
            stt('dve', ot[s][:, hs], bank[:, :], ss[s][:, 3:4], gpost[:, hs], ALU.mult, ALU.mult,
                [bk, 'ss%d' % s, 'gpost'], ['ot%d' % s])
        tt('dve', ot[s][:, :], ot[s][:, :], xt[s][:, :], ALU.add, ['ot%d' % s, 'xt%d' % s], ['ot%d' % s])
        dma(out_d[rows, :], ot[s][:, :], ['ot%d' % s], [], eng='pool')
    P.emit()
    return nc


def prep_inputs(inputs, b):
    f = lambda a: np.ascontiguousarray(np.asarray(a, dtype=np.float32))
    x = f(inputs['x'])
    par = np.zeros((128, NPAR), np.float32)
    cols = []
    cols.append(f(inputs['pre_norm_g'])[0].reshape(8, 128).T)
    cols.append(f(inputs['rw_shift_prev'])[0].reshape(18, 128).T)
    cols.append(f(inputs['rw_shift_next'])[0].reshape(18, 128).T)
    cols.append(f(inputs['rw_w0'])[0].reshape(8, 128).T)
    cols.append(f(inputs['rw_a0'])[0].reshape(8, 128).T)
    cols.append(f(inputs['rw_k_k'])[0].reshape(4, 128).T)
    cols.append(f(inputs['rw_k_a'])[0].reshape(4, 128).T)
    cols.append(f(inputs['rw_r_k'])[0].reshape(4, 128).T)
    cols.append(f(inputs['rw_ln_w'])[0].reshape(4, 128).T)
    cols.append(f(inputs['rw_ln_b'])[0].reshape(4, 128).T)
    lbl = f(inputs['hg_lb_logits'])
    cols.append(lbl[0].reshape(4, 128).T)
    cols.append(lbl[1].reshape(4, 128).T)
    cols.append(f(inputs['hg_norm_g'])[0].reshape(1, 128).T)
    allc = np.concatenate(cols, axis=1)
    par[:, :allc.shape[1]] = allc
    m = {
        "xT": np.ascontiguousarray(x[b].T),
        "x": np.ascontiguousarray(x[b]),
        "w_in": f(inputs['w_in'])[0],
        "w_out": f(inputs['w_out'])[0],
        "par": par,
        "w2s": f(inputs['rw_w2'])[0].reshape(128, 512),
        "a2s": f(inputs['rw_a2'])[0].reshape(128, 512),
        "gpost": np.ascontiguousarray(np.broadcast_to(f(inputs['post_norm_g'])[0][None, :], (128, D))),
    }
    return m


def kernel(**inputs):
    nc = build()
    in_maps = [prep_inputs(inputs, c % 4) for c in range(8)]
    res = run_bass_kernel_spmd(nc, in_maps, core_ids=list(range(8)))
    out = np.stack([np.asarray(res.results[c]["out"], dtype=np.float32) for c in range(4)], axis=0)
    return out
```

```python
import numpy as np
import concourse.bass as bass
import concourse.mybir as mybir
from concourse.bass_utils import run_bass_kernel_spmd

F32 = mybir.dt.float32
BF16 = mybir.dt.bfloat16
AF = mybir.ActivationFunctionType
ALU = mybir.AluOpType

T = 4096
D = 1024
NPC = 8
PW = 512
C = 64
NCH = 64
NCB = 38
INC = 4864
CDEC = 0.6065306597126334
NORM_EPS = 1e-6
GN_EPS = 64e-5
HG_LVL = 99
STAGGER = 0
TRANSITIVE = True
NPAR = 96


class Prog:
    ENG = ('pe', 'act', 'dve', 'pool', 'sp')

    def __init__(self, nc, n_dma_sems=10):
        self.nc = nc
        self.ops = {e: [] for e in self.ENG}
        self.sems = {e: nc.alloc_semaphore(name='s_' + e) for e in self.ENG}
        self.dma_sems = [nc.alloc_semaphore(name='d%d' % i) for i in range(n_dma_sems)]
        self.dma_val = [0] * n_dma_sems
        self.dma_rr = 0
        self.count = {e: 0 for e in self.ENG}
        self.seen = {e: {} for e in self.ENG}
        self.pending = {e: [] for e in self.ENG}
        self.lastw = {}
        self.readers = {}
        self.clock = {}

    def _need(self, eng, ev, waits):
        if ev is None:
            return
        sid, val, src = ev
        if src == eng and eng == 'pe':
            return
        if self.seen[eng].get(sid, 0) >= val:
            return
        self.seen[eng][sid] = val
        waits.append((sid, val))
        if TRANSITIVE:
            for k2, v2 in self.clock.get((sid, val), {}).items():
                if self.seen[eng].get(k2, 0) < v2:
                    self.seen[eng][k2] = v2

    def op(self, eng, fn, reads=(), writes=(), dma=False):
        waits = []
        for ev in self.pending[eng]:
            self._need(eng, ev, waits)
        self.pending[eng] = []
        for k in reads:
            self._need(eng, self.lastw.get(k), waits)
        for k in writes:
            self._need(eng, self.lastw.get(k), waits)
            for ev in self.readers.get(k, ()):
                self._need(eng, ev, waits)
        if dma:
            di = self.dma_rr
            self.dma_rr = (self.dma_rr + 1) % len(self.dma_sems)
            if self.dma_val[di] > 0:
                self._need(eng, (('d', di), self.dma_val[di], 'dma'), waits)
            self.dma_val[di] += 16
            ev = (('d', di), self.dma_val[di], 'dma')
            inc = (self.dma_sems[di], 16)
        else:
            self.count[eng] += 1
            ev = (('e', eng), self.count[eng], eng)
            inc = (self.sems[eng], 1)
        if TRANSITIVE:
            self.clock[(ev[0], ev[1])] = dict(self.seen[eng])
        for k in writes:
            self.lastw[k] = ev
            self.readers[k] = []
        for k in reads:
            if k not in writes:
                self.readers.setdefault(k, []).append(ev)
        self.ops[eng].append((fn, waits, inc))
        return ev

    def all_events(self):
        evs = [(('e', e), self.count[e], e) for e in self.ENG if self.count[e] > 0]
        evs += [(('d', i), v, 'dma') for i, v in enumerate(self.dma_val) if v > 0]
        return evs

    def barrier(self):
        evs = self.all_events()
        for e in self.ENG:
            self.pending[e] = list(evs)

    def _semh(self, sid):
        return self.dma_sems[sid[1]] if sid[0] == 'd' else self.sems[sid[1]]

    def emit(self):
        nc = self.nc
        final_events = self.all_events()
        with nc.Block() as block:
            def run(engname, engobj):
                for fn, waits, inc in self.ops[engname]:
                    for sid, val in waits:
                        engobj.wait_ge(self._semh(sid), val)
                    fn(engobj).then_inc(inc[0], inc[1])
                if engname == 'sp':
                    for sid, val, _ in final_events:
                        engobj.wait_ge(self._semh(sid), val)

            @block.tensor
            def _(e): run('pe', e)

            @block.scalar
            def _(e): run('act', e)

            @block.vector
            def _(e): run('dve', e)

            @block.gpsimd
            def _(e): run('pool', e)

            @block.sync
            def _(e): run('sp', e)


class Arena:
    def __init__(self, nc, cap):
        self.nc = nc
        self.cur = 16512
        self.cap = cap
        self.n = 0

    def alloc(self, name, shape, dtype):
        sz = 1
        for s in shape[1:]:
            sz *= s
        sz *= 2 if dtype == BF16 else 4
        sz = (sz + 63) // 64 * 64
        off = self.cur
        self.cur += sz
        assert self.cur <= self.cap, (name, self.cur, self.cap)
        self.n += 1
        return self.nc.alloc_sbuf_tensor_at("%s_%d" % (name, self.n), list(shape), dtype, offset=off)

    def mark(self):
        return self.cur

    def reset(self, m):
        print('arena peak', self.cur, 'of', self.cap)
        self.cur = m


def build(stop_after=None, rw_units=4, hg_units=4):
    nc = bass.Bass("TRN2", target_bir_lowering=False)
    xT_d = nc.dram_tensor("xT", [D, T], F32, kind="ExternalInput").ap()
    x_d = nc.dram_tensor("x", [T, D], F32, kind="ExternalInput").ap()
    win_d = nc.dram_tensor("w_in", [D, INC], F32, kind="ExternalInput").ap()
    wout_d = nc.dram_tensor("w_out", [D, D], F32, kind="ExternalInput").ap()
    par_d = nc.dram_tensor("par", [128, NPAR], F32, kind="ExternalInput").ap()
    w2s_d = nc.dram_tensor("w2s", [128, 512], F32, kind="ExternalInput").ap()
    a2s_d = nc.dram_tensor("a2s", [128, 512], F32, kind="ExternalInput").ap()
    gpost_d = nc.dram_tensor("gpost", [128, D], F32, kind="ExternalInput").ap()
    out_d = nc.dram_tensor("out", [T, D], F32, kind="ExternalOutput").ap()
    dbg = stop_after is not None
    pT_d = nc.dram_tensor("pT", [INC, T], F32, kind="ExternalOutput" if dbg else "Internal").ap()
    mixT_d = nc.dram_tensor("mixT", [D, T], BF16, kind="ExternalOutput" if dbg else "Internal").ap()

    P = Prog(nc)
    A = Arena(nc, 229344)
    psb = [nc.alloc_psum_tensor("psb%d" % i, [128, 512], F32) for i in range(8)]
    ps_rr = [0]

    def PS():
        b = ps_rr[0]
        ps_rr[0] = (b + 1) % 8
        return psb[b], "ps%d" % b

    def tt(eng, out, in0, in1, op, r, w):
        P.op(eng, lambda e: e.tensor_tensor(out=out, in0=in0, in1=in1, op=op), reads=r, writes=w)

    def ts(eng, out, in0, s1, s2, op0, op1, r, w):
        if s2 is None:
            P.op(eng, lambda e: e.tensor_scalar(out=out, in0=in0, scalar1=s1, scalar2=None, op0=op0), reads=r, writes=w)
        else:
            P.op(eng, lambda e: e.tensor_scalar(out=out, in0=in0, scalar1=s1, scalar2=s2, op0=op0, op1=op1), reads=r, writes=w)

    def stt(eng, out, in0, scalar, in1, op0, op1, r, w):
        P.op(eng, lambda e: e.scalar_tensor_tensor(out=out, in0=in0, scalar=scalar, in1=in1, op0=op0, op1=op1),
             reads=r, writes=w)

    def act(out, in_, func, r, w, bias=None, scale=1.0, accum=None):
        kw = {}
        if bias is not None:
            kw['bias'] = bias
        if accum is not None:
            kw['accum_out'] = accum
        P.op('act', lambda e: e.activation(out=out, in_=in_, func=func, scale=scale, **kw), reads=r, writes=w)

    def mm(out, lhsT, rhs, r, w, start=True, stop=True):
        P.op('pe', lambda e: e.matmul(out, lhsT=lhsT, rhs=rhs, start=start, stop=stop), reads=r, writes=w)

    def cp(eng, out, in_, r, w):
        if eng == 'act':
            act(out, in_, AF.Identity, r, w)
        else:
            P.op(eng, lambda e: e.tensor_copy(out=out, in_=in_), reads=r, writes=w)

    def dma(out, in_, r, w, eng='sp'):
        return P.op(eng, lambda e: e.dma_start(out=out, in_=in_), reads=r, writes=w, dma=True)

    def mset(eng, ap, val, w):
        P.op(eng, lambda e: e.memset(ap, val), writes=w)

    def recip(out, in_, r, w):
        P.op('dve', lambda e: e.reciprocal(out=out, in_=in_), reads=r, writes=w)

    def scan(out, d0, d1, r, w):
        P.op('dve', lambda e: e.tensor_tensor_scan(out=out, data0=d0, data1=d1, initial=0.0, op0=ALU.mult, op1=ALU.add),
             reads=r, writes=w)

    def asel(out, in_, pattern, cm, cmpop, r, w, base=0):
        P.op('pool', lambda e: e.affine_select(out=out, in_=in_, pattern=pattern, base=base, channel_multiplier=cm,
                                              compare_op=cmpop, fill=0.0), reads=r, writes=w)

    par = A.alloc("par", [128, NPAR], F32)
    dma(par[:, :], par_d[:, :], [], ['par'])
    PO = {}
    o = 0
    for nm, n in (('gpre', 8), ('mup', 18), ('mun', 18), ('w0', 8), ('a0', 8), ('kk', 4), ('ka', 4), ('rk', 4),
                  ('lnw', 4), ('lnb', 4), ('lb0', 4), ('lb1', 4), ('hgn', 1)):
        PO[nm] = o
        o += n
    assert o <= NPAR

    def pcol(nm, i=0):
        return par[:, PO[nm] + i:PO[nm] + i + 1]

    onesf = A.alloc("onesf", [128, 128], F32)
    negf = A.alloc("negf", [128, 128], F32)
    identf = A.alloc("identf", [128, 128], F32)
    identb = A.alloc("identb", [128, 128], BF16)
    onesb = A.alloc("onesb", [128, 128], BF16)
    bones = A.alloc("bones", [128, 128], BF16)
    epsn = A.alloc("epsn", [128, 1], F32)
    epsg = A.alloc("epsg", [128, 1], F32)
    c0 = A.alloc("c0", [128, 18], F32)
    omka = A.alloc("omka", [128, 4], F32)
    lb = A.alloc("lb", [128, 4], F32)
    oml = A.alloc("oml", [128, 4], F32)
    rmask = A.alloc("rmask", [128, 8, 64], F32)
    mset('pool', onesf[:, :], 1.0, ['onesf'])
    mset('pool', negf[:, :], -1.0, ['negf'])
    mset('pool', onesb[:, :], 1.0, ['onesb'])
    mset('pool', bones[:, :], 0.0, ['bones'])
    mset('pool', bones[0:64, 0:64], 1.0, ['bones'])
    mset('pool', bones[64:128, 64:128], 1.0, ['bones'])
    mset('pool', epsn[:, :], NORM_EPS, ['epsn'])
    mset('pool', epsg[:, :], GN_EPS, ['epsg'])
    mset('pool', rmask[:, :, :], 1.0, ['rmask'])
    mset('pool', rmask[:, :, 0:1], 0.0, ['rmask'])
    asel(identf[:, :], onesf[:, :], [[1, 128]], -1, ALU.is_equal, ['onesf'], ['identf'])
    cp('pool', identb[:, :], identf[:, :], ['identf'], ['identb'])
    tt('pool', c0[:, :], par[:, PO['mup']:PO['mup'] + 18], par[:, PO['mun']:PO['mun'] + 18], ALU.add, ['par'], ['c0'])
    ts('pool', c0[:, :], c0[:, :], -1.0, 1.0, ALU.mult, ALU.add, ['c0'], ['c0'])
    ts('pool', omka[:, :], par[:, PO['ka']:PO['ka'] + 4], -1.0, 1.0, ALU.mult, ALU.add, ['par'], ['omka'])
    w0h = A.alloc("w0h", [128, 8], F32)
    a0h = A.alloc("a0h", [128, 8], F32)
    halfc = A.alloc("halfc", [128, 1], F32)
    ts('pool', w0h[:, :], par[:, PO['w0']:PO['w0'] + 8], 0.5, None, ALU.mult, None, ['par'], ['w0h'])
    ts('pool', a0h[:, :], par[:, PO['a0']:PO['a0'] + 8], 0.5, None, ALU.mult, None, ['par'], ['a0h'])
    mset('pool', halfc[:, :], 0.5, ['halfc'])
    tt('pool', lb[:, :], par[:, PO['lb0']:PO['lb0'] + 4], par[:, PO['lb1']:PO['lb1'] + 4], ALU.subtract, ['par'], ['lb'])
    act(lb[:, :], lb[:, :], AF.Sigmoid, ['lb'], ['lb'])
    ts('pool', oml[:, :], lb[:, :], -1.0, 1.0, ALU.mult, ALU.add, ['lb'], ['oml'])

    maskX = [A.alloc("maskX%d" % d, [128, 512], BF16) for d in range(2)]
    maskH = [A.alloc("maskH%d" % d, [64, 8, 64], F32) for d in range(2)]
    signs = A.alloc("signs", [128, 512], F32)
    for d in range(2):
        mset('pool', maskX[d][:, :], 0.0, ['maskX%d' % d])
        if d == 0:
            pat, cm = [[1, 64]], -1
        else:
            pat, cm = [[-1, 64]], 1
        patT, cmT = ([[-1, 64]], 1) if d == 0 else ([[1, 64]], -1)
        for h in range(2):
            pr = slice(64 * h, 64 * h + 64)
            asel(maskX[d][pr, 64 * h:64 * h + 64], onesf[pr, 0:64], pat, cm, ALU.is_gt, ['onesf'], ['maskX%d' % d])
            asel(maskX[d][pr, 128:192], onesf[pr, 0:64], pat, cm, ALU.is_ge, ['onesf'], ['maskX%d' % d])
            asel(maskX[d][pr, 192 + 64 * h:192 + 64 * h + 64], negf[pr, 0:64], patT, cmT, ALU.is_gt, ['negf'], ['maskX%d' % d])
            asel(maskX[d][pr, 320 + 64 * h:320 + 64 * h + 64], negf[pr, 0:64], pat, cm, ALU.is_gt, ['negf'], ['maskX%d' % d])
            asel(maskX[d][pr, 448:512], onesf[pr, 0:64], pat, cm, ALU.is_ge, ['onesf'], ['maskX%d' % d])
            if h == 0:
                for q in range(8):
                    asel(maskH[d][0:64, q, :], onesf[0:64, 0:64], pat, cm, ALU.is_ge, ['onesf'], ['maskH%d' % d])
    mset('pool', signs[:, :], 1.0, ['signs'])
    mset('pool', signs[:, 128:256], -1.0, ['signs'])
    mset('pool', signs[:, 384:512], -1.0, ['signs'])

    w2b = A.alloc("w2b", [128, 512], BF16)
    a2b = A.alloc("a2b", [128, 512], BF16)
    mconst = A.mark()
    stg = [A.alloc("stg", [128, 512], F32) for _ in range(2)]
    dma(stg[0][:, :], w2s_d[:, :], [], ['stg0'])
    cp('dve', w2b[:, :], stg[0][:, :], ['stg0'], ['w2b'])
    dma(stg[1][:, :], a2s_d[:, :], [], ['stg1'])
    cp('dve', a2b[:, :], stg[1][:, :], ['stg1'], ['a2b'])

    xTb = A.alloc("xTb", [128, 8, T], BF16)
    rstd = A.alloc("rstd", [128, PW], F32)
    xst = [A.alloc("xst", [128, 8, PW], F32) for _ in range(2)]
    sq = [A.alloc("sq", [128, PW], BF16) for _ in range(2)]
    tmpa = A.alloc("tmpa", [128, PW], F32)
    i_x = 0
    for pc in range(NPC):
        tsl = slice(pc * PW, (pc + 1) * PW)
        s = pc % 2
        bank, bk = PS()
        for kc in range(8):
            s2 = i_x % 2
            i_x += 1
            dma(xst[s][:, kc, :], xT_d[kc * 128:(kc + 1) * 128, tsl], [], ['xst%d_%d' % (s, kc)])
            act(sq[s2][:, :], xst[s][:, kc, :], AF.Square, ['xst%d_%d' % (s, kc)], ['sq%d' % s2])
            mm(bank[:, :], onesb[:, :], sq[s2][:, :], ['onesb', 'sq%d' % s2], [bk], start=(kc == 0), stop=(kc == 7))
        act(tmpa[:, :], bank[:, :], AF.Ln, [bk, 'epsn'], ['tmpa'], bias=epsn[:, 0:1], scale=1.0 / D)
        act(rstd[:, :], tmpa[:, :], AF.Exp, ['tmpa'], ['rstd'], scale=-0.5)
        for kc in range(8):
            stt('dve', xTb[:, kc, tsl], xst[s][:, kc, :], pcol('gpre', kc), rstd[:, :], ALU.mult, ALU.mult,
                ['xst%d_%d' % (s, kc), 'par', 'rstd'], ['xTb%d' % pc])

    wst = [A.alloc("wst", [128, 8, 128], F32) for _ in range(2)]
    wb = [A.alloc("wb", [128, 8, 128], BF16) for _ in range(2)]
    pcb = [A.alloc("pcb", [128, T + 2], F32) for _ in range(3)]
    pout = [A.alloc("pout", [128, PW], F32) for _ in range(6)]
    for s in range(3):
        mset('pool', pcb[s][:, 0:1], 0.0, ['pcbh%d' % s])
        mset('pool', pcb[s][:, T + 1:T + 2], 0.0, ['pcbh%d' % s])
    win_v = win_d.rearrange("(kc p) n -> p kc n", p=128)

    def load_w(cb):
        s = cb % 2
        dma(wst[s][:, :, :], win_v[:, :, cb * 128:(cb + 1) * 128], [], ['wst%d' % s])
        cp('act', wb[s][:, :, :], wst[s][:, :, :], ['wst%d' % s], ['wb%d' % s])

    i_po = 0
    load_w(0)
    for cb in range(NCB):
        s = cb % 2
        sp_ = cb % 3
        if cb + 1 < NCB:
            load_w(cb + 1)
        for pc in range(NPC):
            tsl = slice(pc * PW, (pc + 1) * PW)
            bank, bk = PS()
            for kc in range(8):
                mm(bank[:, :], wb[s][:, kc, :], xTb[:, kc, tsl], ['wb%d' % s, 'xTb%d' % pc], [bk],
                   start=(kc == 0), stop=(kc == 7))
            cp('act', pcb[sp_][:, 1 + pc * PW:1 + (pc + 1) * PW], bank[:, :], [bk], ['pcb%d_%d' % (sp_, pc)])
        for pc in range(NPC):
            rk = ['pcb%d_%d' % (sp_, q) for q in (pc - 1, pc, pc + 1) if 0 <= q < NPC] + ['pcbh%d' % sp_]
            if cb < 18:
                po = pout[i_po % 6]
                pk = 'pout%d' % (i_po % 6)
                i_po += 1
                b0 = 1 + pc * PW
                act(po[:, :], pcb[sp_][:, b0:b0 + PW], AF.Identity, rk + ['c0'], [pk], scale=c0[:, cb:cb + 1])
                stt('dve', po[:, :], pcb[sp_][:, b0 - 1:b0 - 1 + PW], pcol('mup', cb), po[:, :], ALU.mult, ALU.add,
                    rk + ['par', pk], [pk])
                stt('dve', po[:, :], pcb[sp_][:, b0 + 1:b0 + 1 + PW], pcol('mun', cb), po[:, :], ALU.mult, ALU.add,
                    rk + ['par', pk], [pk])
                dma(pT_d[cb * 128:(cb + 1) * 128, pc * PW:(pc + 1) * PW], po[:, :], [pk], [], eng='pool')
            else:
                dma(pT_d[cb * 128:(cb + 1) * 128, pc * PW:(pc + 1) * PW], pcb[sp_][:, 1 + pc * PW:1 + (pc + 1) * PW],
                    rk, [], eng='pool')
    P.barrier()
    A.reset(mconst)
    if stop_after == 'A':
        P.emit()
        return nc

    def pslice(cb, pc):
        return pT_d[cb * 128:(cb + 1) * 128, pc * PW:(pc + 1) * PW]

    mB = A.mark()
    Yacc = A.alloc("Yacc", [128, T], F32)
    Rhat = [A.alloc("Rhat", [128, T], BF16) for _ in range(2)]
    MTS = A.alloc("MTS", [128, NCH, 2, 128], BF16)
    NS = A.alloc("NS", [128, NCH, 2, 128], BF16)
    bonus = A.alloc("bonus", [128, T], BF16)
    sgp = A.alloc("sgp", [128, PW], BF16)
    GC = A.alloc("GC", [128, 2, NCH], F32)
    ld = {nm: A.alloc("ld_" + nm, [128, PW], F32) for nm in ('r', 'k', 'v', 'lw', 'la')}
    th = A.alloc("th", [128, PW], BF16)
    lab = A.alloc("lab", [128, PW], BF16)
    f32t = {nm: A.alloc("t_" + nm, [128, PW], F32) for nm in
            ('sg', 'a', 'S', 'Sd', 'eGx', 'kkm', 'nrm', 'kap', 'kd0', 'kd1')}
    f32t['Sx'] = f32t['sg']
    f32t['t1'] = f32t['nrm']
    f32t['b'] = f32t['kkm']
    for nm in ('eG', 'enG', 'eGd', 'eGxb'):
        f32t[nm] = A.alloc("t_" + nm, [128, PW], BF16)
    sqk = A.alloc("sqk", [128, PW], BF16)
    QR = A.alloc("QR", [128, 8, 192], BF16)
    bbd = A.alloc("bbd", [128, 8, 128], BF16)
    kbd = A.alloc("kbd", [128, 8, 128], BF16)
    Kdbd = A.alloc("Kdbd", [128, 8, 128], BF16)
    Bdbd = A.alloc("Bdbd", [128, 8, 128], BF16)
    vbd = A.alloc("vbd", [128, 8, 128], BF16)
    Vtm = A.alloc("Vtm", [128, 8, 128], BF16)
    Kdtm = A.alloc("Kdtm", [128, 8, 128], BF16)
    GB = []
    for g_ in range(2):
        GB.append(dict(AT=A.alloc("AT", [128, 4, 640], BF16),
                       LX=[A.alloc("LX", [128, 4, 256], BF16) for _ in range(2)],
                       Ms=[A.alloc("Ms", [128, 4, 128], BF16) for _ in range(2)],
                       KA=A.alloc("KA", [128, 4, 256], BF16),
                       KU=A.alloc("KU", [128, 4, 256], BF16)))
    for tns, nm in ((QR, 'QR'), (bbd, 'bbd'), (kbd, 'kbd'), (Kdbd, 'Kdbd'), (Bdbd, 'Bdbd'), (vbd, 'vbd')):
        mset('pool', tns[:, :, :], 0.0, [nm])

    def run_window(gens, width):
        pend = list(gens)
        active = []
        while pend or active:
            while pend and len(active) < width:
                active.append(pend.pop(0))
            for g_ in list(active):
                try:
                    next(g_)
                except StopIteration:
                    active.remove(g_)

    def rw_unit(u):
        r_cb, k_cb, v_cb, g_cb = u, 4 + u, 8 + u, 12 + u
        usl = slice(u * 128, (u + 1) * 128)
        t = f32t
        v3 = lambda ap: ap.rearrange("p (c k) -> p c k", k=64)

        def piece_a(pc):
            tsl = slice(pc * PW, (pc + 1) * PW)
            for nm, cb in (('lw', 16), ('la', 17), ('k', k_cb), ('r', r_cb), ('v', v_cb)):
                dma(ld[nm][:, :], pslice(cb, pc), [], ['ld_' + nm])
            act(th[:, :], ld['lw'][:, :], AF.Tanh, ['ld_lw'], ['th'])
            cp('pool', lab[:, :], ld['la'][:, :], ['ld_la'], ['lab'])
            t = f32t
            act(t['kkm'][:, :], ld['k'][:, :], AF.Identity, ['ld_k', 'par'], ['kkm'], scale=pcol('kk', u))
            act(sqk[:, :], t['kkm'][:, :], AF.Square, ['kkm'], ['sqk'])
            bank, bk = PS()
            mm(bank[:, :], bones[:, :], sqk[:, :], ['bones', 'sqk'], [bk])
            act(t['nrm'][:, :], bank[:, :], AF.Sqrt, [bk], ['nrm'])
            ts('dve', t['nrm'][:, :], t['nrm'][:, :], 1e-12, None, ALU.max, None, ['nrm'], ['nrm'])
            recip(t['nrm'][:, :], t['nrm'][:, :], ['nrm'], ['nrm'])
            tt('pool', t['kap'][:, :], t['kkm'][:, :], t['nrm'][:, :], ALU.mult, ['kkm', 'nrm'], ['kap'])
            yield

        def piece_b(pc):
            for h in range(2):
                pr = slice(64 * h, 64 * h + 64)
                cp('act' if h == 0 else 'dve', vbd[pr, :, 64 * h:64 * h + 64], ld['v'][pr, :].rearrange("p (c k) -> p c k", k=64),
                   ['ld_v'], ['vbd'])
            for half in range(2):
                bank, bk = PS()
                for i in range(4):
                    c = half * 4 + i
                    mm(bank[:, i * 128:(i + 1) * 128], vbd[:, c, :], identb[:, :], ['vbd', 'identb'], [bk])
                cp('act', Vtm[:, half * 4:half * 4 + 4, :], bank[:, :].rearrange("p (c k) -> p c k", k=128), [bk], ['Vtm'])

        def dprep(pc, d):
            dsl = slice(64 * d, 64 * d + 64)
            bw, bkw = PS()
            mm(bw[:, :], w2b[dsl, usl], th[dsl, :], ['w2b', 'th'], [bkw])
            ba, bka = PS()
            mm(ba[:, :], a2b[dsl, usl], lab[dsl, :], ['a2b', 'lab'], [bka])
            act(t['sg'][:, :], bw[:, :], AF.Tanh, [bkw, 'w0h'], ['sg'], scale=0.5, bias=w0h[:, d * 4 + u:d * 4 + u + 1])
            act(t['a'][:, :], ba[:, :], AF.Tanh, [bka, 'a0h'], ['a'], scale=0.5, bias=a0h[:, d * 4 + u:d * 4 + u + 1])
            act(t['sg'][:, :], t['sg'][:, :], AF.Identity, ['sg', 'halfc'], ['sg'], scale=0.5, bias=halfc[:, 0:1])
            act(t['a'][:, :], t['a'][:, :], AF.Identity, ['a', 'halfc'], ['a'], scale=0.5, bias=halfc[:, 0:1])
            yield
            S3 = t['S'][:, :].rearrange("p (c k) -> p c k", k=64)
            if d == 0:
                scan(t['S'][:, :], rmask[:, :, :].rearrange("p c k -> p (c k)"), t['sg'][:, :], ['rmask', 'sg'], ['S'])
                Stot = S3[:, :, 63:64]
                Stot2 = S3[:, :, 63]
            else:
                scan(t['S'][:, ::-1], rmask[:, :, :].rearrange("p c k -> p (c k)"), t['sg'][:, ::-1], ['rmask', 'sg'], ['S'])
                Stot = S3[:, :, 0:1]
                Stot2 = S3[:, :, 0]
            tt('pool', t['Sx'][:, :], t['S'][:, :], t['sg'][:, :], ALU.subtract, ['S', 'sg'], ['sg'])
            yield
            act(t['eG'][:, :], t['S'][:, :], AF.Exp, ['S'], ['eG'], scale=-CDEC)
            act(t['eGxb'][:, :], t['Sx'][:, :], AF.Exp, ['sg'], ['eGxb'], scale=-CDEC)
            act(t['enG'][:, :], t['S'][:, :], AF.Exp, ['S'], ['enG'], scale=CDEC)
            act(GC[:, d, pc * 8:(pc + 1) * 8], Stot2, AF.Exp, ['S'], ['GC%d' % pc], scale=-CDEC)
            tt('pool', t['eGd'][:, :].rearrange("p (c k) -> p c k", k=64), t['enG'][:, :].rearrange("p (c k) -> p c k", k=64),
               GC[:, d, pc * 8:(pc + 1) * 8].unsqueeze(2).broadcast_to([128, 8, 64]), ALU.mult, ['enG', 'GC%d' % pc], ['eGd'])
            yield
            kd = t['kd%d' % d]
            kdk = 'kd%d' % d
            act(t['t1'][:, :], t['a'][:, :], AF.Identity, ['a', 'par', 'omka'], ['nrm'], scale=pcol('ka', u),
                bias=omka[:, u:u + 1])
            tt('pool', kd[:, :], t['nrm'][:, :], ld['k'][:, :], ALU.mult, ['nrm', 'ld_k'], [kdk])
            tt('pool', t['kkm'][:, :], t['kap'][:, :], t['a'][:, :], ALU.mult, ['kap', 'a'], ['kkm'])
            yield

        def dprod(pc, d):
            kd = t['kd%d' % d]
            kdk = 'kd%d' % d
            v3 = lambda ap: ap.rearrange("p (c k) -> p c k", k=64)
            for h in range(2):
                pr = slice(64 * h, 64 * h + 64)
                fs = slice(64 * h, 64 * h + 64)
                tt('dve', Kdbd[pr, :, fs], v3(kd[pr, :]), v3(t['eGd'][pr, :]), ALU.mult, [kdk, 'eGd'], ['Kdbd'])
                tt('dve', Bdbd[pr, :, fs], v3(t['kkm'][pr, :]), v3(t['eGd'][pr, :]), ALU.mult, ['kkm', 'eGd'], ['Bdbd'])
            tt('dve', QR[:, :, 128:192], v3(ld['r'][:, :]), v3(t['eG'][:, :]), ALU.mult, ['ld_r', 'eG'], ['QR'])
            for h in range(2):
                pr = slice(64 * h, 64 * h + 64)
                fs = slice(64 * h, 64 * h + 64)
                e1 = 'pool' if h == 0 else 'dve'
                tt(e1, QR[pr, :, fs], v3(t['kap'][pr, :]), v3(t['eGxb'][pr, :]), ALU.mult, ['kap', 'eGxb'], ['QR'])
                tt(e1, bbd[pr, :, fs], v3(t['kkm'][pr, :]), v3(t['enG'][pr, :]), ALU.mult, ['kkm', 'enG'], ['bbd'])
                tt(e1, kbd[pr, :, fs], v3(kd[pr, :]), v3(t['enG'][pr, :]), ALU.mult, [kdk, 'enG'], ['kbd'])
            for half in range(2):
                G_ = GB[half]
                for src, sk, dst, dk_, eng_ in ((QR, 'QR', G_['KA'][:, :, 0:128], 'KAk%d' % half, 'act'),
                                              (Kdbd, 'Kdbd', Kdtm[:, half * 4:half * 4 + 4, :], 'Kdtm%d' % half, 'dve'),
                                              (Bdbd, 'Bdbd', G_['AT'][:, :, 512:640], 'ATB%d' % half, 'act')):
                    bank, bk = PS()
                    for i in range(4):
                        c = half * 4 + i
                        mm(bank[:, i * 128:(i + 1) * 128], src[:, c, 0:128], identb[:, :], [sk, 'identb'], [bk])
                    cp(eng_, dst, bank[:, :].rearrange("p (c k) -> p c k", k=128), [bk], [dk_])

        def grp_gen(grp, pc, d):
            G_ = GB[grp]
            AT, LX, Ms, KA, KU = G_['AT'], G_['LX'], G_['Ms'], G_['KA'], G_['KU']
            cs = [grp * 4 + i for i in range(4)]
            atk = lambda i: 'AT%d_%d' % (grp, i)
            lxk = lambda q, i: 'LX%d_%d_%d' % (grp, q, i)
            msk = lambda q, i: 'Ms%d_%d_%d' % (grp, q, i)
            for i, c in enumerate(cs):
                bank, bk = PS()
                mm(bank[:, 0:192], kbd[:, c, :], QR[:, c, :], ['kbd', 'QR'], [bk])
                mm(bank[:, 192:320], QR[:, c, 0:128], bbd[:, c, :], ['bbd', 'QR'], [bk])
                mm(bank[:, 320:512], bbd[:, c, :], QR[:, c, :], ['bbd', 'QR'], [bk])
                tt('dve', AT[:, i, 0:512], bank[:, :], maskX[d][:, :], ALU.mult, [bk, 'maskX%d' % d], [atk(i)])
                tt('pool', LX[0][:, i, 128:256], AT[:, i, 320:448], identb[:, :], ALU.add, [atk(i), 'identb'], [lxk(0, i) + 'x'])
            yield
            cur = 0
            for j in range(6):
                nxt = 1 - cur if j >= 1 else 0
                if j == 0:
                    bankA, bkA = PS()
                    for i in range(4):
                        mm(bankA[:, i * 128:(i + 1) * 128], AT[:, i, 192:320], AT[:, i, 320:448], [atk(i)], [bkA])
                    banksA = [(bankA, bkA)]
                elif j <= 3:
                    banksA = []
                    for pair in range(2):
                        bankA, bkA = PS()
                        banksA.append((bankA, bkA))
                        for jj in range(2):
                            i = pair * 2 + jj
                            mm(bankA[:, jj * 256:(jj + 1) * 256], Ms[cur][:, i, :], LX[cur][:, i, :],
                               [msk(cur, i), lxk(cur, i), lxk(cur, i) + 'x'], [bkA])
                else:
                    bankA, bkA = PS()
                    for i in range(4):
                        mm(bankA[:, i * 128:(i + 1) * 128], Ms[cur][:, i, :], LX[cur][:, i, 128:256],
                           [msk(cur, i), lxk(cur, i) + 'x'], [bkA])
                    banksA = [(bankA, bkA)]
                if j <= 4:
                    bankB, bkB = PS()
                    for i in range(4):
                        if j == 0:
                            mm(bankB[:, i * 128:(i + 1) * 128], AT[:, i, 320:448], AT[:, i, 192:320], [atk(i)], [bkB])
                        else:
                            mm(bankB[:, i * 128:(i + 1) * 128], LX[cur][:, i, 0:128], Ms[cur][:, i, :],
                               [lxk(cur, i), msk(cur, i)], [bkB])
                mnx = 0 if j == 0 else nxt
                if j <= 4:
                    cp('act', Ms[mnx][:, :, :], bankB[:, :].rearrange("p (c k) -> p c k", k=128), [bkB],
                       [msk(mnx, i) for i in range(4)])
                if j == 0:
                    cp('act', LX[0][:, :, 0:128], bankA[:, :].rearrange("p (c k) -> p c k", k=128), [bkA],
                       [lxk(0, i) for i in range(4)])
                elif j <= 3:
                    for pair in range(2):
                        bankA, bkA = banksA[pair]
                        b3 = bankA[:, :].rearrange("p (c k) -> p c k", k=256)
                        ii = [pair * 2, pair * 2 + 1]
                        cp('act', LX[nxt][:, pair * 2:pair * 2 + 2, 0:128], b3[:, :, 0:128], [bkA], [lxk(nxt, i) for i in ii])
                        tt('dve', LX[nxt][:, pair * 2:pair * 2 + 2, 128:256], b3[:, :, 128:256],
                           LX[cur][:, pair * 2:pair * 2 + 2, 128:256], ALU.add,
                           [bkA] + [lxk(cur, i) + 'x' for i in ii] + [lxk(nxt, i) for i in ii], [lxk(nxt, i) + 'x' for i in ii])
                else:
                    tt('dve', LX[nxt][:, :, 128:256], bankA[:, :].rearrange("p (c k) -> p c k", k=128),
                       LX[cur][:, :, 128:256], ALU.add, [bkA] + [lxk(cur, i) + 'x' for i in range(4)],
                       [lxk(nxt, i) + 'x' for i in range(4)])
                cur = mnx if j == 0 else nxt
                yield
            xk = lambda i: lxk(cur, i) + 'x'
            bank, bk = PS()
            for i, c in enumerate(cs):
                mm(bank[:, i * 128:(i + 1) * 128], AT[:, i, 0:128], Vtm[:, c, :], [atk(i), 'Vtm'], [bk])
            act(KA[:, :, 128:256], bank[:, :].rearrange("p (c k) -> p c k", k=128), AF.Identity, [bk], ['KAv%d' % grp], scale=-1.0)
            yield
            for pair in range(2):
                bank, bk = PS()
                for jj in range(2):
                    i = pair * 2 + jj
                    mm(bank[:, jj * 256:(jj + 1) * 256], LX[cur][:, i, 128:256], KA[:, i, :],
                       [xk(i), 'KAk%d' % grp, 'KAv%d' % grp], [bk])
                cp('act', KU[:, pair * 2:pair * 2 + 2, :], bank[:, :].rearrange("p (c k) -> p c k", k=256), [bk],
                   ['KU%d_%d' % (grp, pair)])
            kuk = lambda i: 'KU%d_%d' % (grp, i // 2)
            t0 = pc * PW + grp * 256
            yield
            for pair in range(2):
                bank, bk = PS()
                for jj in range(2):
                    i = pair * 2 + jj
                    mm(bank[:, jj * 192:(jj + 1) * 192], KU[:, i, 0:128], AT[:, i, 448:640], [kuk(i), atk(i), 'ATB%d' % grp], [bk])
                b3 = bank[:, 0:384].rearrange("p (c k) -> p c k", k=192)
                ta = t0 + pair * 128
                tt('dve', Rhat[d][:, ta:ta + 128].rearrange("p (c k) -> p c k", k=64),
                   QR[:, grp * 4 + pair * 2:grp * 4 + pair * 2 + 2, 128:192], b3[:, :, 0:64],
                   ALU.subtract, [bk, 'QR'], ['Rhat%d_%d' % (d, pc)])
                for jj in range(2):
                    i = pair * 2 + jj
                    cg = pc * 8 + cs[i]
                    st = cg if d == 0 else NCH - 1 - cg
                    stt('dve', MTS[:, st, d, :], identf[:, :], GC[:, d, cg:cg + 1], bank[:, jj * 192 + 64:(jj + 1) * 192],
                        ALU.mult, ALU.subtract, [bk, 'identf', 'GC%d' % pc], ['MTS%d' % st])
            bank, bk = PS()
            for i, c in enumerate(cs):
                mm(bank[:, i * 64:(i + 1) * 64], Vtm[:, c, :], AT[:, i, 128:192], ['Vtm', atk(i)], [bk], start=True, stop=False)
                mm(bank[:, i * 64:(i + 1) * 64], KU[:, i, 128:256], AT[:, i, 448:512], [kuk(i), atk(i)], [bk], start=False, stop=True)
            if d == 0:
                cp('act', Yacc[:, t0:t0 + 256], bank[:, 0:256], [bk], ['Yacc%d' % pc])
            else:
                tt('dve', Yacc[:, t0:t0 + 256], bank[:, 0:256], Yacc[:, t0:t0 + 256], ALU.add,
                   [bk, 'Yacc%d' % pc], ['Yacc%d' % pc])
            yield
            bank, bk = PS()
            for i, c in enumerate(cs):
                mm(bank[:, i * 128:(i + 1) * 128], Kdtm[:, c, :], Vtm[:, c, :], ['Kdtm%d' % grp, 'Vtm'], [bk], start=True, stop=False)
                mm(bank[:, i * 128:(i + 1) * 128], AT[:, i, 512:640], KU[:, i, 128:256], ['ATB%d' % grp, kuk(i)], [bk],
                   start=False, stop=True)
            for i, c in enumerate(cs):
                cg = pc * 8 + c
                st = cg if d == 0 else NCH - 1 - cg
                cp('act', NS[:, st, d, :], bank[:, i * 128:(i + 1) * 128], [bk], ['NS%d' % st])
            yield

        def bonus_f(pc):
            tsl = slice(pc * PW, (pc + 1) * PW)
            t = f32t
            tt('pool', t['nrm'][:, :], t['kd0'][:, :], t['kd1'][:, :], ALU.add, ['kd0', 'kd1'], ['nrm'])
            tt('pool', t['nrm'][:, :], t['nrm'][:, :], ld['r'][:, :], ALU.mult, ['nrm', 'ld_r'], ['nrm'])
            act(sqk[:, :], t['nrm'][:, :], AF.Identity, ['nrm', 'par'], ['sqk'], scale=pcol('rk', u))
            bank, bk = PS()
            mm(bank[:, :], bones[:, :], sqk[:, :], ['bones', 'sqk'], [bk])
            tt('dve', bonus[:, tsl], bank[:, :], ld['v'][:, :], ALU.mult, [bk, 'ld_v'], ['bonus%d' % pc])

        def interleave(gens, stagger=STAGGER):
            gens = list(gens)
            for _ in range(stagger):
                try:
                    next(gens[0])
                except StopIteration:
                    gens.pop(0)
                    break
            while gens:
                for g_ in list(gens):
                    try:
                        next(g_)
                    except StopIteration:
                        gens.remove(g_)

        def head():
            yield from piece_a(0)
            piece_b(0)
            yield
            yield from dprep(0, 0)
            dprod(0, 0)
            yield

        def body():
            for pc in range(NPC):
                interleave([grp_gen(0, pc, 0), grp_gen(1, pc, 0), dprep(pc, 1)])
                dprod(pc, 1)
                bonus_f(pc)
                nxt = []
                if pc + 1 < NPC:
                    def nxt_gen(pc=pc):
                        yield from piece_a(pc + 1)
                        yield from dprep(pc + 1, 0)
                    nxt = [nxt_gen()]
                interleave([grp_gen(0, pc, 1), grp_gen(1, pc, 1)] + nxt)
                if pc + 1 < NPC:
                    piece_b(pc + 1)
                    dprod(pc + 1, 0)


        def chain_out():
            for s in range(1, NCH):
                bank, bk = PS()
                for d in range(2):
                    mm(bank[:, d * 128:(d + 1) * 128], MTS[:, s, d, :], NS[:, s - 1, d, :], ['MTS%d' % s, 'NS%d' % (s - 1)], [bk])
                tt('dve', NS[:, s, :, :], bank[:, 0:256].rearrange("p (c k) -> p c k", k=128), NS[:, s, :, :], ALU.add,
                   [bk, 'NS%d' % s], ['NS%d' % s])
                if s % 2 == 0:
                    yield
            for d in range(2):
                for pc in range(NPC):
                    bank, bk = PS()
                    n = 0
                    for c in range(8):
                        cg = pc * 8 + c
                        st = cg if d == 0 else NCH - 1 - cg
                        if st == 0:
                            continue
                        mm(bank[:, c * 64:(c + 1) * 64], NS[:, st - 1, d, :], Rhat[d][:, cg * 64:(cg + 1) * 64],
                           ['NS%d' % (st - 1), 'Rhat%d_%d' % (d, pc)], [bk])
                    lo, hi = 0, 8
                    if d == 0 and pc == 0:
                        lo = 1
                    if d == 1 and pc == NPC - 1:
                        hi = 7
                    tsl2 = slice(pc * PW + lo * 64, pc * PW + hi * 64)
                    tt('dve', Yacc[:, tsl2], bank[:, lo * 64:hi * 64], Yacc[:, tsl2], ALU.add, [bk, 'Yacc%d' % pc], ['Yacc%d' % pc])
                    yield

        def epi():
            EY = GB[0]['KU'][:, :, :].rearrange("p a b -> p (a b)").rearrange("p (a b) -> p a b", b=PW)
            t = f32t
            for pc in range(NPC):
                tsl = slice(pc * PW, (pc + 1) * PW)
                cp('pool', EY[:, 0, :], Yacc[:, tsl], ['Yacc%d' % pc], ['KU0_0', 'KU0_1'])
                act(EY[:, 1, :], Yacc[:, tsl], AF.Square, ['Yacc%d' % pc], ['KU0_0', 'KU0_1'])
                b1, bk1 = PS()
                mm(b1[:, :], bones[:, :], EY[:, 0, :], ['bones', 'KU0_0', 'KU0_1'], [bk1])
                b2, bk2 = PS()
                mm(b2[:, :], bones[:, :], EY[:, 1, :], ['bones', 'KU0_0', 'KU0_1'], [bk2])
                act(t['S'][:, :], b1[:, :], AF.Identity, [bk1], ['S'], scale=1.0 / 64)
                tt('pool', t['sg'][:, :], t['S'][:, :], t['S'][:, :], ALU.mult, ['S'], ['sg'])
                stt('dve', t['Sd'][:, :], b2[:, :], 1.0 / 64, t['sg'][:, :], ALU.mult, ALU.subtract, [bk2, 'sg'], ['Sd'])
                act(t['a'][:, :], t['Sd'][:, :], AF.Ln, ['Sd', 'epsg'], ['a'], bias=epsg[:, 0:1])
                act(t['a'][:, :], t['a'][:, :], AF.Exp, ['a'], ['a'], scale=-0.5)
                tt('pool', t['eGx'][:, :], Yacc[:, tsl], t['S'][:, :], ALU.subtract, ['Yacc%d' % pc, 'S'], ['eGx'])
                tt('pool', t['eGx'][:, :], t['eGx'][:, :], t['a'][:, :], ALU.mult, ['eGx', 'a'], ['eGx'])
                act(t['eGx'][:, :], t['eGx'][:, :], AF.Identity, ['eGx', 'par'], ['eGx'], scale=pcol('lnw', u), bias=pcol('lnb', u))
                tt('dve', t['eGx'][:, :], t['eGx'][:, :], bonus[:, tsl], ALU.add, ['eGx', 'bonus%d' % pc], ['eGx'])
                dma(ld['lw'][:, :], pslice(g_cb, pc), [], ['ld_lw'])
                act(sgp[:, :], ld['lw'][:, :], AF.Silu, ['ld_lw'], ['sgp'])
                tt('dve', sqk[:, :], t['eGx'][:, :], sgp[:, :], ALU.mult, ['eGx', 'sgp'], ['sqk'])
                dma(mixT_d[u * 128:(u + 1) * 128, tsl], sqk[:, :], ['sqk'], [])

        return head, body, chain_out, epi

    units = [rw_unit(u) for u in range(rw_units)]
    if units:
        for _ in units[0][0]():
            pass
    for u in range(rw_units):
        head_, body_, chain_, epi_ = units[u]
        body_()
        gl = [chain_()]
        if u + 1 < rw_units:
            gl.append(units[u + 1][0]())
        run_window(gl, 2)
        epi_()
    P.barrier()
    A.reset(mB)
    if stop_after == 'B1':
        P.emit()
        return nc

    Yacc = A.alloc("hYacc", [128, T], F32)
    Qt = [A.alloc("Qt", [128, T], BF16) for _ in range(2)]
    NS = A.alloc("hNS", [128, NCH, 2, 128], BF16)
    GCS = [A.alloc("GCS", [128, NCH], F32) for _ in range(2)]
    sgate = A.alloc("hsgate", [128, T], BF16)
    HP = []
    for q_ in range(2):
        HP.append(dict(ld={nm: A.alloc("hld_" + nm, [128, PW], F32) for nm in ('q', 'f0', 'f1', 'i', 'g')},
                       vb=A.alloc("hvb", [128, PW], BF16), Vtm=A.alloc("hVtm", [64, 8, 128], BF16)))
    HS = []
    for q_ in range(4):
        HS.append(dict(t={nm: A.alloc("ht_" + nm, [128, PW], F32) for nm in ('f', 'lf', 'kf', 'G', 'Gd')},
                       eG=A.alloc("heG", [128, PW], BF16), enG=A.alloc("henG", [128, PW], BF16),
                       eGd=A.alloc("heGd", [128, PW], BF16),
                       kb=A.alloc("hkb", [128, PW], BF16), Kd=A.alloc("hKd", [128, PW], BF16),
                       Kdtm=A.alloc("hKdtm", [64, 8, 128], BF16), AT=A.alloc("hAT", [64, 8, 64], BF16)))
    r1 = A.alloc("hr1", [128, PW], F32)
    r2 = A.alloc("hr2", [128, PW], F32)
    osq = A.alloc("osq", [128, PW], BF16)
    obf = A.alloc("obf", [128, PW], BF16)

    def hg_unit(h):
        cbs = {'q': 18 + h, 'f0': 22 + h, 'f1': 26 + h, 'i': 30 + h, 'g': 34 + h}

        def piece_gen(pc):
            q_ = pc % 2
            B_ = HP[q_]
            ld = B_['ld']
            tsl = slice(pc * PW, (pc + 1) * PW)
            for nm in ('f0', 'f1', 'q', 'i', 'g'):
                dma(ld[nm][:, :], pslice(cbs[nm], pc), [], ['hld%d_%s' % (q_, nm)])
            cp('pool', B_['vb'][:, :], ld['i'][:, :], ['hld%d_i' % q_], ['hvb%d' % q_])
            act(sgate[:, tsl], ld['g'][:, :], AF.Silu, ['hld%d_g' % q_], ['hsg%d' % pc])
            yield
            for half in range(2):
                bank, bk = PS()
                for i in range(4):
                    c = half * 4 + i
                    mm(bank[0:64, i * 128:(i + 1) * 128], B_['vb'][:, c * 64:(c + 1) * 64], identb[:, :],
                       ['hvb%d' % q_, 'identb'], [bk])
                cp('dve', B_['Vtm'][:, half * 4:half * 4 + 4, :], bank[0:64, :].rearrange("p (c k) -> p c k", k=128),
                   [bk], ['hVtm%d' % q_])
            yield

        def pair_gen(pc):
            q_ = pc % 2
            B_ = HP[q_]
            ld = B_['ld']
            tsl = slice(pc * PW, (pc + 1) * PW)
            SS = [HS[q_ * 2 + d] for d in range(2)]
            KK = [(lambda nm, sid=q_ * 2 + d: 'h%s_%d' % (nm, sid)) for d in range(2)]
            for d in range(2):
                t, K = SS[d]['t'], KK[d]
                fk = 'f%d' % d
                act(t['f'][:, :], ld[fk][:, :], AF.Sigmoid, ['hld%d_%s' % (q_, fk)], [K('f')])
            for d in range(2):
                t, K = SS[d]['t'], KK[d]
                ts('dve', t['f'][:, :], t['f'][:, :], oml[:, h:h + 1], lb[:, h:h + 1], ALU.mult, ALU.add, [K('f'), 'oml', 'lb'], [K('f')])
            for d in range(2):
                t, K = SS[d]['t'], KK[d]
                act(t['lf'][:, :], t['f'][:, :], AF.Ln, [K('f')], [K('lf')])
                ts('pool', t['kf'][:, :], t['f'][:, :], -1.0, 1.0, ALU.mult, ALU.add, [K('f')], [K('kf')])
            yield
            GT = []
            for d in range(2):
                t, K = SS[d]['t'], KK[d]
                G3 = t['G'][:, :].rearrange("p (c k) -> p c k", k=64)
                if d == 0:
                    scan(t['G'][:, :], rmask[:, :, :].rearrange("p c k -> p (c k)"), t['lf'][:, :], ['rmask', K('lf')], [K('G')])
                    Gtot = G3[:, :, 63:64]
                    Gtot2 = G3[:, :, 63]
                else:
                    scan(t['G'][:, ::-1], rmask[:, :, :].rearrange("p c k -> p (c k)"), t['lf'][:, ::-1], ['rmask', K('lf')], [K('G')])
                    Gtot = G3[:, :, 0:1]
                    Gtot2 = G3[:, :, 0]
                GT.append(Gtot2)
                tt('pool', t['Gd'][:, :].rearrange("p (c k) -> p c k", k=64), Gtot.broadcast_to([128, 8, 64]), G3,
                   ALU.subtract, [K('G')], [K('Gd')])
            yield
            for d in range(2):
                S_, t, K = SS[d], SS[d]['t'], KK[d]
                act(S_['eG'][:, :], t['G'][:, :], AF.Exp, [K('G')], [K('eG')])
                act(S_['enG'][:, :], t['G'][:, :], AF.Exp, [K('G')], [K('enG')], scale=-1.0)
                act(S_['eGd'][:, :], t['Gd'][:, :], AF.Exp, [K('Gd')], [K('eGd')])
                if d == 0:
                    act(GCS[0][:, pc * 8:(pc + 1) * 8], GT[d], AF.Exp, [K('G')], ['GCS%d_%d' % (d, s_) for s_ in range(pc * 8, pc * 8 + 8)])
                else:
                    lo_ = NCH - 8 - pc * 8
                    act(GCS[1][:, lo_:lo_ + 8][:, ::-1], GT[d], AF.Exp, [K('G')], ['GCS%d_%d' % (d, s_) for s_ in range(lo_, lo_ + 8)])
            yield
            for d in range(2):
                S_, t, K = SS[d], SS[d]['t'], KK[d]
                tt('dve', Qt[d][:, tsl], ld['q'][:, :], S_['eG'][:, :], ALU.mult, ['hld%d_q' % q_, K('eG')], ['Qt%d_%d' % (d, pc)])
                tt('pool', S_['kb'][:, :], t['kf'][:, :], S_['enG'][:, :], ALU.mult, [K('kf'), K('enG')], [K('kb')])
                tt('dve', S_['Kd'][:, :], t['kf'][:, :], S_['eGd'][:, :], ALU.mult, [K('kf'), K('eGd')], [K('Kd')])
            yield
            for d in range(2):
                S_, t, K = SS[d], SS[d]['t'], KK[d]
                for half in range(2):
                    bank, bk = PS()
                    for i in range(4):
                        c = half * 4 + i
                        mm(bank[0:64, i * 128:(i + 1) * 128], S_['Kd'][:, c * 64:(c + 1) * 64], identb[:, :], [K('Kd'), 'identb'], [bk])
                    cp('act' if half == 0 else 'dve', S_['Kdtm'][:, half * 4:half * 4 + 4, :],
                       bank[0:64, :].rearrange("p (c k) -> p c k", k=128), [bk], [K('Kdtm')])
                bank, bk = PS()
                for c in range(8):
                    mm(bank[0:64, c * 64:(c + 1) * 64], S_['kb'][:, c * 64:(c + 1) * 64],
                       Qt[d][:, pc * PW + c * 64:pc * PW + (c + 1) * 64], [K('kb'), 'Qt%d_%d' % (d, pc)], [bk])
                tt('dve', S_['AT'][:, :, :], bank[0:64, :].rearrange("p (c k) -> p c k", k=64), maskH[d][:, :, :], ALU.mult,
                   [bk, 'maskH%d' % d], [K('AT')])
            yield
            for d in range(2):
                S_, t, K = SS[d], SS[d]['t'], KK[d]
                bank, bk = PS()
                for c in range(8):
                    mm(bank[:, c * 64:(c + 1) * 64], B_['Vtm'][:, c, :], S_['AT'][:, c, :], ['hVtm%d' % q_, K('AT')], [bk])
                if d == 0:
                    cp('act', Yacc[:, tsl], bank[:, :], [bk], ['hY%d' % pc])
                else:
                    tt('dve', Yacc[:, tsl], bank[:, :], Yacc[:, tsl], ALU.add, [bk, 'hY%d' % pc], ['hY%d' % pc])
                for half in range(2):
                    bank, bk = PS()
                    for i in range(4):
                        c = half * 4 + i
                        mm(bank[:, i * 128:(i + 1) * 128], S_['Kdtm'][:, c, :], B_['Vtm'][:, c, :], [K('Kdtm'), 'hVtm%d' % q_], [bk])
                    for i in range(4):
                        c = half * 4 + i
                        cg = pc * 8 + c
                        st = cg if d == 0 else NCH - 1 - cg
                        cp('act' if half == 0 else 'dve', NS[:, st, d, :], bank[:, i * 128:(i + 1) * 128], [bk], ['hNS%d_%d' % (d, st)])
            yield

        gens = []
        for pc in range(NPC):
            def work(pc=pc):
                yield from piece_gen(pc)
                yield from pair_gen(pc)
            gens.append(work())
        run_window(gens, 2)
        for s in range(1, NCH):
            for d in range(2):
                stt('dve', NS[:, s, d, :], NS[:, s - 1, d, :], GCS[d][:, s:s + 1], NS[:, s, d, :],
                    ALU.mult, ALU.add, ['hNS%d_%d' % (d, s - 1), 'hNS%d_%d' % (d, s), 'GCS%d_%d' % (d, s)], ['hNS%d_%d' % (d, s)])
        for d in range(2):
            for pc in range(NPC):
                bank, bk = PS()
                lo, hi = 0, 8
                for c in range(8):
                    cg = pc * 8 + c
                    st = cg if d == 0 else NCH - 1 - cg
                    if st == 0:
                        if d == 0:
                            lo = 1
                        else:
                            hi = 7
                        continue
                    mm(bank[:, c * 64:(c + 1) * 64], NS[:, st - 1, d, :], Qt[d][:, cg * 64:(cg + 1) * 64],
                       ['hNS%d_%d' % (d, st - 1), 'Qt%d_%d' % (d, pc)], [bk])
                tsl2 = slice(pc * PW + lo * 64, pc * PW + hi * 64)
                tt('dve', Yacc[:, tsl2], bank[:, lo * 64:hi * 64], Yacc[:, tsl2], ALU.add, [bk, 'hY%d' % pc], ['hY%d' % pc])
        for pc in range(NPC):
            tsl = slice(pc * PW, (pc + 1) * PW)
            act(osq[:, :], Yacc[:, tsl], AF.Square, ['hY%d' % pc], ['osq'])
            bank, bk = PS()
            mm(bank[:, :], onesb[:, :], osq[:, :], ['onesb', 'osq'], [bk])
            act(r1[:, :], bank[:, :], AF.Ln, [bk, 'epsn'], ['hr1'], bias=epsn[:, 0:1], scale=1.0 / 128)
            act(r1[:, :], r1[:, :], AF.Exp, ['hr1'], ['hr1'], scale=-0.5)
            tt('dve', r2[:, :], Yacc[:, tsl], r1[:, :], ALU.mult, ['hY%d' % pc, 'hr1'], ['hr2'])
            stt('dve', obf[:, :], r2[:, :], pcol('hgn', 0), sgate[:, tsl], ALU.mult, ALU.mult,
                ['hr2', 'par', 'hsg%d' % pc], ['obf'])
            dma(mixT_d[512 + h * 128:512 + (h + 1) * 128, tsl], obf[:, :], ['obf'], [], eng='pool')

    for h in range(hg_units):
        hg_unit(h)
    P.barrier()
    A.reset(mB)
    if stop_after == 'B':
        P.emit()
        return nc

    wob = A.alloc("wob", [128, 8, D], BF16)
    gpost = A.alloc("gpost", [128, D], F32)
    wost = [A.alloc("wost", [128, D], F32) for _ in range(2)]
    dma(gpost[:, :], gpost_d[:, :], [], ['gpost'])
    wout_v = wout_d.rearrange("(kc p) n -> p kc n", p=128)
    for kc in range(8):
        dma(wost[kc % 2][:, :], wout_v[:, kc, :], [], ['wost%d' % (kc % 2)])
        cp('dve' if kc % 2 == 0 else 'act', wob[:, kc, :], wost[kc % 2][:, :], ['wost%d' % (kc % 2)], ['wob'])
    NSL = 4
    mx = [A.alloc("mx", [128, 8, 128], BF16) for _ in range(NSL)]
    xt = [A.alloc("xt", [128, D], F32) for _ in range(NSL)]
    ot = [A.alloc("ot", [128, D], F32) for _ in range(NSL)]
    junk = A.alloc("junk", [128, 512], BF16)
    ss = [A.alloc("ss", [128, 4], F32) for _ in range(NSL)]
    mix_v = mixT_d.rearrange("(kc p) t -> p kc t", p=128)
    NT = T // 128

    def c_load(tt_i):
        s = tt_i % NSL
        rows = slice(tt_i * 128, (tt_i + 1) * 128)
        dma(mx[s][:, :, :], mix_v[:, :, rows], [], ['mx%d' % s])
        dma(xt[s][:, :], x_d[rows, :], [], ['xt%d' % s])

    for i in range(min(NSL - 1, NT)):
        c_load(i)
    for tt_i in range(NT):
        s = tt_i % NSL
        rows = slice(tt_i * 128, (tt_i + 1) * 128)
        if tt_i + NSL - 1 < NT:
            c_load(tt_i + NSL - 1)
        banks = []
        for half in range(2):
            bank, bk = PS()
            banks.append((bank, bk))
            for kc in range(8):
                mm(bank[:, :], mx[s][:, kc, :], wob[:, kc, half * 512:(half + 1) * 512], ['mx%d' % s, 'wob'], [bk],
                   start=(kc == 0), stop=(kc == 7))
            act(junk[:, :], bank[:, :], AF.Square, [bk], ['junk', 'ss%d' % s], accum=ss[s][:, half:half + 1])
        act(ss[s][:, 2:3], ss[s][:, 0:1], AF.Identity, ['ss%d' % s], ['ss%d' % s], bias=ss[s][:, 1:2])
        act(ss[s][:, 3:4], ss[s][:, 2:3], AF.Sqrt, ['ss%d' % s, 'epsn'], ['ss%d' % s], bias=epsn[:, 0:1], scale=1.0 / D)
        recip(ss[s][:, 3:4], ss[s][:, 3:4], ['ss%d' % s], ['ss%d' % s])
        for half in range(2):
            bank, bk = banks[half]
            hs = slice(half * 512, (half + 1) * 512)
            stt('dve', ot[s][:, hs], bank[:, :], ss[s][:, 3:4], gpost[:, hs], ALU.mult, ALU.mult,
                [bk, 'ss%d' % s, 'gpost'], ['ot%d' % s])
        tt('dve', ot[s][:, :], ot[s][:, :], xt[s][:, :], ALU.add, ['ot%d' % s, 'xt%d' % s], ['ot%d' % s])
        dma(out_d[rows, :], ot[s][:, :], ['ot%d' % s], [], eng='pool')
    P.emit()
    return nc


def prep_inputs(inputs, b):
    f = lambda a: np.ascontiguousarray(np.asarray(a, dtype=np.float32))
    x = f(inputs['x'])
    par = np.zeros((128, NPAR), np.float32)
    cols = []
    cols.append(f(inputs['pre_norm_g'])[0].reshape(8, 128).T)
    cols.append(f(inputs['rw_shift_prev'])[0].reshape(18, 128).T)
    cols.append(f(inputs['rw_shift_next'])[0].reshape(18, 128).T)
    cols.append(f(inputs['rw_w0'])[0].reshape(8, 128).T)
    cols.append(f(inputs['rw_a0'])[0].reshape(8, 128).T)
    cols.append(f(inputs['rw_k_k'])[0].reshape(4, 128).T)
    cols.append(f(inputs['rw_k_a'])[0].reshape(4, 128).T)
    cols.append(f(inputs['rw_r_k'])[0].reshape(4, 128).T)
    cols.append(f(inputs['rw_ln_w'])[0].reshape(4, 128).T)
    cols.append(f(inputs['rw_ln_b'])[0].reshape(4, 128).T)
    lbl = f(inputs['hg_lb_logits'])
    cols.append(lbl[0].reshape(4, 128).T)
    cols.append(lbl[1].reshape(4, 128).T)
    cols.append(f(inputs['hg_norm_g'])[0].reshape(1, 128).T)
    allc = np.concatenate(cols, axis=1)
    par[:, :allc.shape[1]] = allc
    m = {
        "xT": np.ascontiguousarray(x[b].T),
        "x": np.ascontiguousarray(x[b]),
        "w_in": f(inputs['w_in'])[0],
        "w_out": f(inputs['w_out'])[0],
        "par": par,
        "w2s": f(inputs['rw_w2'])[0].reshape(128, 512),
        "a2s": f(inputs['rw_a2'])[0].reshape(128, 512),
        "gpost": np.ascontiguousarray(np.broadcast_to(f(inputs['post_norm_g'])[0][None, :], (128, D))),
    }
    return m


def kernel(**inputs):
    nc = build()
    in_maps = [prep_inputs(inputs, c % 4) for c in range(8)]
    res = run_bass_kernel_spmd(nc, in_maps, core_ids=list(range(8)))
    out = np.stack([np.asarray(res.results[c]["out"], dtype=np.float32) for c in range(4)], axis=0)
    return out
```

```python
import numpy as np
import concourse.bass as bass
import concourse.mybir as mybir
from concourse.bass_utils import run_bass_kernel_spmd

F32 = mybir.dt.float32
BF16 = mybir.dt.bfloat16
AF = mybir.ActivationFunctionType
ALU = mybir.AluOpType

T = 4096
D = 1024
NPC = 8
PW = 512
C = 64
NCH = 64
NCB = 38
INC = 4864
CDEC = 0.6065306597126334
NORM_EPS = 1e-6
GN_EPS = 64e-5
HG_LVL = 99
STAGGER = 0
TRANSITIVE = True
NPAR = 96


class Prog:
    ENG = ('pe', 'act', 'dve', 'pool', 'sp')

    def __init__(self, nc, n_dma_sems=10):
        self.nc = nc
        self.ops = {e: [] for e in self.ENG}
        self.sems = {e: nc.alloc_semaphore(name='s_' + e) for e in self.ENG}
        self.dma_sems = [nc.alloc_semaphore(name='d%d' % i) for i in range(n_dma_sems)]
        self.dma_val = [0] * n_dma_sems
        self.dma_rr = 0
        self.count = {e: 0 for e in self.ENG}
        self.seen = {e: {} for e in self.ENG}
        self.pending = {e: [] for e in self.ENG}
        self.lastw = {}
        self.readers = {}
        self.clock = {}

    def _need(self, eng, ev, waits):
        if ev is None:
            return
        sid, val, src = ev
        if src == eng and eng == 'pe':
            return
        if self.seen[eng].get(sid, 0) >= val:
            return
        self.seen[eng][sid] = val
        waits.append((sid, val))
        if TRANSITIVE:
            for k2, v2 in self.clock.get((sid, val), {}).items():
                if self.seen[eng].get(k2, 0) < v2:
                    self.seen[eng][k2] = v2

    def op(self, eng, fn, reads=(), writes=(), dma=False):
        waits = []
        for ev in self.pending[eng]:
            self._need(eng, ev, waits)
        self.pending[eng] = []
        for k in reads:
            self._need(eng, self.lastw.get(k), waits)
        for k in writes:
            self._need(eng, self.lastw.get(k), waits)
            for ev in self.readers.get(k, ()):
                self._need(eng, ev, waits)
        if dma:
            di = self.dma_rr
            self.dma_rr = (self.dma_rr + 1) % len(self.dma_sems)
            if self.dma_val[di] > 0:
                self._need(eng, (('d', di), self.dma_val[di], 'dma'), waits)
            self.dma_val[di] += 16
            ev = (('d', di), self.dma_val[di], 'dma')
            inc = (self.dma_sems[di], 16)
        else:
            self.count[eng] += 1
            ev = (('e', eng), self.count[eng], eng)
            inc = (self.sems[eng], 1)
        if TRANSITIVE:
            self.clock[(ev[0], ev[1])] = dict(self.seen[eng])
        for k in writes:
            self.lastw[k] = ev
            self.readers[k] = []
        for k in reads:
            if k not in writes:
                self.readers.setdefault(k, []).append(ev)
        self.ops[eng].append((fn, waits, inc))
        return ev

    def all_events(self):
        evs = [(('e', e), self.count[e], e) for e in self.ENG if self.count[e] > 0]
        evs += [(('d', i), v, 'dma') for i, v in enumerate(self.dma_val) if v > 0]
        return evs

    def barrier(self):
        evs = self.all_events()
        for e in self.ENG:
            self.pending[e] = list(evs)

    def _semh(self, sid):
        return self.dma_sems[sid[1]] if sid[0] == 'd' else self.sems[sid[1]]

    def emit(self):
        nc = self.nc
        final_events = self.all_events()
        with nc.Block() as block:
            def run(engname, engobj):
                for fn, waits, inc in self.ops[engname]:
                    for sid, val in waits:
                        engobj.wait_ge(self._semh(sid), val)
                    fn(engobj).then_inc(inc[0], inc[1])
                if engname == 'sp':
                    for sid, val, _ in final_events:
                        engobj.wait_ge(self._semh(sid), val)

            @block.tensor
            def _(e): run('pe', e)

            @block.scalar
            def _(e): run('act', e)

            @block.vector
            def _(e): run('dve', e)

            @block.gpsimd
            def _(e): run('pool', e)

            @block.sync
            def _(e): run('sp', e)


class Arena:
    def __init__(self, nc, cap):
        self.nc = nc
        self.cur = 16512
        self.cap = cap
        self.n = 0

    def alloc(self, name, shape, dtype):
        sz = 1
        for s in shape[1:]:
            sz *= s
        sz *= 2 if dtype == BF16 else 4
        sz = (sz + 63) // 64 * 64
        off = self.cur
        self.cur += sz
        assert self.cur <= self.cap, (name, self.cur, self.cap)
        self.n += 1
        return self.nc.alloc_sbuf_tensor_at("%s_%d" % (name, self.n), list(shape), dtype, offset=off)

    def mark(self):
        return self.cur

    def reset(self, m):
        print('arena peak', self.cur, 'of', self.cap)
        self.cur = m


def build(stop_after=None, rw_units=4, hg_units=4):
    nc = bass.Bass("TRN2", target_bir_lowering=False)
    xT_d = nc.dram_tensor("xT", [D, T], F32, kind="ExternalInput").ap()
    x_d = nc.dram_tensor("x", [T, D], F32, kind="ExternalInput").ap()
    win_d = nc.dram_tensor("w_in", [D, INC], F32, kind="ExternalInput").ap()
    wout_d = nc.dram_tensor("w_out", [D, D], F32, kind="ExternalInput").ap()
    par_d = nc.dram_tensor("par", [128, NPAR], F32, kind="ExternalInput").ap()
    w2s_d = nc.dram_tensor("w2s", [128, 512], F32, kind="ExternalInput").ap()
    a2s_d = nc.dram_tensor("a2s", [128, 512], F32, kind="ExternalInput").ap()
    gpost_d = nc.dram_tensor("gpost", [128, D], F32, kind="ExternalInput").ap()
    out_d = nc.dram_tensor("out", [T, D], F32, kind="ExternalOutput").ap()
    dbg = stop_after is not None
    pT_d = nc.dram_tensor("pT", [INC, T], F32, kind="ExternalOutput" if dbg else "Internal").ap()
    mixT_d = nc.dram_tensor("mixT", [D, T], BF16, kind="ExternalOutput" if dbg else "Internal").ap()

    P = Prog(nc)
    A = Arena(nc, 229344)
    psb = [nc.alloc_psum_tensor("psb%d" % i, [128, 512], F32) for i in range(8)]
    ps_rr = [0]

    def PS():
        b = ps_rr[0]
        ps_rr[0] = (b + 1) % 8
        return psb[b], "ps%d" % b

    def tt(eng, out, in0, in1, op, r, w):
        P.op(eng, lambda e: e.tensor_tensor(out=out, in0=in0, in1=in1, op=op), reads=r, writes=w)

    def ts(eng, out, in0, s1, s2, op0, op1, r, w):
        if s2 is None:
            P.op(eng, lambda e: e.tensor_scalar(out=out, in0=in0, scalar1=s1, scalar2=None, op0=op0), reads=r, writes=w)
        else:
            P.op(eng, lambda e: e.tensor_scalar(out=out, in0=in0, scalar1=s1, scalar2=s2, op0=op0, op1=op1), reads=r, writes=w)

    def stt(eng, out, in0, scalar, in1, op0, op1, r, w):
        P.op(eng, lambda e: e.scalar_tensor_tensor(out=out, in0=in0, scalar=scalar, in1=in1, op0=op0, op1=op1),
             reads=r, writes=w)

    def act(out, in_, func, r, w, bias=None, scale=1.0, accum=None):
        kw = {}
        if bias is not None:
            kw['bias'] = bias
        if accum is not None:
            kw['accum_out'] = accum
        P.op('act', lambda e: e.activation(out=out, in_=in_, func=func, scale=scale, **kw), reads=r, writes=w)

    def mm(out, lhsT, rhs, r, w, start=True, stop=True):
        P.op('pe', lambda e: e.matmul(out, lhsT=lhsT, rhs=rhs, start=start, stop=stop), reads=r, writes=w)

    def cp(eng, out, in_, r, w):
        if eng == 'act':
            act(out, in_, AF.Identity, r, w)
        else:
            P.op(eng, lambda e: e.tensor_copy(out=out, in_=in_), reads=r, writes=w)

    def dma(out, in_, r, w, eng='sp'):
        return P.op(eng, lambda e: e.dma_start(out=out, in_=in_), reads=r, writes=w, dma=True)

    def mset(eng, ap, val, w):
        P.op(eng, lambda e: e.memset(ap, val), writes=w)

    def recip(out, in_, r, w):
        P.op('dve', lambda e: e.reciprocal(out=out, in_=in_), reads=r, writes=w)

    def scan(out, d0, d1, r, w):
        P.op('dve', lambda e: e.tensor_tensor_scan(out=out, data0=d0, data1=d1, initial=0.0, op0=ALU.mult, op1=ALU.add),
             reads=r, writes=w)

    def asel(out, in_, pattern, cm, cmpop, r, w, base=0):
        P.op('pool', lambda e: e.affine_select(out=out, in_=in_, pattern=pattern, base=base, channel_multiplier=cm,
                                              compare_op=cmpop, fill=0.0), reads=r, writes=w)

    par = A.alloc("par", [128, NPAR], F32)
    dma(par[:, :], par_d[:, :], [], ['par'])
    PO = {}
    o = 0
    for nm, n in (('gpre', 8), ('mup', 18), ('mun', 18), ('w0', 8), ('a0', 8), ('kk', 4), ('ka', 4), ('rk', 4),
                  ('lnw', 4), ('lnb', 4), ('lb0', 4), ('lb1', 4), ('hgn', 1)):
        PO[nm] = o
        o += n
    assert o <= NPAR

    def pcol(nm, i=0):
        return par[:, PO[nm] + i:PO[nm] + i + 1]

    onesf = A.alloc("onesf", [128, 128], F32)
    negf = A.alloc("negf", [128, 128], F32)
    identf = A.alloc("identf", [128, 128], F32)
    identb = A.alloc("identb", [128, 128], BF16)
    onesb = A.alloc("onesb", [128, 128], BF16)
    bones = A.alloc("bones", [128, 128], BF16)
    epsn = A.alloc("epsn", [128, 1], F32)
    epsg = A.alloc("epsg", [128, 1], F32)
    c0 = A.alloc("c0", [128, 18], F32)
    omka = A.alloc("omka", [128, 4], F32)
    lb = A.alloc("lb", [128, 4], F32)
    oml = A.alloc("oml", [128, 4], F32)
    rmask = A.alloc("rmask", [128, 8, 64], F32)
    mset('pool', onesf[:, :], 1.0, ['onesf'])
    mset('pool', negf[:, :], -1.0, ['negf'])
    mset('pool', onesb[:, :], 1.0, ['onesb'])
    mset('pool', bones[:, :], 0.0, ['bones'])
    mset('pool', bones[0:64, 0:64], 1.0, ['bones'])
    mset('pool', bones[64:128, 64:128], 1.0, ['bones'])
    mset('pool', epsn[:, :], NORM_EPS, ['epsn'])
    mset('pool', epsg[:, :], GN_EPS, ['epsg'])
    mset('pool', rmask[:, :, :], 1.0, ['rmask'])
    mset('pool', rmask[:, :, 0:1], 0.0, ['rmask'])
    asel(identf[:, :], onesf[:, :], [[1, 128]], -1, ALU.is_equal, ['onesf'], ['identf'])
    cp('pool', identb[:, :], identf[:, :], ['identf'], ['identb'])
    tt('pool', c0[:, :], par[:, PO['mup']:PO['mup'] + 18], par[:, PO['mun']:PO['mun'] + 18], ALU.add, ['par'], ['c0'])
    ts('pool', c0[:, :], c0[:, :], -1.0, 1.0, ALU.mult, ALU.add, ['c0'], ['c0'])
    ts('pool', omka[:, :], par[:, PO['ka']:PO['ka'] + 4], -1.0, 1.0, ALU.mult, ALU.add, ['par'], ['omka'])
    w0h = A.alloc("w0h", [128, 8], F32)
    a0h = A.alloc("a0h", [128, 8], F32)
    halfc = A.alloc("halfc", [128, 1], F32)
    ts('pool', w0h[:, :], par[:, PO['w0']:PO['w0'] + 8], 0.5, None, ALU.mult, None, ['par'], ['w0h'])
    ts('pool', a0h[:, :], par[:, PO['a0']:PO['a0'] + 8], 0.5, None, ALU.mult, None, ['par'], ['a0h'])
    mset('pool', halfc[:, :], 0.5, ['halfc'])
    tt('pool', lb[:, :], par[:, PO['lb0']:PO['lb0'] + 4], par[:, PO['lb1']:PO['lb1'] + 4], ALU.subtract, ['par'], ['lb'])
    act(lb[:, :], lb[:, :], AF.Sigmoid, ['lb'], ['lb'])
    ts('pool', oml[:, :], lb[:, :], -1.0, 1.0, ALU.mult, ALU.add, ['lb'], ['oml'])

    maskX = [A.alloc("maskX%d" % d, [128, 512], BF16) for d in range(2)]
    maskH = [A.alloc("maskH%d" % d, [64, 8, 64], F32) for d in range(2)]
    signs = A.alloc("signs", [128, 512], F32)
    for d in range(2):
        mset('pool', maskX[d][:, :], 0.0, ['maskX%d' % d])
        if d == 0:
            pat, cm = [[1, 64]], -1
        else:
            pat, cm = [[-1, 64]], 1
        patT, cmT = ([[-1, 64]], 1) if d == 0 else ([[1, 64]], -1)
        for h in range(2):
            pr = slice(64 * h, 64 * h + 64)
            asel(maskX[d][pr, 64 * h:64 * h + 64], onesf[pr, 0:64], pat, cm, ALU.is_gt, ['onesf'], ['maskX%d' % d])
            asel(maskX[d][pr, 128:192], onesf[pr, 0:64], pat, cm, ALU.is_ge, ['onesf'], ['maskX%d' % d])
            asel(maskX[d][pr, 192 + 64 * h:192 + 64 * h + 64], negf[pr, 0:64], patT, cmT, ALU.is_gt, ['negf'], ['maskX%d' % d])
            asel(maskX[d][pr, 320 + 64 * h:320 + 64 * h + 64], negf[pr, 0:64], pat, cm, ALU.is_gt, ['negf'], ['maskX%d' % d])
            asel(maskX[d][pr, 448:512], onesf[pr, 0:64], pat, cm, ALU.is_ge, ['onesf'], ['maskX%d' % d])
            if h == 0:
                for q in range(8):
                    asel(maskH[d][0:64, q, :], onesf[0:64, 0:64], pat, cm, ALU.is_ge, ['onesf'], ['maskH%d' % d])
    mset('pool', signs[:, :], 1.0, ['signs'])
    mset('pool', signs[:, 128:256], -1.0, ['signs'])
    mset('pool', signs[:, 384:512], -1.0, ['signs'])

    w2b = A.alloc("w2b", [128, 512], BF16)
    a2b = A.alloc("a2b", [128, 512], BF16)
    mconst = A.mark()
    stg = [A.alloc("stg", [128, 512], F32) for _ in range(2)]
    dma(stg[0][:, :], w2s_d[:, :], [], ['stg0'])
    cp('dve', w2b[:, :], stg[0][:, :], ['stg0'], ['w2b'])
    dma(stg[1][:, :], a2s_d[:, :], [], ['stg1'])
    cp('dve', a2b[:, :], stg[1][:, :], ['stg1'], ['a2b'])

    xTb = A.alloc("xTb", [128, 8, T], BF16)
    rstd = A.alloc("rstd", [128, PW], F32)
    xst = [A.alloc("xst", [128, 8, PW], F32) for _ in range(2)]
    sq = [A.alloc("sq", [128, PW], BF16) for _ in range(2)]
    tmpa = A.alloc("tmpa", [128, PW], F32)
    i_x = 0
    for pc in range(NPC):
        tsl = slice(pc * PW, (pc + 1) * PW)
        s = pc % 2
        bank, bk = PS()
        for kc in range(8):
            s2 = i_x % 2
            i_x += 1
            dma(xst[s][:, kc, :], xT_d[kc * 128:(kc + 1) * 128, tsl], [], ['xst%d_%d' % (s, kc)])
            act(sq[s2][:, :], xst[s][:, kc, :], AF.Square, ['xst%d_%d' % (s, kc)], ['sq%d' % s2])
            mm(bank[:, :], onesb[:, :], sq[s2][:, :], ['onesb', 'sq%d' % s2], [bk], start=(kc == 0), stop=(kc == 7))
        act(tmpa[:, :], bank[:, :], AF.Ln, [bk, 'epsn'], ['tmpa'], bias=epsn[:, 0:1], scale=1.0 / D)
        act(rstd[:, :], tmpa[:, :], AF.Exp, ['tmpa'], ['rstd'], scale=-0.5)
        for kc in range(8):
            stt('dve', xTb[:, kc, tsl], xst[s][:, kc, :], pcol('gpre', kc), rstd[:, :], ALU.mult, ALU.mult,
                ['xst%d_%d' % (s, kc), 'par', 'rstd'], ['xTb%d' % pc])

    wst = [A.alloc("wst", [128, 8, 128], F32) for _ in range(2)]
    wb = [A.alloc("wb", [128, 8, 128], BF16) for _ in range(2)]
    pcb = [A.alloc("pcb", [128, T + 2], F32) for _ in range(3)]
    pout = [A.alloc("pout", [128, PW], F32) for _ in range(6)]
    for s in range(3):
        mset('pool', pcb[s][:, 0:1], 0.0, ['pcbh%d' % s])
        mset('pool', pcb[s][:, T + 1:T + 2], 0.0, ['pcbh%d' % s])
    win_v = win_d.rearrange("(kc p) n -> p kc n", p=128)

    def load_w(cb):
        s = cb % 2
        dma(wst[s][:, :, :], win_v[:, :, cb * 128:(cb + 1) * 128], [], ['wst%d' % s])
        cp('act', wb[s][:, :, :], wst[s][:, :, :], ['wst%d' % s], ['wb%d' % s])

    i_po = 0
    load_w(0)
    for cb in range(NCB):
        s = cb % 2
        sp_ = cb % 3
        if cb + 1 < NCB:
            load_w(cb + 1)
        for pc in range(NPC):
            tsl = slice(pc * PW, (pc + 1) * PW)
            bank, bk = PS()
            for kc in range(8):
                mm(bank[:, :], wb[s][:, kc, :], xTb[:, kc, tsl], ['wb%d' % s, 'xTb%d' % pc], [bk],
                   start=(kc == 0), stop=(kc == 7))
            cp('act', pcb[sp_][:, 1 + pc * PW:1 + (pc + 1) * PW], bank[:, :], [bk], ['pcb%d_%d' % (sp_, pc)])
        for pc in range(NPC):
            rk = ['pcb%d_%d' % (sp_, q) for q in (pc - 1, pc, pc + 1) if 0 <= q < NPC] + ['pcbh%d' % sp_]
            if cb < 18:
                po = pout[i_po % 6]
                pk = 'pout%d' % (i_po % 6)
                i_po += 1
                b0 = 1 + pc * PW
                act(po[:, :], pcb[sp_][:, b0:b0 + PW], AF.Identity, rk + ['c0'], [pk], scale=c0[:, cb:cb + 1])
                stt('dve', po[:, :], pcb[sp_][:, b0 - 1:b0 - 1 + PW], pcol('mup', cb), po[:, :], ALU.mult, ALU.add,
                    rk + ['par', pk], [pk])
                stt('dve', po[:, :], pcb[sp_][:, b0 + 1:b0 + 1 + PW], pcol('mun', cb), po[:, :], ALU.mult, ALU.add,
                    rk + ['par', pk], [pk])
                dma(pT_d[cb * 128:(cb + 1) * 128, pc * PW:(pc + 1) * PW], po[:, :], [pk], [], eng='pool')
            else:
                dma(pT_d[cb * 128:(cb + 1) * 128, pc * PW:(pc + 1) * PW], pcb[sp_][:, 1 + pc * PW:1 + (pc + 1) * PW],
                    rk, [], eng='pool')
    P.barrier()
    A.reset(mconst)
    if stop_after == 'A':
        P.emit()
        return nc

    def pslice(cb, pc):
        return pT_d[cb * 128:(cb + 1) * 128, pc * PW:(pc + 1) * PW]

    mB = A.mark()
    Yacc = A.alloc("Yacc", [128, T], F32)
    Rhat = [A.alloc("Rhat", [128, T], BF16) for _ in range(2)]
    MTS = A.alloc("MTS", [128, NCH, 2, 128], BF16)
    NS = A.alloc("NS", [128, NCH, 2, 128], BF16)
    bonus = A.alloc("bonus", [128, T], BF16)
    sgp = A.alloc("sgp", [128, PW], BF16)
    GC = A.alloc("GC", [128, 2, NCH], F32)
    ld = {nm: A.alloc("ld_" + nm, [128, PW], F32) for nm in ('r', 'k', 'v', 'lw', 'la')}
    th = A.alloc("th", [128, PW], BF16)
    lab = A.alloc("lab", [128, PW], BF16)
    f32t = {nm: A.alloc("t_" + nm, [128, PW], F32) for nm in
            ('sg', 'a', 'S', 'Sd', 'eGx', 'kkm', 'nrm', 'kap', 'kd0', 'kd1')}
    f32t['Sx'] = f32t['sg']
    f32t['t1'] = f32t['nrm']
    f32t['b'] = f32t['kkm']
    for nm in ('eG', 'enG', 'eGd', 'eGxb'):
        f32t[nm] = A.alloc("t_" + nm, [128, PW], BF16)
    sqk = A.alloc("sqk", [128, PW], BF16)
    QR = A.alloc("QR", [128, 8, 192], BF16)
    bbd = A.alloc("bbd", [128, 8, 128], BF16)
    kbd = A.alloc("kbd", [128, 8, 128], BF16)
    Kdbd = A.alloc("Kdbd", [128, 8, 128], BF16)
    Bdbd = A.alloc("Bdbd", [128, 8, 128], BF16)
    vbd = A.alloc("vbd", [128, 8, 128], BF16)
    Vtm = A.alloc("Vtm", [128, 8, 128], BF16)
    Kdtm = A.alloc("Kdtm", [128, 8, 128], BF16)
    GB = []
    for g_ in range(2):
        GB.append(dict(AT=A.alloc("AT", [128, 4, 640], BF16),
                       LX=[A.alloc("LX", [128, 4, 256], BF16) for _ in range(2)],
                       Ms=[A.alloc("Ms", [128, 4, 128], BF16) for _ in range(2)],
                       KA=A.alloc("KA", [128, 4, 256], BF16),
                       KU=A.alloc("KU", [128, 4, 256], BF16)))
    for tns, nm in ((QR, 'QR'), (bbd, 'bbd'), (kbd, 'kbd'), (Kdbd, 'Kdbd'), (Bdbd, 'Bdbd'), (vbd, 'vbd')):
        mset('pool', tns[:, :, :], 0.0, [nm])

    def run_window(gens, width):
        pend = list(gens)
        active = []
        while pend or active:
            while pend and len(active) < width:
                active.append(pend.pop(0))
            for g_ in list(active):
                try:
                    next(g_)
                except StopIteration:
                    active.remove(g_)

    def rw_unit(u):
        r_cb, k_cb, v_cb, g_cb = u, 4 + u, 8 + u, 12 + u
        usl = slice(u * 128, (u + 1) * 128)
        t = f32t
        v3 = lambda ap: ap.rearrange("p (c k) -> p c k", k=64)

        def piece_a(pc):
            tsl = slice(pc * PW, (pc + 1) * PW)
            for nm, cb in (('lw', 16), ('la', 17), ('k', k_cb), ('r', r_cb), ('v', v_cb)):
                dma(ld[nm][:, :], pslice(cb, pc), [], ['ld_' + nm])
            act(th[:, :], ld['lw'][:, :], AF.Tanh, ['ld_lw'], ['th'])
            cp('pool', lab[:, :], ld['la'][:, :], ['ld_la'], ['lab'])
            t = f32t
            act(t['kkm'][:, :], ld['k'][:, :], AF.Identity, ['ld_k', 'par'], ['kkm'], scale=pcol('kk', u))
            act(sqk[:, :], t['kkm'][:, :], AF.Square, ['kkm'], ['sqk'])
            bank, bk = PS()
            mm(bank[:, :], bones[:, :], sqk[:, :], ['bones', 'sqk'], [bk])
            act(t['nrm'][:, :], bank[:, :], AF.Sqrt, [bk], ['nrm'])
            ts('dve', t['nrm'][:, :], t['nrm'][:, :], 1e-12, None, ALU.max, None, ['nrm'], ['nrm'])
            recip(t['nrm'][:, :], t['nrm'][:, :], ['nrm'], ['nrm'])
            tt('pool', t['kap'][:, :], t['kkm'][:, :], t['nrm'][:, :], ALU.mult, ['kkm', 'nrm'], ['kap'])
            yield

        def piece_b(pc):
            for h in range(2):
                pr = slice(64 * h, 64 * h + 64)
                cp('act' if h == 0 else 'dve', vbd[pr, :, 64 * h:64 * h + 64], ld['v'][pr, :].rearrange("p (c k) -> p c k", k=64),
                   ['ld_v'], ['vbd'])
            for half in range(2):
                bank, bk = PS()
                for i in range(4):
                    c = half * 4 + i
                    mm(bank[:, i * 128:(i + 1) * 128], vbd[:, c, :], identb[:, :], ['vbd', 'identb'], [bk])
                cp('act', Vtm[:, half * 4:half * 4 + 4, :], bank[:, :].rearrange("p (c k) -> p c k", k=128), [bk], ['Vtm'])

        def dprep(pc, d):
            dsl = slice(64 * d, 64 * d + 64)
            bw, bkw = PS()
            mm(bw[:, :], w2b[dsl, usl], th[dsl, :], ['w2b', 'th'], [bkw])
            ba, bka = PS()
            mm(ba[:, :], a2b[dsl, usl], lab[dsl, :], ['a2b', 'lab'], [bka])
            act(t['sg'][:, :], bw[:, :], AF.Tanh, [bkw, 'w0h'], ['sg'], scale=0.5, bias=w0h[:, d * 4 + u:d * 4 + u + 1])
            act(t['a'][:, :], ba[:, :], AF.Tanh, [bka, 'a0h'], ['a'], scale=0.5, bias=a0h[:, d * 4 + u:d * 4 + u + 1])
            act(t['sg'][:, :], t['sg'][:, :], AF.Identity, ['sg', 'halfc'], ['sg'], scale=0.5, bias=halfc[:, 0:1])
            act(t['a'][:, :], t['a'][:, :], AF.Identity, ['a', 'halfc'], ['a'], scale=0.5, bias=halfc[:, 0:1])
            yield
            S3 = t['S'][:, :].rearrange("p (c k) -> p c k", k=64)
            if d == 0:
                scan(t['S'][:, :], rmask[:, :, :].rearrange("p c k -> p (c k)"), t['sg'][:, :], ['rmask', 'sg'], ['S'])
                Stot = S3[:, :, 63:64]
                Stot2 = S3[:, :, 63]
            else:
                scan(t['S'][:, ::-1], rmask[:, :, :].rearrange("p c k -> p (c k)"), t['sg'][:, ::-1], ['rmask', 'sg'], ['S'])
                Stot = S3[:, :, 0:1]
                Stot2 = S3[:, :, 0]
            tt('pool', t['Sx'][:, :], t['S'][:, :], t['sg'][:, :], ALU.subtract, ['S', 'sg'], ['sg'])
            yield
            act(t['eG'][:, :], t['S'][:, :], AF.Exp, ['S'], ['eG'], scale=-CDEC)
            act(t['eGxb'][:, :], t['Sx'][:, :], AF.Exp, ['sg'], ['eGxb'], scale=-CDEC)
            act(t['enG'][:, :], t['S'][:, :], AF.Exp, ['S'], ['enG'], scale=CDEC)
            act(GC[:, d, pc * 8:(pc + 1) * 8], Stot2, AF.Exp, ['S'], ['GC%d' % pc], scale=-CDEC)
            tt('pool', t['eGd'][:, :].rearrange("p (c k) -> p c k", k=64), t['enG'][:, :].rearrange("p (c k) -> p c k", k=64),
               GC[:, d, pc * 8:(pc + 1) * 8].unsqueeze(2).broadcast_to([128, 8, 64]), ALU.mult, ['enG', 'GC%d' % pc], ['eGd'])
            yield
            kd = t['kd%d' % d]
            kdk = 'kd%d' % d
            act(t['t1'][:, :], t['a'][:, :], AF.Identity, ['a', 'par', 'omka'], ['nrm'], scale=pcol('ka', u),
                bias=omka[:, u:u + 1])
            tt('pool', kd[:, :], t['nrm'][:, :], ld['k'][:, :], ALU.mult, ['nrm', 'ld_k'], [kdk])
            tt('pool', t['kkm'][:, :], t['kap'][:, :], t['a'][:, :], ALU.mult, ['kap', 'a'], ['kkm'])
            yield

        def dprod(pc, d):
            kd = t['kd%d' % d]
            kdk = 'kd%d' % d
            v3 = lambda ap: ap.rearrange("p (c k) -> p c k", k=64)
            for h in range(2):
                pr = slice(64 * h, 64 * h + 64)
                fs = slice(64 * h, 64 * h + 64)
                tt('dve', Kdbd[pr, :, fs], v3(kd[pr, :]), v3(t['eGd'][pr, :]), ALU.mult, [kdk, 'eGd'], ['Kdbd'])
                tt('dve', Bdbd[pr, :, fs], v3(t['kkm'][pr, :]), v3(t['eGd'][pr, :]), ALU.mult, ['kkm', 'eGd'], ['Bdbd'])
            tt('dve', QR[:, :, 128:192], v3(ld['r'][:, :]), v3(t['eG'][:, :]), ALU.mult, ['ld_r', 'eG'], ['QR'])
            for h in range(2):
                pr = slice(64 * h, 64 * h + 64)
                fs = slice(64 * h, 64 * h + 64)
                e1 = 'pool' if h == 0 else 'dve'
                tt(e1, QR[pr, :, fs], v3(t['kap'][pr, :]), v3(t['eGxb'][pr, :]), ALU.mult, ['kap', 'eGxb'], ['QR'])
                tt(e1, bbd[pr, :, fs], v3(t['kkm'][pr, :]), v3(t['enG'][pr, :]), ALU.mult, ['kkm', 'enG'], ['bbd'])
                tt(e1, kbd[pr, :, fs], v3(kd[pr, :]), v3(t['enG'][pr, :]), ALU.mult, [kdk, 'enG'], ['kbd'])
            for half in range(2):
                G_ = GB[half]
                for src, sk, dst, dk_, eng_ in ((QR, 'QR', G_['KA'][:, :, 0:128], 'KAk%d' % half, 'act'),
                                              (Kdbd, 'Kdbd', Kdtm[:, half * 4:half * 4 + 4, :], 'Kdtm%d' % half, 'dve'),
                                              (Bdbd, 'Bdbd', G_['AT'][:, :, 512:640], 'ATB%d' % half, 'act')):
                    bank, bk = PS()
                    for i in range(4):
                        c = half * 4 + i
                        mm(bank[:, i * 128:(i + 1) * 128], src[:, c, 0:128], identb[:, :], [sk, 'identb'], [bk])
                    cp(eng_, dst, bank[:, :].rearrange("p (c k) -> p c k", k=128), [bk], [dk_])

        def grp_gen(grp, pc, d):
            G_ = GB[grp]
            AT, LX, Ms, KA, KU = G_['AT'], G_['LX'], G_['Ms'], G_['KA'], G_['KU']
            cs = [grp * 4 + i for i in range(4)]
            atk = lambda i: 'AT%d_%d' % (grp, i)
            lxk = lambda q, i: 'LX%d_%d_%d' % (grp, q, i)
            msk = lambda q, i: 'Ms%d_%d_%d' % (grp, q, i)
            for i, c in enumerate(cs):
                bank, bk = PS()
                mm(bank[:, 0:192], kbd[:, c, :], QR[:, c, :], ['kbd', 'QR'], [bk])
                mm(bank[:, 192:320], QR[:, c, 0:128], bbd[:, c, :], ['bbd', 'QR'], [bk])
                mm(bank[:, 320:512], bbd[:, c, :], QR[:, c, :], ['bbd', 'QR'], [bk])
                tt('dve', AT[:, i, 0:512], bank[:, :], maskX[d][:, :], ALU.mult, [bk, 'maskX%d' % d], [atk(i)])
                tt('pool', LX[0][:, i, 128:256], AT[:, i, 320:448], identb[:, :], ALU.add, [atk(i), 'identb'], [lxk(0, i) + 'x'])
            yield
            cur = 0
            for j in range(6):
                nxt = 1 - cur if j >= 1 else 0
                if j == 0:
                    bankA, bkA = PS()
                    for i in range(4):
                        mm(bankA[:, i * 128:(i + 1) * 128], AT[:, i, 192:320], AT[:, i, 320:448], [atk(i)], [bkA])
                    banksA = [(bankA, bkA)]
                elif j <= 3:
                    banksA = []
                    for pair in range(2):
                        bankA, bkA = PS()
                        banksA.append((bankA, bkA))
                        for jj in range(2):
                            i = pair * 2 + jj
                            mm(bankA[:, jj * 256:(jj + 1) * 256], Ms[cur][:, i, :], LX[cur][:, i, :],
                               [msk(cur, i), lxk(cur, i), lxk(cur, i) + 'x'], [bkA])
                else:
                    bankA, bkA = PS()
                    for i in range(4):
                        mm(bankA[:, i * 128:(i + 1) * 128], Ms[cur][:, i, :], LX[cur][:, i, 128:256],
                           [msk(cur, i), lxk(cur, i) + 'x'], [bkA])
                    banksA = [(bankA, bkA)]
                if j <= 4:
                    bankB, bkB = PS()
                    for i in range(4):
                        if j == 0:
                            mm(bankB[:, i * 128:(i + 1) * 128], AT[:, i, 320:448], AT[:, i, 192:320], [atk(i)], [bkB])
                        else:
                            mm(bankB[:, i * 128:(i + 1) * 128], LX[cur][:, i, 0:128], Ms[cur][:, i, :],
                               [lxk(cur, i), msk(cur, i)], [bkB])
                mnx = 0 if j == 0 else nxt
                if j <= 4:
                    cp('act', Ms[mnx][:, :, :], bankB[:, :].rearrange("p (c k) -> p c k", k=128), [bkB],
                       [msk(mnx, i) for i in range(4)])
                if j == 0:
                    cp('act', LX[0][:, :, 0:128], bankA[:, :].rearrange("p (c k) -> p c k", k=128), [bkA],
                       [lxk(0, i) for i in range(4)])
                elif j <= 3:
                    for pair in range(2):
                        bankA, bkA = banksA[pair]
                        b3 = bankA[:, :].rearrange("p (c k) -> p c k", k=256)
                        ii = [pair * 2, pair * 2 + 1]
                        cp('act', LX[nxt][:, pair * 2:pair * 2 + 2, 0:128], b3[:, :, 0:128], [bkA], [lxk(nxt, i) for i in ii])
                        tt('dve', LX[nxt][:, pair * 2:pair * 2 + 2, 128:256], b3[:, :, 128:256],
                           LX[cur][:, pair * 2:pair * 2 + 2, 128:256], ALU.add,
                           [bkA] + [lxk(cur, i) + 'x' for i in ii] + [lxk(nxt, i) for i in ii], [lxk(nxt, i) + 'x' for i in ii])
                else:
                    tt('dve', LX[nxt][:, :, 128:256], bankA[:, :].rearrange("p (c k) -> p c k", k=128),
                       LX[cur][:, :, 128:256], ALU.add, [bkA] + [lxk(cur, i) + 'x' for i in range(4)],
                       [lxk(nxt, i) + 'x' for i in range(4)])
                cur = mnx if j == 0 else nxt
                yield
            xk = lambda i: lxk(cur, i) + 'x'
            bank, bk = PS()
            for i, c in enumerate(cs):
                mm(bank[:, i * 128:(i + 1) * 128], AT[:, i, 0:128], Vtm[:, c, :], [atk(i), 'Vtm'], [bk])
            act(KA[:, :, 128:256], bank[:, :].rearrange("p (c k) -> p c k", k=128), AF.Identity, [bk], ['KAv%d' % grp], scale=-1.0)
            yield
            for pair in range(2):
                bank, bk = PS()
                for jj in range(2):
                    i = pair * 2 + jj
                    mm(bank[:, jj * 256:(jj + 1) * 256], LX[cur][:, i, 128:256], KA[:, i, :],
                       [xk(i), 'KAk%d' % grp, 'KAv%d' % grp], [bk])
                cp('act', KU[:, pair * 2:pair * 2 + 2, :], bank[:, :].rearrange("p (c k) -> p c k", k=256), [bk],
                   ['KU%d_%d' % (grp, pair)])
            kuk = lambda i: 'KU%d_%d' % (grp, i // 2)
            t0 = pc * PW + grp * 256
            yield
            for pair in range(2):
                bank, bk = PS()
                for jj in range(2):
                    i = pair * 2 + jj
                    mm(bank[:, jj * 192:(jj + 1) * 192], KU[:, i, 0:128], AT[:, i, 448:640], [kuk(i), atk(i), 'ATB%d' % grp], [bk])
                b3 = bank[:, 0:384].rearrange("p (c k) -> p c k", k=192)
                ta = t0 + pair * 128
                tt('dve', Rhat[d][:, ta:ta + 128].rearrange("p (c k) -> p c k", k=64),
                   QR[:, grp * 4 + pair * 2:grp * 4 + pair * 2 + 2, 128:192], b3[:, :, 0:64],
                   ALU.subtract, [bk, 'QR'], ['Rhat%d_%d' % (d, pc)])
                for jj in range(2):
                    i = pair * 2 + jj
                    cg = pc * 8 + cs[i]
                    st = cg if d == 0 else NCH - 1 - cg
                    stt('dve', MTS[:, st, d, :], identf[:, :], GC[:, d, cg:cg + 1], bank[:, jj * 192 + 64:(jj + 1) * 192],
                        ALU.mult, ALU.subtract, [bk, 'identf', 'GC%d' % pc], ['MTS%d' % st])
            bank, bk = PS()
            for i, c in enumerate(cs):
                mm(bank[:, i * 64:(i + 1) * 64], Vtm[:, c, :], AT[:, i, 128:192], ['Vtm', atk(i)], [bk], start=True, stop=False)
                mm(bank[:, i * 64:(i + 1) * 64], KU[:, i, 128:256], AT[:, i, 448:512], [kuk(i), atk(i)], [bk], start=False, stop=True)
            if d == 0:
                cp('act', Yacc[:, t0:t0 + 256], bank[:, 0:256], [bk], ['Yacc%d' % pc])
            else:
                tt('dve', Yacc[:, t0:t0 + 256], bank[:, 0:256], Yacc[:, t0:t0 + 256], ALU.add,
                   [bk, 'Yacc%d' % pc], ['Yacc%d' % pc])
            yield
            bank, bk = PS()
            for i, c in enumerate(cs):
                mm(bank[:, i * 128:(i + 1) * 128], Kdtm[:, c, :], Vtm[:, c, :], ['Kdtm%d' % grp, 'Vtm'], [bk], start=True, stop=False)
                mm(bank[:, i * 128:(i + 1) * 128], AT[:, i, 512:640], KU[:, i, 128:256], ['ATB%d' % grp, kuk(i)], [bk],
                   start=False, stop=True)
            for i, c in enumerate(cs):
                cg = pc * 8 + c
                st = cg if d == 0 else NCH - 1 - cg
                cp('act', NS[:, st, d, :], bank[:, i * 128:(i + 1) * 128], [bk], ['NS%d' % st])
            yield

        def bonus_f(pc):
            tsl = slice(pc * PW, (pc + 1) * PW)
            t = f32t
            tt('pool', t['nrm'][:, :], t['kd0'][:, :], t['kd1'][:, :], ALU.add, ['kd0', 'kd1'], ['nrm'])
            tt('pool', t['nrm'][:, :], t['nrm'][:, :], ld['r'][:, :], ALU.mult, ['nrm', 'ld_r'], ['nrm'])
            act(sqk[:, :], t['nrm'][:, :], AF.Identity, ['nrm', 'par'], ['sqk'], scale=pcol('rk', u))
            bank, bk = PS()
            mm(bank[:, :], bones[:, :], sqk[:, :], ['bones', 'sqk'], [bk])
            tt('dve', bonus[:, tsl], bank[:, :], ld['v'][:, :], ALU.mult, [bk, 'ld_v'], ['bonus%d' % pc])

        def interleave(gens, stagger=STAGGER):
            gens = list(gens)
            for _ in range(stagger):
                try:
                    next(gens[0])
                except StopIteration:
                    gens.pop(0)
                    break
            while gens:
                for g_ in list(gens):
                    try:
                        next(g_)
                    except StopIteration:
                        gens.remove(g_)

        def head():
            yield from piece_a(0)
            piece_b(0)
            yield
            yield from dprep(0, 0)
            dprod(0, 0)
            yield

        def body():
            for pc in range(NPC):
                interleave([grp_gen(0, pc, 0), grp_gen(1, pc, 0), dprep(pc, 1)])
                dprod(pc, 1)
                bonus_f(pc)
                nxt = []
                if pc + 1 < NPC:
                    def nxt_gen(pc=pc):
                        yield from piece_a(pc + 1)
                        yield from dprep(pc + 1, 0)
                    nxt = [nxt_gen()]
                interleave([grp_gen(0, pc, 1), grp_gen(1, pc, 1)] + nxt)
                if pc + 1 < NPC:
                    piece_b(pc + 1)
                    dprod(pc + 1, 0)


        def chain_out():
            for s in range(1, NCH):
                bank, bk = PS()
                for d in range(2):
                    mm(bank[:, d * 128:(d + 1) * 128], MTS[:, s, d, :], NS[:, s - 1, d, :], ['MTS%d' % s, 'NS%d' % (s - 1)], [bk])
                tt('dve', NS[:, s, :, :], bank[:, 0:256].rearrange("p (c k) -> p c k", k=128), NS[:, s, :, :], ALU.add,
                   [bk, 'NS%d' % s], ['NS%d' % s])
                if s % 2 == 0:
                    yield
            for d in range(2):
                for pc in range(NPC):
                    bank, bk = PS()
                    n = 0
                    for c in range(8):
                        cg = pc * 8 + c
                        st = cg if d == 0 else NCH - 1 - cg
                        if st == 0:
                            continue
                        mm(bank[:, c * 64:(c + 1) * 64], NS[:, st - 1, d, :], Rhat[d][:, cg * 64:(cg + 1) * 64],
                           ['NS%d' % (st - 1), 'Rhat%d_%d' % (d, pc)], [bk])
                    lo, hi = 0, 8
                    if d == 0 and pc == 0:
                        lo = 1
                    if d == 1 and pc == NPC - 1:
                        hi = 7
                    tsl2 = slice(pc * PW + lo * 64, pc * PW + hi * 64)
                    tt('dve', Yacc[:, tsl2], bank[:, lo * 64:hi * 64], Yacc[:, tsl2], ALU.add, [bk, 'Yacc%d' % pc], ['Yacc%d' % pc])
                    yield

        def epi():
            EY = GB[0]['KU'][:, :, :].rearrange("p a b -> p (a b)").rearrange("p (a b) -> p a b", b=PW)
            t = f32t
            for pc in range(NPC):
                tsl = slice(pc * PW, (pc + 1) * PW)
                cp('act', EY[:, 0, :], Yacc[:, tsl], ['Yacc%d' % pc], ['KU0_0', 'KU0_1'])
                act(EY[:, 1, :], Yacc[:, tsl], AF.Square, ['Yacc%d' % pc], ['KU0_0', 'KU0_1'])
                b1, bk1 = PS()
                mm(b1[:, :], bones[:, :], EY[:, 0, :], ['bones', 'KU0_0', 'KU0_1'], [bk1])
                b2, bk2 = PS()
                mm(b2[:, :], bones[:, :], EY[:, 1, :], ['bones', 'KU0_0', 'KU0_1'], [bk2])
                act(t['S'][:, :], b1[:, :], AF.Identity, [bk1], ['S'], scale=1.0 / 64)
                act(t['sg'][:, :], t['S'][:, :], AF.Square, ['S'], ['sg'])
                stt('dve', t['Sd'][:, :], b2[:, :], 1.0 / 64, t['sg'][:, :], ALU.mult, ALU.subtract, [bk2, 'sg'], ['Sd'])
                act(t['a'][:, :], t['Sd'][:, :], AF.Ln, ['Sd', 'epsg'], ['a'], bias=epsg[:, 0:1])
                act(t['a'][:, :], t['a'][:, :], AF.Exp, ['a'], ['a'], scale=-0.5)
                tt('dve', t['eGx'][:, :], Yacc[:, tsl], t['S'][:, :], ALU.subtract, ['Yacc%d' % pc, 'S'], ['eGx'])
                tt('dve', t['eGx'][:, :], t['eGx'][:, :], t['a'][:, :], ALU.mult, ['eGx', 'a'], ['eGx'])
                act(t['eGx'][:, :], t['eGx'][:, :], AF.Identity, ['eGx', 'par'], ['eGx'], scale=pcol('lnw', u), bias=pcol('lnb', u))
                tt('dve', t['eGx'][:, :], t['eGx'][:, :], bonus[:, tsl], ALU.add, ['eGx', 'bonus%d' % pc], ['eGx'])
                dma(ld['lw'][:, :], pslice(g_cb, pc), [], ['ld_lw'])
                act(sgp[:, :], ld['lw'][:, :], AF.Silu, ['ld_lw'], ['sgp'])
                tt('dve', sqk[:, :], t['eGx'][:, :], sgp[:, :], ALU.mult, ['eGx', 'sgp'], ['sqk'])
                dma(mixT_d[u * 128:(u + 1) * 128, tsl], sqk[:, :], ['sqk'], [])

        return head, body, chain_out, epi

    units = [rw_unit(u) for u in range(rw_units)]
    if units:
        for _ in units[0][0]():
            pass
    for u in range(rw_units):
        head_, body_, chain_, epi_ = units[u]
        body_()
        gl = [chain_()]
        if u + 1 < rw_units:
            gl.append(units[u + 1][0]())
        run_window(gl, 2)
        epi_()
    P.barrier()
    A.reset(mB)
    if stop_after == 'B1':
        P.emit()
        return nc

    Yacc = A.alloc("hYacc", [128, T], F32)
    Qt = [A.alloc("Qt", [128, T], BF16) for _ in range(2)]
    NS = A.alloc("hNS", [128, NCH, 2, 128], BF16)
    GCS = [A.alloc("GCS", [128, NCH], F32) for _ in range(2)]
    sgate = A.alloc("hsgate", [128, T], BF16)
    HP = []
    for q_ in range(2):
        HP.append(dict(ld={nm: A.alloc("hld_" + nm, [128, PW], F32) for nm in ('q', 'f0', 'f1', 'i', 'g')},
                       vb=A.alloc("hvb", [128, PW], BF16), Vtm=A.alloc("hVtm", [64, 8, 128], BF16)))
    HS = []
    for q_ in range(4):
        HS.append(dict(t={nm: A.alloc("ht_" + nm, [128, PW], F32) for nm in ('f', 'lf', 'kf', 'G', 'Gd')},
                       eG=A.alloc("heG", [128, PW], BF16), enG=A.alloc("henG", [128, PW], BF16),
                       eGd=A.alloc("heGd", [128, PW], BF16),
                       kb=A.alloc("hkb", [128, PW], BF16), Kd=A.alloc("hKd", [128, PW], BF16),
                       Kdtm=A.alloc("hKdtm", [64, 8, 128], BF16), AT=A.alloc("hAT", [64, 8, 64], BF16)))
    r1 = A.alloc("hr1", [128, PW], F32)
    r2 = A.alloc("hr2", [128, PW], F32)
    osq = A.alloc("osq", [128, PW], BF16)
    obf = A.alloc("obf", [128, PW], BF16)

    def hg_unit(h):
        cbs = {'q': 18 + h, 'f0': 22 + h, 'f1': 26 + h, 'i': 30 + h, 'g': 34 + h}

        def piece_gen(pc):
            q_ = pc % 2
            B_ = HP[q_]
            ld = B_['ld']
            tsl = slice(pc * PW, (pc + 1) * PW)
            for nm in ('f0', 'f1', 'q', 'i', 'g'):
                dma(ld[nm][:, :], pslice(cbs[nm], pc), [], ['hld%d_%s' % (q_, nm)])
            cp('pool', B_['vb'][:, :], ld['i'][:, :], ['hld%d_i' % q_], ['hvb%d' % q_])
            act(sgate[:, tsl], ld['g'][:, :], AF.Silu, ['hld%d_g' % q_], ['hsg%d' % pc])
            yield
            for half in range(2):
                bank, bk = PS()
                for i in range(4):
                    c = half * 4 + i
                    mm(bank[0:64, i * 128:(i + 1) * 128], B_['vb'][:, c * 64:(c + 1) * 64], identb[:, :],
                       ['hvb%d' % q_, 'identb'], [bk])
                cp('dve', B_['Vtm'][:, half * 4:half * 4 + 4, :], bank[0:64, :].rearrange("p (c k) -> p c k", k=128),
                   [bk], ['hVtm%d' % q_])
            yield

        def pair_gen(pc):
            q_ = pc % 2
            B_ = HP[q_]
            ld = B_['ld']
            tsl = slice(pc * PW, (pc + 1) * PW)
            SS = [HS[q_ * 2 + d] for d in range(2)]
            KK = [(lambda nm, sid=q_ * 2 + d: 'h%s_%d' % (nm, sid)) for d in range(2)]
            for d in range(2):
                t, K = SS[d]['t'], KK[d]
                fk = 'f%d' % d
                act(t['f'][:, :], ld[fk][:, :], AF.Sigmoid, ['hld%d_%s' % (q_, fk)], [K('f')])
            for d in range(2):
                t, K = SS[d]['t'], KK[d]
                ts('dve', t['f'][:, :], t['f'][:, :], oml[:, h:h + 1], lb[:, h:h + 1], ALU.mult, ALU.add, [K('f'), 'oml', 'lb'], [K('f')])
            for d in range(2):
                t, K = SS[d]['t'], KK[d]
                act(t['lf'][:, :], t['f'][:, :], AF.Ln, [K('f')], [K('lf')])
                ts('pool', t['kf'][:, :], t['f'][:, :], -1.0, 1.0, ALU.mult, ALU.add, [K('f')], [K('kf')])
            yield
            GT = []
            for d in range(2):
                t, K = SS[d]['t'], KK[d]
                G3 = t['G'][:, :].rearrange("p (c k) -> p c k", k=64)
                if d == 0:
                    scan(t['G'][:, :], rmask[:, :, :].rearrange("p c k -> p (c k)"), t['lf'][:, :], ['rmask', K('lf')], [K('G')])
                    Gtot = G3[:, :, 63:64]
                    Gtot2 = G3[:, :, 63]
                else:
                    scan(t['G'][:, ::-1], rmask[:, :, :].rearrange("p c k -> p (c k)"), t['lf'][:, ::-1], ['rmask', K('lf')], [K('G')])
                    Gtot = G3[:, :, 0:1]
                    Gtot2 = G3[:, :, 0]
                GT.append(Gtot2)
                tt('pool', t['Gd'][:, :].rearrange("p (c k) -> p c k", k=64), Gtot.broadcast_to([128, 8, 64]), G3,
                   ALU.subtract, [K('G')], [K('Gd')])
            yield
            for d in range(2):
                S_, t, K = SS[d], SS[d]['t'], KK[d]
                act(S_['eG'][:, :], t['G'][:, :], AF.Exp, [K('G')], [K('eG')])
                act(S_['enG'][:, :], t['G'][:, :], AF.Exp, [K('G')], [K('enG')], scale=-1.0)
                act(S_['eGd'][:, :], t['Gd'][:, :], AF.Exp, [K('Gd')], [K('eGd')])
                if d == 0:
                    act(GCS[0][:, pc * 8:(pc + 1) * 8], GT[d], AF.Exp, [K('G')], ['GCS%d_%d' % (d, s_) for s_ in range(pc * 8, pc * 8 + 8)])
                else:
                    lo_ = NCH - 8 - pc * 8
                    act(GCS[1][:, lo_:lo_ + 8][:, ::-1], GT[d], AF.Exp, [K('G')], ['GCS%d_%d' % (d, s_) for s_ in range(lo_, lo_ + 8)])
            yield
            for d in range(2):
                S_, t, K = SS[d], SS[d]['t'], KK[d]
                tt('dve', Qt[d][:, tsl], ld['q'][:, :], S_['eG'][:, :], ALU.mult, ['hld%d_q' % q_, K('eG')], ['Qt%d_%d' % (d, pc)])
                tt('pool', S_['kb'][:, :], t['kf'][:, :], S_['enG'][:, :], ALU.mult, [K('kf'), K('enG')], [K('kb')])
                tt('dve', S_['Kd'][:, :], t['kf'][:, :], S_['eGd'][:, :], ALU.mult, [K('kf'), K('eGd')], [K('Kd')])
            yield
            for d in range(2):
                S_, t, K = SS[d], SS[d]['t'], KK[d]
                for half in range(2):
                    bank, bk = PS()
                    for i in range(4):
                        c = half * 4 + i
                        mm(bank[0:64, i * 128:(i + 1) * 128], S_['Kd'][:, c * 64:(c + 1) * 64], identb[:, :], [K('Kd'), 'identb'], [bk])
                    cp('act' if half == 0 else 'dve', S_['Kdtm'][:, half * 4:half * 4 + 4, :],
                       bank[0:64, :].rearrange("p (c k) -> p c k", k=128), [bk], [K('Kdtm')])
                bank, bk = PS()
                for c in range(8):
                    mm(bank[0:64, c * 64:(c + 1) * 64], S_['kb'][:, c * 64:(c + 1) * 64],
                       Qt[d][:, pc * PW + c * 64:pc * PW + (c + 1) * 64], [K('kb'), 'Qt%d_%d' % (d, pc)], [bk])
                tt('dve', S_['AT'][:, :, :], bank[0:64, :].rearrange("p (c k) -> p c k", k=64), maskH[d][:, :, :], ALU.mult,
                   [bk, 'maskH%d' % d], [K('AT')])
            yield
            for d in range(2):
                S_, t, K = SS[d], SS[d]['t'], KK[d]
                bank, bk = PS()
                for c in range(8):
                    mm(bank[:, c * 64:(c + 1) * 64], B_['Vtm'][:, c, :], S_['AT'][:, c, :], ['hVtm%d' % q_, K('AT')], [bk])
                if d == 0:
                    cp('act', Yacc[:, tsl], bank[:, :], [bk], ['hY%d' % pc])
                else:
                    tt('dve', Yacc[:, tsl], bank[:, :], Yacc[:, tsl], ALU.add, [bk, 'hY%d' % pc], ['hY%d' % pc])
                for half in range(2):
                    bank, bk = PS()
                    for i in range(4):
                        c = half * 4 + i
                        mm(bank[:, i * 128:(i + 1) * 128], S_['Kdtm'][:, c, :], B_['Vtm'][:, c, :], [K('Kdtm'), 'hVtm%d' % q_], [bk])
                    for i in range(4):
                        c = half * 4 + i
                        cg = pc * 8 + c
                        st = cg if d == 0 else NCH - 1 - cg
                        cp('act' if half == 0 else 'dve', NS[:, st, d, :], bank[:, i * 128:(i + 1) * 128], [bk], ['hNS%d_%d' % (d, st)])
            yield

        gens = []
        for pc in range(NPC):
            def work(pc=pc):
                yield from piece_gen(pc)
                yield from pair_gen(pc)
            gens.append(work())
        run_window(gens, 2)
        for s in range(1, NCH):
            for d in range(2):
                stt('dve', NS[:, s, d, :], NS[:, s - 1, d, :], GCS[d][:, s:s + 1], NS[:, s, d, :],
                    ALU.mult, ALU.add, ['hNS%d_%d' % (d, s - 1), 'hNS%d_%d' % (d, s), 'GCS%d_%d' % (d, s)], ['hNS%d_%d' % (d, s)])
        for d in range(2):
            for pc in range(NPC):
                bank, bk = PS()
                lo, hi = 0, 8
                for c in range(8):
                    cg = pc * 8 + c
                    st = cg if d == 0 else NCH - 1 - cg
                    if st == 0:
                        if d == 0:
                            lo = 1
                        else:
                            hi = 7
                        continue
                    mm(bank[:, c * 64:(c + 1) * 64], NS[:, st - 1, d, :], Qt[d][:, cg * 64:(cg + 1) * 64],
                       ['hNS%d_%d' % (d, st - 1), 'Qt%d_%d' % (d, pc)], [bk])
                tsl2 = slice(pc * PW + lo * 64, pc * PW + hi * 64)
                tt('dve', Yacc[:, tsl2], bank[:, lo * 64:hi * 64], Yacc[:, tsl2], ALU.add, [bk, 'hY%d' % pc], ['hY%d' % pc])
        for pc in range(NPC):
            tsl = slice(pc * PW, (pc + 1) * PW)
            act(osq[:, :], Yacc[:, tsl], AF.Square, ['hY%d' % pc], ['osq'])
            bank, bk = PS()
            mm(bank[:, :], onesb[:, :], osq[:, :], ['onesb', 'osq'], [bk])
            act(r1[:, :], bank[:, :], AF.Ln, [bk, 'epsn'], ['hr1'], bias=epsn[:, 0:1], scale=1.0 / 128)
            act(r1[:, :], r1[:, :], AF.Exp, ['hr1'], ['hr1'], scale=-0.5)
            tt('dve', r2[:, :], Yacc[:, tsl], r1[:, :], ALU.mult, ['hY%d' % pc, 'hr1'], ['hr2'])
            stt('dve', obf[:, :], r2[:, :], pcol('hgn', 0), sgate[:, tsl], ALU.mult, ALU.mult,
                ['hr2', 'par', 'hsg%d' % pc], ['obf'])
            dma(mixT_d[512 + h * 128:512 + (h + 1) * 128, tsl], obf[:, :], ['obf'], [], eng='pool')

    for h in range(hg_units):
        hg_unit(h)
    P.barrier()
    A.reset(mB)
    if stop_after == 'B':
        P.emit()
        return nc

    wob = A.alloc("wob", [128, 8, D], BF16)
    gpost = A.alloc("gpost", [128, D], F32)
    wost = [A.alloc("wost", [128, D], F32) for _ in range(2)]
    dma(gpost[:, :], gpost_d[:, :], [], ['gpost'])
    wout_v = wout_d.rearrange("(kc p) n -> p kc n", p=128)
    for kc in range(8):
        dma(wost[kc % 2][:, :], wout_v[:, kc, :], [], ['wost%d' % (kc % 2)])
        cp('dve' if kc % 2 == 0 else 'act', wob[:, kc, :], wost[kc % 2][:, :], ['wost%d' % (kc % 2)], ['wob'])
    NSL = 4
    mx = [A.alloc("mx", [128, 8, 128], BF16) for _ in range(NSL)]
    xt = [A.alloc("xt", [128, D], F32) for _ in range(NSL)]
    ot = [A.alloc("ot", [128, D], F32) for _ in range(NSL)]
    junk = A.alloc("junk", [128, 512], BF16)
    ss = [A.alloc("ss", [128, 4], F32) for _ in range(NSL)]
    mix_v = mixT_d.rearrange("(kc p) t -> p kc t", p=128)
    NT = T // 128

    def c_load(tt_i):
        s = tt_i % NSL
        rows = slice(tt_i * 128, (tt_i + 1) * 128)
        dma(mx[s][:, :, :], mix_v[:, :, rows], [], ['mx%d' % s])
        dma(xt[s][:, :], x_d[rows, :], [], ['xt%d' % s])

    for i in range(min(NSL - 1, NT)):
        c_load(i)
    for tt_i in range(NT):
        s = tt_i % NSL
        rows = slice(tt_i * 128, (tt_i + 1) * 128)
        if tt_i + NSL - 1 < NT:
            c_load(tt_i + NSL - 1)
        banks = []
        for half in range(2):
            bank, bk = PS()
            banks.append((bank, bk))
            for kc in range(8):
                mm(bank[:, :], mx[s][:, kc, :], wob[:, kc, half * 512:(half + 1) * 512], ['mx%d' % s, 'wob'], [bk],
                   start=(kc == 0), stop=(kc == 7))
            act(junk[:, :], bank[:, :], AF.Square, [bk], ['junk', 'ss%d' % s], accum=ss[s][:, half:half + 1])
        act(ss[s][:, 2:3], ss[s][:, 0:1], AF.Identity, ['ss%d' % s], ['ss%d' % s], bias=ss[s][:, 1:2])
        act(ss[s][:, 3:4], ss[s][:, 2:3], AF.Sqrt, ['ss%d' % s, 'epsn'], ['ss%d' % s], bias=epsn[:, 0:1], scale=1.0 / D)
        recip(ss[s][:, 3:4], ss[s][:, 3:4], ['ss%d' % s], ['ss%d' % s])
        for half in range(2):
            bank, bk = banks[half]
            hs = slice(half * 512, (half + 1) * 512)
            stt('dve', ot[s][:, hs], bank[:, :], ss[s][:, 3:4], gpost[:, hs], ALU.mult, ALU.mult,
                [bk, 'ss%d' % s, 'gpost'], ['ot%d' % s])
        tt('dve', ot[s][:, :], ot[s][:, :], xt[s][:, :], ALU.add, ['ot%d' % s, 'xt%d' % s], ['ot%d' % s])
        dma(out_d[rows, :], ot[s][:, :], ['ot%d' % s], [], eng='pool')
    P.emit()
    return nc


def prep_inputs(inputs, b):
    f = lambda a: np.ascontiguousarray(np.asarray(a, dtype=np.float32))
    x = f(inputs['x'])
    par = np.zeros((128, NPAR), np.float32)
    cols = []
    cols.append(f(inputs['pre_norm_g'])[0].reshape(8, 128).T)
    cols.append(f(inputs['rw_shift_prev'])[0].reshape(18, 128).T)
    cols.append(f(inputs['rw_shift_next'])[0].reshape(18, 128).T)
    cols.append(f(inputs['rw_w0'])[0].reshape(8, 128).T)
    cols.append(f(inputs['rw_a0'])[0].reshape(8, 128).T)
    cols.append(f(inputs['rw_k_k'])[0].reshape(4, 128).T)
    cols.append(f(inputs['rw_k_a'])[0].reshape(4, 128).T)
    cols.append(f(inputs['rw_r_k'])[0].reshape(4, 128).T)
    cols.append(f(inputs['rw_ln_w'])[0].reshape(4, 128).T)
    cols.append(f(inputs['rw_ln_b'])[0].reshape(4, 128).T)
    lbl = f(inputs['hg_lb_logits'])
    cols.append(lbl[0].reshape(4, 128).T)
    cols.append(lbl[1].reshape(4, 128).T)
    cols.append(f(inputs['hg_norm_g'])[0].reshape(1, 128).T)
    allc = np.concatenate(cols, axis=1)
    par[:, :allc.shape[1]] = allc
    m = {
        "xT": np.ascontiguousarray(x[b].T),
        "x": np.ascontiguousarray(x[b]),
        "w_in": f(inputs['w_in'])[0],
        "w_out": f(inputs['w_out'])[0],
        "par": par,
        "w2s": f(inputs['rw_w2'])[0].reshape(128, 512),
        "a2s": f(inputs['rw_a2'])[0].reshape(128, 512),
        "gpost": np.ascontiguousarray(np.broadcast_to(f(inputs['post_norm_g'])[0][None, :], (128, D))),
    }
    return m


def kernel(**inputs):
    nc = build()
    in_maps = [prep_inputs(inputs, c % 4) for c in range(8)]
    res = run_bass_kernel_spmd(nc, in_maps, core_ids=list(range(8)))
    out = np.stack([np.asarray(res.results[c]["out"], dtype=np.float32) for c in range(4)], axis=0)
    return out
```
